# Optimizing a Trainium2 kernel written in Bass

```python
import math
import jax, jax.numpy as jnp
from jax import lax
import numpy as np

D_MODEL = 1024
BATCH = 4
SEQ = 4096
DEPTH = 2

CTX_LEN = 256
GRID_W = 64
HEAD_DIM = 64
EPS = 1e-6
ROPE_BASE = 10000.0
A_HEADS = 8
A_KV_HEADS = 2
A_WINDOW = 128
A_BLOCK = 128
B_HEADS = 4
B_DK = 64
B_DV = 128
B_GATE_RANK = 16
B_GATE_NORM = 16.0
B_CHUNK = 16
C_HEADS = 8
C_WIN_ROWS = 8
C_WIN_COLS = 16
C_QBLOCK_COLS = 16
C_KBLOCK_COLS = 32
FFN_HIDDEN = -(-8 * D_MODEL // (3 * 256)) * 256
IN_SIZES = (A_HEADS * HEAD_DIM, A_KV_HEADS * HEAD_DIM, A_KV_HEADS * HEAD_DIM,
            B_HEADS * B_DK, B_HEADS * B_DK, B_HEADS * B_DV, B_HEADS * B_DV, 2 * B_GATE_RANK,
            C_HEADS * HEAD_DIM, C_HEADS * HEAD_DIM, C_HEADS * HEAD_DIM)
IN_TOTAL = sum(IN_SIZES)

kernel_name = "hybrid_gated_parallel_mixer_dit"


def rmsnorm(x, g):
    xf = x.astype(jnp.float32)
    y = xf * lax.rsqrt(jnp.mean(xf * xf, axis=-1, keepdims=True) + EPS)
    return (y * g.astype(jnp.float32)).astype(x.dtype)


def split_projection(t):
    idx = []
    acc = 0
    for s in IN_SIZES[:-1]:
        acc += s
        idx.append(acc)
    return jnp.split(t, idx, axis=-1)


def split_heads(t, n_heads):
    return t.reshape(t.shape[0], t.shape[1], n_heads, -1)


def axial_rope_tables(L):
    t = jnp.arange(L)
    row = (t // GRID_W).astype(jnp.float32)
    col = (t % GRID_W).astype(jnp.float32)
    n_freq = HEAD_DIM // 4
    inv = ROPE_BASE ** (-jnp.arange(n_freq, dtype=jnp.float32) / n_freq)
    ang = jnp.concatenate([row[:, None] * inv[None], col[:, None] * inv[None]], axis=-1)
    return jnp.cos(ang), jnp.sin(ang)


def apply_rope(x, cos, sin):
    half = HEAD_DIM // 2
    x1, x2 = x[..., :half], x[..., half:]
    c = cos[None, :, None, :]
    s = sin[None, :, None, :]
    return jnp.concatenate([x1 * c - x2 * s, x1 * s + x2 * c], axis=-1).astype(x.dtype)


def window_attention(q, k, v, k_ctx, v_ctx, sink):
    B, L, Hq, dh = q.shape
    n_kv = k.shape[2]
    G = Hq // n_kv
    Lc = k_ctx.shape[1]
    blk = A_BLOCK
    nb = L // blk
    qb = q.reshape(B, nb, blk, n_kv, G, dh)
    pad = ((0, 0), (blk, blk), (0, 0), (0, 0))
    kp = jnp.pad(k, pad).reshape(B, nb + 2, blk, n_kv, dh)
    vp = jnp.pad(v, pad).reshape(B, nb + 2, blk, n_kv, dh)
    k_band = jnp.concatenate([kp[:, :-2], kp[:, 1:-1], kp[:, 2:]], axis=2)
    v_band = jnp.concatenate([vp[:, :-2], vp[:, 1:-1], vp[:, 2:]], axis=2)
    scale = dh ** -0.5
    s_loc = jnp.einsum('bnqhgd,bnkhd->bnhgqk', qb, k_band).astype(jnp.float32) * scale
    s_ctx = jnp.einsum('bnqhgd,bchd->bnhgqc', qb, k_ctx).astype(jnp.float32) * scale
    qi = jnp.arange(blk)[:, None]
    kj = jnp.arange(3 * blk)[None, :] - blk
    j_abs = jnp.arange(nb)[:, None, None] * blk + kj[None]
    valid = (jnp.abs(kj - qi)[None] <= A_WINDOW) & (j_abs >= 0) & (j_abs < L)
    s_loc = jnp.where(valid[None, :, None, None], s_loc, -jnp.inf)
    sink_col = jnp.broadcast_to(sink.astype(jnp.float32).reshape(n_kv, G)[None, None, :, :, None, None],
                                s_loc.shape[:-1] + (1,))
    p = jax.nn.softmax(jnp.concatenate([s_loc, s_ctx, sink_col], axis=-1), axis=-1)
    nk = 3 * blk
    p_loc = p[..., :nk].astype(v.dtype)
    p_ctx = p[..., nk:nk + Lc].astype(v.dtype)
    o = (jnp.einsum('bnhgqk,bnkhd->bnqhgd', p_loc, v_band)
         + jnp.einsum('bnhgqc,bchd->bnqhgd', p_ctx, v_ctx))
    return o.reshape(B, L, Hq * dh)


def dense_ctx_attention(q, k, v, n_kv, sink):
    B, Lc, Hq, dh = q.shape
    G = Hq // n_kv
    qg = q.reshape(B, Lc, n_kv, G, dh)
    s = jnp.einsum('bqhgd,bkhd->bhgqk', qg, k).astype(jnp.float32) * dh ** -0.5
    if sink is not None:
        sink_col = jnp.broadcast_to(sink.astype(jnp.float32).reshape(n_kv, G)[None, :, :, None, None],
                                    (B, n_kv, G, Lc, 1))
        s = jnp.concatenate([s, sink_col], axis=-1)
    p = jax.nn.softmax(s, axis=-1)[..., :Lc].astype(v.dtype)
    o = jnp.einsum('bhgqk,bkhd->bqhgd', p, v)
    return o.reshape(B, Lc, Hq * dh)


def neighbourhood_attention(q, k, v, k_ctx, v_ctx, rpb):
    B, L, H, dh = q.shape
    Lc = k_ctx.shape[1]
    rows = L // GRID_W
    kh = min(C_WIN_ROWS, rows)
    qbw = C_QBLOCK_COLS
    kbw = C_KBLOCK_COLS
    nm = GRID_W // qbw
    qg = q.reshape(B, rows, nm, qbw, H, dh)
    kg = k.reshape(B, rows, GRID_W, H, dh)
    vg = v.reshape(B, rows, GRID_W, H, dh)
    r = jnp.arange(rows)
    row_idx = jnp.clip(r - kh // 2, 0, rows - kh)[:, None] + jnp.arange(kh)[None]
    m = jnp.arange(nm)
    col_start = jnp.clip(m * qbw - C_WIN_COLS // 2, 0, GRID_W - kbw)
    col_idx = col_start[:, None] + jnp.arange(kbw)[None]
    ri = row_idx[:, None, :, None]
    ci = col_idx[None, :, None, :]
    k_nb = kg[:, ri, ci]
    v_nb = vg[:, ri, ci]
    scale = dh ** -0.5
    s_loc = jnp.einsum('brmqhd,brmijhd->brmhqij', qg, k_nb).astype(jnp.float32) * scale
    qcol = m[:, None] * qbw + jnp.arange(qbw)[None]
    wstart = jnp.clip(qcol - C_WIN_COLS // 2, 0, GRID_W - C_WIN_COLS)
    kcol = col_idx[:, None, :]
    col_ok = (kcol >= wstart[..., None]) & (kcol < wstart[..., None] + C_WIN_COLS)
    d_row = row_idx - r[:, None] + (C_WIN_ROWS - 1)
    d_col = jnp.clip(kcol - qcol[..., None], -(C_WIN_COLS - 1), C_WIN_COLS - 1) + (C_WIN_COLS - 1)
    bias = rpb[:, d_row[:, None, None, :, None], d_col[None, :, :, None, :]]
    bias = jnp.moveaxis(bias, 0, 2).astype(jnp.float32)
    s_loc = jnp.where(col_ok[None, None, :, None, :, None, :], s_loc + bias[None], -jnp.inf)
    s_loc = s_loc.reshape(B, rows, nm, H, qbw, kh * kbw)
    s_ctx = jnp.einsum('brmqhd,bchd->brmhqc', qg, k_ctx).astype(jnp.float32) * scale
    p = jax.nn.softmax(jnp.concatenate([s_loc, s_ctx], axis=-1), axis=-1)
    nk = kh * kbw
    p_loc = p[..., :nk].reshape(B, rows, nm, H, qbw, kh, kbw).astype(v.dtype)
    p_ctx = p[..., nk:].astype(v.dtype)
    o = (jnp.einsum('brmhqij,brmijhd->brmqhd', p_loc, v_nb)
         + jnp.einsum('brmhqc,bchd->brmqhd', p_ctx, v_ctx))
    return o.reshape(B, L, H * dh)


def gla_chunked(q, k, v, g, s0):
    B, H, L, dk = q.shape
    dv = v.shape[-1]
    C = B_CHUNK
    n = L // C
    qc = q.reshape(B, H, n, C, dk)
    kc = k.reshape(B, H, n, C, dk)
    vc = v.reshape(B, H, n, C, dv)
    b = jnp.cumsum(g.astype(jnp.float32).reshape(B, H, n, C, dk), axis=3)
    b_last = b[:, :, :, -1:]
    causal = jnp.tril(jnp.ones((C, C), dtype=bool))
    diff = b[:, :, :, :, None, :] - b[:, :, :, None, :, :]
    decay = jnp.exp(jnp.where(causal[:, :, None], diff, -jnp.inf))
    A = jnp.einsum('bhnid,bhnjd,bhnijd->bhnij', qc, kc, decay)
    o_intra = jnp.einsum('bhnij,bhnjd->bhnid', A, vc)
    k_tail = kc * jnp.exp(b_last - b)
    upd = jnp.einsum('bhncd,bhnce->bhnde', k_tail, vc)
    chunk_decay = jnp.exp(b_last[:, :, :, 0])

    def step(S, inp):
        dec, u = inp
        return dec[..., None] * S + u, S

    s_final, s_before = lax.scan(step, s0, (jnp.moveaxis(chunk_decay, 2, 0), jnp.moveaxis(upd, 2, 0)))
    s_before = jnp.moveaxis(s_before, 0, 2)
    o_inter = jnp.einsum('bhncd,bhnde->bhnce', qc * jnp.exp(b), s_before)
    o = (o_intra + o_inter).reshape(B, H, L, dv).astype(v.dtype)
    return o, s_final


def gla_bidirectional(q, k, v, g_fwd, g_bwd, s0_fwd, s0_bwd):
    o_f, s_f = gla_chunked(q, k, v, g_fwd, s0_fwd)
    flip = lambda t: jnp.flip(t, axis=2)
    o_b, s_b = gla_chunked(flip(q), flip(k), flip(v), flip(g_bwd), s0_bwd)
    return o_f + flip(o_b), s_f, s_b


def to_gla_heads(t, n_heads):
    return split_heads(t, n_heads).transpose(0, 2, 1, 3)


def gla_log_gates(z_low, w, bias):
    z = (z_low @ w + bias).astype(jnp.float32)
    return to_gla_heads(jax.nn.log_sigmoid(z) / B_GATE_NORM, B_HEADS)


def gla_output(o, r, norm_g):
    B, H, L, dv = o.shape
    o = rmsnorm(o.transpose(0, 2, 1, 3), norm_g).reshape(B, L, H * dv)
    return (o * jax.nn.silu(r)).astype(r.dtype)


def merge_branches(h, y_a, y_b, y_c, w_ba, w_bb, w_bc, w_merge, b_merge, w_out):
    gates = jax.nn.sigmoid(h @ w_merge + b_merge)
    g_a, g_b, g_c = jnp.split(gates, 3, axis=-1)
    mixed = g_a * (y_a @ w_ba) + g_b * (y_b @ w_bb) + g_c * (y_c @ w_bc)
    return mixed @ w_out


def swiglu(h, w_in, w_out):
    gate, up = jnp.split(h @ w_in, 2, axis=-1)
    return (jax.nn.silu(gate) * up) @ w_out


def setup_inputs(seed: int = 0) -> dict:
    key = jax.random.key(seed)
    ks = iter(jax.random.split(key, 32))

    def nrm(shape, scale):
        return jax.random.normal(next(ks), shape, jnp.float32) * scale

    D = D_MODEL
    return {
        'x': nrm((BATCH, SEQ, D), 1.0),
        'c': nrm((BATCH, D), 1.0),
        'ctx': nrm((BATCH, CTX_LEN, D), 1.0),
        'c_ctx': nrm((D,), 1.0),
        'w_ada': nrm((DEPTH, D, 6 * D), 0.5 * D ** -0.5),
        'b_ada': nrm((DEPTH, 6 * D), 0.02),
        'norm_mix': 1.0 + nrm((DEPTH, D), 0.02),
        'w_in': nrm((DEPTH, D, IN_TOTAL), D ** -0.5),
        'attn_sink': nrm((DEPTH, A_HEADS), 0.5),
        'gla_gate_w_fwd': nrm((DEPTH, B_GATE_RANK, B_HEADS * B_DK), B_GATE_RANK ** -0.5),
        'gla_gate_b_fwd': nrm((DEPTH, B_HEADS * B_DK), 0.1),
        'gla_gate_w_bwd': nrm((DEPTH, B_GATE_RANK, B_HEADS * B_DK), B_GATE_RANK ** -0.5),
        'gla_gate_b_bwd': nrm((DEPTH, B_HEADS * B_DK), 0.1),
        'gla_norm': 1.0 + nrm((DEPTH, B_DV), 0.02),
        'na_rpb': nrm((DEPTH, C_HEADS, 2 * C_WIN_ROWS - 1, 2 * C_WIN_COLS - 1), 0.02),
        'w_branch_a': nrm((DEPTH, A_HEADS * HEAD_DIM, D), (A_HEADS * HEAD_DIM) ** -0.5),
        'w_branch_b': nrm((DEPTH, B_HEADS * B_DV, D), (B_HEADS * B_DV) ** -0.5),
        'w_branch_c': nrm((DEPTH, C_HEADS * HEAD_DIM, D), (C_HEADS * HEAD_DIM) ** -0.5),
        'w_merge': nrm((DEPTH, D, 3 * D), D ** -0.5),
        'b_merge': nrm((DEPTH, 3 * D), 0.02),
        'w_out': nrm((DEPTH, D, D), D ** -0.5),
        'norm_ffn': 1.0 + nrm((DEPTH, D), 0.02),
        'w_ffn_in': nrm((DEPTH, D, 2 * FFN_HIDDEN), D ** -0.5),
        'w_ffn_out': nrm((DEPTH, FFN_HIDDEN, D), FFN_HIDDEN ** -0.5),
        'final_norm': 1.0 + nrm((D,), 0.02),
    }


def reference(x, c, ctx, c_ctx, w_ada, b_ada, norm_mix, w_in, attn_sink, gla_gate_w_fwd, gla_gate_b_fwd,
              gla_gate_w_bwd, gla_gate_b_bwd, gla_norm, na_rpb, w_branch_a, w_branch_b, w_branch_c,
              w_merge, b_merge, w_out, norm_ffn, w_ffn_in, w_ffn_out, final_norm):
    B, L, D = x.shape
    cos, sin = axial_rope_tables(L)
    xc = ctx
    for l in range(DEPTH):
        last = l == DEPTH - 1
        mod = (jax.nn.silu(c) @ w_ada[l] + b_ada[l])[:, None, :]
        mod_c = (jax.nn.silu(c_ctx) @ w_ada[l] + b_ada[l])[None, None, :]
        sh_m, sc_m, gt_m, sh_f, sc_f, gt_f = jnp.split(mod, 6, axis=-1)
        shc_m, scc_m, gtc_m, shc_f, scc_f, gtc_f = jnp.split(mod_c, 6, axis=-1)

        h = rmsnorm(x, norm_mix[l]) * (1.0 + sc_m) + sh_m
        hc = rmsnorm(xc, norm_mix[l]) * (1.0 + scc_m) + shc_m
        a_q, a_k, a_v, g_q, g_k, g_v, g_r, g_a, n_q, n_k, n_v = split_projection(h @ w_in[l])
        ac_q, ac_k, ac_v, gc_q, gc_k, gc_v, gc_r, gc_a, nc_q, nc_k, nc_v = split_projection(hc @ w_in[l])

        qa = apply_rope(split_heads(a_q, A_HEADS), cos, sin)
        ka = apply_rope(split_heads(a_k, A_KV_HEADS), cos, sin)
        va = split_heads(a_v, A_KV_HEADS)
        qa_c = split_heads(ac_q, A_HEADS)
        ka_c = split_heads(ac_k, A_KV_HEADS)
        va_c = split_heads(ac_v, A_KV_HEADS)
        y_a = window_attention(qa, ka, va, ka_c, va_c, attn_sink[l])

        R = B_GATE_RANK
        gf_c = gla_log_gates(gc_a[..., :R], gla_gate_w_fwd[l], gla_gate_b_fwd[l])
        gb_c = gla_log_gates(gc_a[..., R:], gla_gate_w_bwd[l], gla_gate_b_bwd[l])
        zeros = jnp.zeros((B, B_HEADS, B_DK, B_DV), jnp.float32)
        o_c, s_ctx_f, s_ctx_b = gla_bidirectional(
            to_gla_heads(gc_q * B_DK ** -0.5, B_HEADS), to_gla_heads(gc_k, B_HEADS), to_gla_heads(gc_v, B_HEADS),
            gf_c, gb_c, zeros, zeros)
        gf = gla_log_gates(g_a[..., :R], gla_gate_w_fwd[l], gla_gate_b_fwd[l])
        gb = gla_log_gates(g_a[..., R:], gla_gate_w_bwd[l], gla_gate_b_bwd[l])
        o_l, _, _ = gla_bidirectional(
            to_gla_heads(g_q * B_DK ** -0.5, B_HEADS), to_gla_heads(g_k, B_HEADS), to_gla_heads(g_v, B_HEADS),
            gf, gb, s_ctx_f, s_ctx_b)
        y_b = gla_output(o_l, g_r, gla_norm[l])

        nk_c = split_heads(nc_k, C_HEADS)
        nv_c = split_heads(nc_v, C_HEADS)
        y_c = neighbourhood_attention(split_heads(n_q, C_HEADS), split_heads(n_k, C_HEADS),
                                      split_heads(n_v, C_HEADS), nk_c, nv_c, na_rpb[l])

        x = x + gt_m * merge_branches(h, y_a, y_b, y_c, w_branch_a[l], w_branch_b[l], w_branch_c[l],
                                      w_merge[l], b_merge[l], w_out[l])
        if not last:
            yc_a = dense_ctx_attention(qa_c, ka_c, va_c, A_KV_HEADS, attn_sink[l])
            yc_b = gla_output(o_c, gc_r, gla_norm[l])
            yc_c = dense_ctx_attention(split_heads(nc_q, C_HEADS), nk_c, nv_c, C_HEADS, None)
            xc = xc + gtc_m * merge_branches(hc, yc_a, yc_b, yc_c, w_branch_a[l], w_branch_b[l],
                                             w_branch_c[l], w_merge[l], b_merge[l], w_out[l])

        hf = rmsnorm(x, norm_ffn[l]) * (1.0 + sc_f) + sh_f
        x = x + gt_f * swiglu(hf, w_ffn_in[l], w_ffn_out[l])
        if not last:
            hfc = rmsnorm(xc, norm_ffn[l]) * (1.0 + scc_f) + shc_f
            xc = xc + gtc_f * swiglu(hfc, w_ffn_in[l], w_ffn_out[l])
    return rmsnorm(x, final_norm)
```

```python
import contextlib
import numpy as np
import concourse.bass as bass
import concourse.mybir as mybir
from concourse.bass_utils import run_bass_kernel_spmd

F32 = mybir.dt.float32
BF16 = mybir.dt.bfloat16
AF = mybir.ActivationFunctionType
ALU = mybir.AluOpType

D = 1024
KC = 8
SEQ = 4096
TL = 3072
CT = 256
TT = TL + CT
TO = 2560 + CT
EPS = 1e-6
HID = 2816
NHC = 22

FM_GROUPS = [('Aq', 512), ('Aqs', 512), ('Ak', 128), ('Aks', 128), ('Cq', 512), ('Ck', 512),
             ('Bq', 256), ('Bk', 256), ('Br', 512), ('Bz', 64)]
TM_GROUPS = [('Av', 128), ('Bv', 512), ('Cv', 512)]
COL = {}
_o = 0
for _n, _w in FM_GROUPS + TM_GROUPS:
    COL[_n] = _o
    _o += _w
NX = _o

PP = {}
_o = 0
def _pp(name, n):
    global _o
    PP[name] = _o
    _o += n
_pp('c', 16)
for _l in range(2):
    _pp(f'bada{_l}', 96); _pp(f'nmix{_l}', 16); _pp(f'nffn{_l}', 16); _pp(f'bmerge{_l}', 24)
    _pp(f'gnorm{_l}', 1); _pp(f'sink{_l}', 8)
_pp('fnorm', 8)
NPP = _o

BLOCKS = [
    [(0, 512, 'F', 0), (512, 512, 'F', 0), (1024, 512, 'F', 0), (1536, 512, 'F', 0), (2048, 512, 'F', 0),
     (2560, 512, 'KV', 0), (TL, 256, 'F', 1)],
    [(0, 512, 'F', 0), (512, 512, 'F', 0), (1024, 512, 'F', 0), (1536, 512, 'F', 0),
     (2048, 512, 'KV', 0), (TL, 256, 'KV', 1)],
]
TFULL = [2560, 2048]
TGLA = [3072, 2560]
NCTAB = 13


class Trk:
    ENG = ('pe', 'act', 'dve', 'pool', 'sp')

    def __init__(s, nc, es):
        s.nc, s.es = nc, es
        s.E = dict(pe=nc.tensor, act=nc.scalar, dve=nc.vector, pool=nc.gpsimd, sp=nc.sync)
        s.sem = {}
        s.cnt = {}
        s.epoch = 0
        s.seen = {e: {} for e in s.ENG}
        s.lastw = {}
        s.rd = {}
        s._new_compute_sems()
        s.ninst = 0
        s.dmap = {}
        s.dfree = []
        s.nd = 0

    def _new_compute_sems(s):
        for e in ('pe', 'act', 'dve', 'pool'):
            s.sem[e] = s.es.enter_context(s.nc.semaphore(f"c{s.epoch}_{e}"))
            s.cnt[e] = 0
            for w in s.ENG:
                s.seen[w].pop(e, None)

    def dsem(s, name):
        if name in s.dmap:
            return s.dmap[name]
        if s.dfree:
            key = s.dfree.pop()
        else:
            key = 'd_%d' % s.nd
            s.nd += 1
            s.sem[key] = s.es.enter_context(s.nc.semaphore(key))
            s.cnt[key] = 0
        s.dmap[name] = key
        return key

    def _wait(s, e, tok):
        k, v = tok
        if k == e and e == 'pe':
            return
        if s.seen[e].get(k, 0) >= v:
            return
        s.E[e].wait_ge(s.sem[k], v)
        s.seen[e][k] = v
        s.ninst += 1

    def _deps(s, e, reads, writes):
        for k in reads:
            if k in s.lastw:
                s._wait(e, s.lastw[k])
        for k in writes:
            for sk, v in s.rd.get(k, {}).items():
                if sk != e:
                    s._wait(e, (sk, v))
            if k in s.lastw and s.lastw[k][0] != e:
                s._wait(e, s.lastw[k])

    def _post(s, tok, reads, writes):
        for k in reads:
            d = s.rd.setdefault(k, {})
            d[tok[0]] = max(d.get(tok[0], 0), tok[1])
        for k in writes:
            s.lastw[k] = tok
            s.rd[k] = {}

    def op(s, e, fn, reads=(), writes=()):
        s._deps(e, reads, writes)
        inst = fn(s.E[e])
        s.cnt[e] += 1
        inst.then_inc(s.sem[e], 1)
        s._post((e, s.cnt[e]), reads, writes)
        s.ninst += 1

    def dma(s, q, semname, out, in_, reads=(), writes=()):
        key = s.dsem(semname)
        s._deps(q, reads, writes)
        inst = s.E[q].dma_start(out=out, in_=in_)
        s.cnt[key] += 16
        inst.then_inc(s.sem[key], 16)
        s._post((key, s.cnt[key]), reads, writes)
        s.ninst += 1

    def barrier(s):
        for e in s.ENG:
            for k in list(s.sem):
                if k == e:
                    continue
                if s.cnt[k] > 0:
                    s._wait(e, (k, s.cnt[k]))
        s.lastw.clear()
        s.rd.clear()
        s.dfree += sorted(s.dmap.values())
        s.dmap = {}
        s.epoch += 1
        s._new_compute_sems()

    def final_wait(s, e='sp'):
        for k in list(s.sem):
            if k != e and s.cnt[k] > 0:
                s._wait(e, (k, s.cnt[k]))


def build(stop_after=None, dbg=False, only=None):
    nc = bass.Bass("TRN2", target_bir_lowering=False)
    okind = "ExternalOutput" if dbg else "Internal"

    def din(name, shape, dt=F32):
        return nc.dram_tensor(name, list(shape), dt, kind="ExternalInput").ap()

    def dscr(name, shape, dt=BF16):
        return nc.dram_tensor(name, list(shape), dt, kind=okind).ap()

    xT_in = din("xT", [D, TL])
    ctxT_in = din("ctxT", [D, CT])
    pp_in = din("pp", [128, NPP])
    wada_in = din("w_ada", [2, D, 6 * D])
    winx_in = din("winx", [2, D, NX])
    gwp_in = din("gwp", [64, 2, 256])
    onesrow_in = din("ones_row", [1, TT])
    cs_in = din("cossin", [128, 2, TT])
    cst_in = din("cst", [128, 3, 128])
    ctab_in = din("ctab", [2, 128, NCTAB * 8 * 128])
    wba_in = din("w_branch_a", [2, 512, D])
    wbb_in = din("w_branch_b", [2, 512, D])
    wbc_in = din("w_branch_c", [2, 512, D])
    wmerge_in = din("w_merge", [2, D, 3 * D])
    wout_in = din("w_out", [2, D, D])
    wffi_in = din("w_ffn_in", [2, D, 2 * HID])
    wffo_in = din("w_ffn_out", [2, HID, D])
    outT = nc.dram_tensor("outT", [D, 2048], F32, kind="ExternalOutput").ap()

    S = {}
    S['hT'] = dscr("s_hT", [D, TT])
    S['Aq'] = dscr("s_Aq", [512, TT]); S['Ak'] = dscr("s_Ak", [128, TT])
    S['Cq'] = dscr("s_Cq", [512, TT]); S['Ck'] = dscr("s_Ck", [512, TT])
    S['Bq'] = dscr("s_Bq", [256, TT]); S['Bk'] = dscr("s_Bk", [256, TT]); S['Br'] = dscr("s_Br", [512, TT])
    S['Bz'] = dscr("s_Bz", [64, TT], F32)
    S['Av'] = dscr("s_Av", [TT, 256]); S['Bv'] = dscr("s_Bv", [TT, 512]); S['Cv'] = dscr("s_Cv", [TT, 1024])
    S['ya'] = dscr("s_ya", [512, TT]); S['yb'] = dscr("s_yb", [512, TT]); S['yc'] = dscr("s_yc", [512, TT])
    S['xs'] = dscr("s_xs", [D, TT], F32)
    S['hf'] = dscr("s_hf", [D, TT])
    if dbg:
        S['mod'] = dscr("s_mod", [128, 2 * 6 * 8 * 2], F32)
        S['oT'] = dscr("s_oT", [512, TO], F32)

    with contextlib.ExitStack() as es:
        T = Trk(nc, es)

        ucnt = [0]

        def sbuf(st, name, shape, dt):
            ucnt[0] += 1
            return st.enter_context(nc.sbuf_tensor(f"{name}_{ucnt[0]}", list(shape), dt))

        ps = [es.enter_context(nc.psum_tensor(f"ps{i}", [128, 512], F32)) for i in range(7)]
        psb = es.enter_context(nc.psum_tensor("psb", [128, 1024], BF16))
        ppt = sbuf(es, "ppt", [128, NPP], F32)
        modt = sbuf(es, "modt", [128, 2, 6, 8, 2], F32)
        cst = sbuf(es, "cstf", [128, 3, 128], F32)
        cstb = sbuf(es, "cstb", [128, 3, 128], BF16)
        ones_bf = sbuf(es, "ones_bf", [128, 128], BF16)
        ones_f = sbuf(es, "ones_f", [128, 128], F32)

        T.dma('sp', 'g0', out=ppt[:], in_=pp_in, writes=['ppt'])
        T.dma('sp', 'g1', out=cst[:], in_=cst_in, writes=['cst'])
        T.op('dve', lambda e: e.tensor_copy(out=cstb[:], in_=cst[:]), reads=['cst'], writes=['cstb'])
        T.op('pool', lambda e: e.memset(ones_bf[:], 1.0), writes=['ones_bf'])
        T.op('pool', lambda e: e.memset(ones_f[:], 1.0), writes=['ones_f'])

        def ppv(name, n):
            return ppt[:, PP[name]:PP[name] + n]

        sc = sbuf(es, "ada_sc", [128, 16], F32)
        adatmp = sbuf(es, "ada_tmp", [128, 16], F32)
        T.op('act', lambda e: e.activation(out=sc[:], in_=ppv('c', 16), func=AF.Silu), reads=['ppt'], writes=['sc'])
        scv = sc[:].rearrange("p (k i) -> p k i", i=2)
        adacnt = [0]

        def ada_piece(l, j, wa, pbank):
            sl = adacnt[0] % len(wa)
            adacnt[0] += 1
            T.dma('sp', f'adaw{sl}', out=wa[sl][:],
                  in_=wada_in[l, :, j * 1024:(j + 1) * 1024].rearrange("(k p) n -> p k n", p=128), writes=[('wa', sl)])

            def mm(e):
                last = None
                for oc in range(8):
                    for k in range(8):
                        last = e.matmul(ps[pbank][:, oc * 2:oc * 2 + 2], wa[sl][:, k, oc * 128:(oc + 1) * 128], scv[:, k, :],
                                        start=(k == 0), stop=(k == 7))
                return last
            T.op('pe', mm, reads=[('wa', sl), 'sc'], writes=[('ps', pbank)])
            b0 = PP[f'bada{l}'] + j * 16
            T.op('dve', lambda e: e.tensor_tensor(out=adatmp[:], in0=ps[pbank][:, 0:16], in1=ppt[:, b0:b0 + 16], op=ALU.add),
                 reads=[('ps', pbank), 'ppt'], writes=['adatmp'])
            dst = {0: 1, 1: 0, 2: 2, 3: 4, 4: 3, 5: 5}[j]
            tv = adatmp[:].rearrange("p (k i) -> p k i", i=2)
            if j in (1, 4):
                nname = f'nmix{l}' if j == 1 else f'nffn{l}'
                T.op('dve', lambda e: e.tensor_scalar(out=adatmp[:], in0=adatmp[:], scalar1=1.0, scalar2=None, op0=ALU.add), reads=['adatmp'], writes=['adatmp'])
                T.op('dve', lambda e: e.tensor_tensor(out=modt[:, l, dst], in0=tv, in1=ppv(nname, 16).rearrange("p (k i) -> p k i", i=2), op=ALU.mult),
                     reads=['adatmp', 'ppt'], writes=[('modt', l, dst)])
            else:
                T.op('dve', lambda e: e.tensor_copy(out=modt[:, l, dst], in_=tv), reads=['adatmp'], writes=[('modt', l, dst)])

        def phase_ada():
            with contextlib.ExitStack() as ph:
                wa = [sbuf(ph, f"ada_w{i}", [128, 8, 1024], F32) for i in range(2)]
                ada_piece(0, 0, wa, 0)
                ada_piece(0, 1, wa, 1)
            T.barrier()

        def norm_block(xt, xkey, sq, rs, tmpf, hT, hkp, n, l, which, isctx):
            T.op('act', lambda e: e.activation(out=sq[:, :, 0:n], in_=xt[:, :, 0:n], func=AF.Square),
                 reads=[xkey], writes=['nb_sq'])

            def mm(e):
                last = None
                for k in range(8):
                    last = e.matmul(ps[6][:, 0:n], ones_bf[:], sq[:, k, 0:n], start=(k == 0), stop=(k == 7))
                return last
            T.op('pe', mm, reads=['nb_sq', 'ones_bf'], writes=[('ps', 6)])
            T.op('dve', lambda e: e.tensor_scalar(out=rs[:, 0:n], in0=ps[6][:, 0:n], scalar1=1.0 / D, scalar2=EPS, op0=ALU.mult, op1=ALU.add),
                 reads=[('ps', 6)], writes=['nb_rs'])
            T.op('act', lambda e: e.activation(out=rs[:, 0:n], in_=rs[:, 0:n], func=AF.Sqrt), reads=['nb_rs'], writes=['nb_rs'])
            T.op('dve', lambda e: e.reciprocal(out=rs[:, 0:n], in_=rs[:, 0:n]), reads=['nb_rs'], writes=['nb_rs'])
            for k in range(8):
                T.op('dve', lambda e, k=k: e.scalar_tensor_tensor(out=tmpf[:, k % 2, 0:n], in0=xt[:, k, 0:n], scalar=modt[:, l, which, k, isctx:isctx + 1],
                                                                  in1=rs[:, 0:n], op0=ALU.mult, op1=ALU.mult),
                     reads=[xkey, 'nb_rs'], writes=[('nb_tmpf', k % 2)])
                T.op('act', lambda e, k=k: e.activation(out=hT[:, k, 0:n], in_=tmpf[:, k % 2, 0:n], func=AF.Identity,
                                                        bias=modt[:, l, which + 1, k, isctx:isctx + 1], scale=1.0),
                     reads=[('nb_tmpf', k % 2)], writes=[(hkp, 'hT', k)])

        def phase_B(l):
            with contextlib.ExitStack() as ph:
                w = sbuf(ph, "B_w", [128, 8, NX], BF16)
                xts = [sbuf(ph, f"B_xt{i}", [128, 8, 512], F32) for i in range(2)]
                hTs = [sbuf(ph, f"B_hT{i}", [128, 8, 512], BF16) for i in range(2)]
                sq = sbuf(ph, "B_sq", [128, 8, 512], BF16)
                rs = sbuf(ph, "B_rs", [128, 512], F32)
                tmpf = sbuf(ph, "B_tmpf", [128, 2, 512], F32)
                cs = sbuf(ph, "B_cs", [128, 2, 512], F32)
                r1 = sbuf(ph, "B_r1", [128, 2, 512], F32)
                r2 = sbuf(ph, "B_r2", [128, 2, 512], F32)
                fmA = sbuf(ph, "B_fmA", [128, 5, 512], BF16)
                fmC = sbuf(ph, "B_fmC", [128, 8, 512], BF16)
                fmB = sbuf(ph, "B_fmB", [128, 8, 512], BF16)
                fmZ = sbuf(ph, "B_fmZ", [64, 512], F32)
                tmA = sbuf(ph, "B_tmA", [128, 4, 2, 128], BF16)
                tmB = sbuf(ph, "B_tmB", [128, 4, 512], BF16)
                tmC = sbuf(ph, "B_tmC", [128, 4, 8, 128], BF16)
                for k in range(8):
                    T.dma('pool', f'wB{k}', out=w[:, k, :], in_=winx_in[l, k * 128:(k + 1) * 128, :], writes=[('w', k)])
                wkeys = [('w', k) for k in range(8)]
                T.op('pool', lambda e: e.memset(tmA[:], 1.0), writes=[('tmA', j) for j in range(4)])
                T.op('pool', lambda e: e.memset(tmC[:], 1.0), writes=[('tmC', j) for j in range(4)])
                psrot = [0]

                def nextps():
                    i = psrot[0] % 6
                    psrot[0] += 1
                    return i

                for bi, (t0, n, kind, isctx) in enumerate(BLOCKS[l]):
                    sl = bi % 2
                    xt, hT = xts[sl], hTs[sl]
                    key = ('blk', sl)
                    if l == 0:
                        src = (ctxT_in if isctx else xT_in[:, t0:t0 + n])
                    else:
                        src = S['xs'][:, t0:t0 + n]
                    T.dma('sp', f'Bx{sl}', out=xt[:, :, 0:n], in_=src.rearrange("(k p) t -> p k t", p=128), writes=[(key, 'xt')])
                    T.dma('sp', 'Bcs', out=cs[:, :, 0:n], in_=cs_in[:, :, t0:t0 + n], writes=['cs'])
                    norm_block(xt, (key, 'xt'), sq, rs, tmpf, hT, key, n, l, 0, isctx)
                    hkeys = [(key, 'hT', k) for k in range(8)]
                    if kind == 'F':
                        T.dma('sp', f'BhT{sl}', out=S['hT'][:, t0:t0 + n].rearrange("(k p) t -> p k t", p=128), in_=hT[:, :, 0:n], reads=hkeys)

                    def proj_fm(grp, c, width=128):
                        pi = nextps()
                        c0 = COL[grp] + c * 128

                        def mm(e):
                            last = None
                            for k in range(8):
                                last = e.matmul(ps[pi][0:width, 0:n], w[:, k, c0:c0 + width], hT[:, k, 0:n], start=(k == 0), stop=(k == 7))
                            return last
                        T.op('pe', mm, reads=wkeys + hkeys, writes=[('ps', pi)])
                        return pi

                    def rope_chunk(grp, grps, c, dst, dkey):
                        p1 = proj_fm(grp, c)
                        p2 = proj_fm(grps, c)
                        rr = (c % 2)
                        T.op('dve', lambda e: e.tensor_tensor(out=r1[:, rr, 0:n], in0=ps[p1][:, 0:n], in1=cs[:, 0, 0:n], op=ALU.mult),
                             reads=[('ps', p1), 'cs'], writes=[('r1', rr)])
                        T.op('dve', lambda e: e.tensor_tensor(out=r2[:, rr, 0:n], in0=ps[p2][:, 0:n], in1=cs[:, 1, 0:n], op=ALU.mult),
                             reads=[('ps', p2), 'cs'], writes=[('r2', rr)])
                        T.op('pool', lambda e: e.tensor_tensor(out=dst, in0=r1[:, rr, 0:n], in1=r2[:, rr, 0:n], op=ALU.add),
                             reads=[('r1', rr), ('r2', rr)], writes=[dkey])

                    evr = [0]

                    def evac(pi, dst, dkey, scale=None, func=None, width=128):
                        src_ = ps[pi][0:width, 0:n]
                        if func is not None:
                            T.op('act', lambda e: e.activation(out=dst, in_=src_, func=func), reads=[('ps', pi)], writes=[dkey])
                        elif scale is not None:
                            T.op('act', lambda e: e.activation(out=dst, in_=src_, func=AF.Copy, scale=scale), reads=[('ps', pi)], writes=[dkey])
                        else:
                            evr[0] += 1
                            if evr[0] % 2:
                                T.op('act', lambda e: e.activation(out=dst, in_=src_, func=AF.Copy), reads=[('ps', pi)], writes=[dkey])
                            else:
                                T.op('dve', lambda e: e.tensor_copy(out=dst, in_=src_), reads=[('ps', pi)], writes=[dkey])

                    def store_fm(name, tile_ap, nchunks, keys, rows=128):
                        T.dma('sp', f'st_{name}', out=S[name][:, t0:t0 + n].rearrange("(c p) t -> p c t", p=rows) if nchunks > 1 else S[name][:, t0:t0 + n],
                              in_=tile_ap, reads=keys)

                    if kind == 'F':
                        for c in range(4):
                            rope_chunk('Aq', 'Aqs', c, fmA[:, c, 0:n], ('fmA', c))
                        store_fm('Aq', fmA[:, 0:4, 0:n], 4, [('fmA', c) for c in range(4)])
                    rope_chunk('Ak', 'Aks', 0, fmA[:, 4, 0:n], ('fmA', 4))
                    store_fm('Ak', fmA[:, 4, 0:n], 1, [('fmA', 4)])
                    if kind == 'F':
                        for c in range(4):
                            evac(proj_fm('Cq', c), fmC[:, c, 0:n], ('fmC', c))
                        store_fm('Cq', fmC[:, 0:4, 0:n], 4, [('fmC', c) for c in range(4)])
                    for c in range(4):
                        evac(proj_fm('Ck', c), fmC[:, 4 + c, 0:n], ('fmC', 4 + c))
                    store_fm('Ck', fmC[:, 4:8, 0:n], 4, [('fmC', 4 + c) for c in range(4)])
                    if kind == 'F':
                        for c in range(2):
                            evac(proj_fm('Bq', c), fmB[:, c, 0:n], ('fmB', c), scale=0.125)
                        store_fm('Bq', fmB[:, 0:2, 0:n], 2, [('fmB', c) for c in range(2)])
                        for c in range(4):
                            evac(proj_fm('Br', c), fmB[:, 4 + c, 0:n], ('fmB', 4 + c), func=AF.Silu)
                        store_fm('Br', fmB[:, 4:8, 0:n], 4, [('fmB', 4 + c) for c in range(4)])
                    for c in range(2):
                        evac(proj_fm('Bk', c), fmB[:, 2 + c, 0:n], ('fmB', 2 + c))
                    store_fm('Bk', fmB[:, 2:4, 0:n], 2, [('fmB', 2 + c) for c in range(2)])
                    pz = proj_fm('Bz', 0, width=64)
                    T.op('dve', lambda e: e.tensor_copy(out=fmZ[:, 0:n], in_=ps[pz][0:64, 0:n]), reads=[('ps', pz)], writes=['fmZ'])
                    T.dma('sp', 'st_Bz', out=S['Bz'][:, t0:t0 + n], in_=fmZ[:, 0:n], reads=['fmZ'])
                    nj = n // 128
                    for j in range(nj):
                        def proj_tm(grp, ncols, j=j):
                            pi = nextps()
                            c0 = COL[grp]

                            def mm(e):
                                last = None
                                for k in range(8):
                                    last = e.matmul(ps[pi][:, 0:ncols], hT[:, k, j * 128:(j + 1) * 128], w[:, k, c0:c0 + ncols], start=(k == 0), stop=(k == 7))
                                return last
                            T.op('pe', mm, reads=wkeys + hkeys, writes=[('ps', pi)])
                            return pi
                        pa = proj_tm('Av', 128)
                        T.op('dve', lambda e, pa=pa, j=j: e.tensor_copy(out=tmA[:, j, :, 0:64], in_=ps[pa][:, 0:128].rearrange("p (g d) -> p g d", g=2)),
                             reads=[('ps', pa)], writes=[('tmA', j)])
                        pb = proj_tm('Bv', 512)
                        T.op('act', lambda e, pb=pb, j=j: e.activation(out=tmB[:, j, :], in_=ps[pb][:, 0:512], func=AF.Copy),
                             reads=[('ps', pb)], writes=[('tmB', j)])
                        pc = proj_tm('Cv', 512)
                        T.op('dve', lambda e, pc=pc, j=j: e.tensor_copy(out=tmC[:, j, :, 0:64], in_=ps[pc][:, 0:512].rearrange("p (g d) -> p g d", g=8)),
                             reads=[('ps', pc)], writes=[('tmC', j)])
                    T.dma('sp', 'st_Av', out=S['Av'][t0:t0 + n, :].rearrange("(j p) c -> p j c", p=128),
                          in_=tmA[:, 0:nj].rearrange("p j g d -> p j (g d)"), reads=[('tmA', j) for j in range(nj)])
                    T.dma('sp', 'st_Bv', out=S['Bv'][t0:t0 + n, :].rearrange("(j p) c -> p j c", p=128),
                          in_=tmB[:, 0:nj, :], reads=[('tmB', j) for j in range(nj)])
                    T.dma('sp', 'st_Cv', out=S['Cv'][t0:t0 + n, :].rearrange("(j p) c -> p j c", p=128),
                          in_=tmC[:, 0:nj].rearrange("p j g d -> p j (g d)"), reads=[('tmC', j) for j in range(nj)])
            T.barrier()


        def phase_G(l):
            with contextlib.ExitStack() as ph:
                qT = sbuf(ph, "G_q", [128, 2, TT], BF16)
                kT = sbuf(ph, "G_k", [128, 2, TT], BF16)
                v = sbuf(ph, "G_v", [128, TT // 128, 512], BF16)
                zT = sbuf(ph, "G_z", [64, TT], F32)
                oT = sbuf(ph, "G_o", [128, 4, TO], F32)
                gw = sbuf(ph, "G_gw", [64, 256], F32)
                mrep = sbuf(ph, "G_mrep", [128, 2, 4, 128], F32)
                NB = 3
                Sst = [sbuf(ph, f"G_S{i}", [128, 2, 128], F32) for i in range(2)]
                Sbf = [sbuf(ph, f"G_Sbf{i}", [128, 2, 128], BF16) for i in range(2)]
                e1 = [sbuf(ph, f"G_e1{i}", [128, 256], F32) for i in range(NB)]
                gpos = [sbuf(ph, f"G_gp{i}", [128, 256], F32) for i in range(NB)]
                eb = [sbuf(ph, f"G_eb{i}", [128, 2, 128], F32) for i in range(NB)]
                enb = [sbuf(ph, f"G_enb{i}", [128, 2, 128], F32) for i in range(NB)]
                qt = [sbuf(ph, f"G_qt{i}", [128, 2, 128], BF16) for i in range(NB)]
                kt = [sbuf(ph, f"G_kt{i}", [128, 2, 128], BF16) for i in range(NB)]
                ktl = [sbuf(ph, f"G_ktl{i}", [128, 2, 128], BF16) for i in range(NB)]
                ktT = [sbuf(ph, f"G_ktT{i}", [128, 2, 128], BF16) for i in range(NB)]
                Am = [sbuf(ph, f"G_Am{i}", [128, 2, 2, 128], BF16) for i in range(NB)]
                T.dma('sp', 'Gq', out=qT[:], in_=S['Bq'].rearrange("(c p) t -> p c t", p=128), writes=['qT'])
                T.dma('sp', 'Gk', out=kT[:], in_=S['Bk'].rearrange("(c p) t -> p c t", p=128), writes=['kT'])
                T.dma('sp', 'Gv', out=v[:], in_=S['Bv'].rearrange("(j p) c -> p j c", p=128), writes=['v'])
                T.dma('sp', 'Gz', out=zT[:], in_=S['Bz'], writes=['zT'])
                T.dma('sp', 'Ggw', out=gw[:], in_=gwp_in[:, l, :], writes=['gw'])
                T.dma('sp', 'Gz', out=zT[16:17, :], in_=onesrow_in, writes=['zT'])
                T.dma('sp', 'Gz', out=zT[48:49, :], in_=onesrow_in, writes=['zT'])
                for d_ in range(2):
                    for h in range(4):
                        T.op('pool', lambda e, d_=d_, h=h: e.tensor_copy(out=mrep[:, d_, h, :], in_=cst[:, d_, :]), reads=['cst'], writes=['mrep'])

                def prep(idx, item):
                    if item[0] != 'chunk':
                        return
                    _, tok0, d_, need_out, ocol = item
                    cb = idx % NB
                    pz = idx % 2
                    rows = slice(0, 17) if d_ == 0 else slice(32, 49)
                    last = 127 if d_ == 0 else 0
                    bz = ps[pz]
                    kz, kcs = ('psz', pz), ('pscs', pz)
                    T.op('pe', lambda e: e.matmul(bz[:, 0:256], zT[rows, tok0:tok0 + 128], gw[rows, :], start=True, stop=True), reads=['zT', 'gw'], writes=[kz])
                    T.op('act', lambda e: e.activation(out=e1[cb][:], in_=bz[:, 0:256], func=AF.Exp, scale=-1.0), reads=[kz], writes=[('e1', cb)])
                    T.op('act', lambda e: e.activation(out=gpos[cb][:], in_=e1[cb][:], func=AF.Ln, bias=1.0, scale=1.0), reads=[('e1', cb)], writes=[('gpos', cb)])

                    def mmcs(e):
                        e.matmul(bz[:, 256:384], gpos[cb][:, 0:128], cst[:, d_, :], start=True, stop=True)
                        return e.matmul(bz[:, 384:512], gpos[cb][:, 128:256], cst[:, d_, :], start=True, stop=True)
                    T.op('pe', mmcs, reads=[('gpos', cb), 'cst'], writes=[kcs])
                    csv = bz[:, 256:512].rearrange("p (a t) -> p a t", a=2)
                    T.op('act', lambda e: e.activation(out=eb[cb][:], in_=csv, func=AF.Exp, scale=-1.0 / 16), reads=[kcs], writes=[('eb', cb)])
                    T.op('act', lambda e: e.activation(out=enb[cb][:], in_=csv, func=AF.Exp, scale=1.0 / 16), reads=[kcs], writes=[('enb', cb)])
                    if need_out:
                        T.op('dve', lambda e: e.tensor_tensor(out=qt[cb][:], in0=qT[:, :, tok0:tok0 + 128], in1=eb[cb][:], op=ALU.mult),
                             reads=['qT', ('eb', cb)], writes=[('qt', cb)])
                    T.op('dve', lambda e: e.tensor_tensor(out=kt[cb][:], in0=kT[:, :, tok0:tok0 + 128], in1=enb[cb][:], op=ALU.mult),
                         reads=['kT', ('enb', cb)], writes=[('kt', cb)])
                    for p in range(2):
                        T.op('pool', lambda e, p=p: e.tensor_scalar(out=ktl[cb][:, p, :], in0=kt[cb][:, p, :], scalar1=eb[cb][:, p, last:last + 1], scalar2=None, op0=ALU.mult),
                             reads=[('kt', cb), ('eb', cb)], writes=[('ktl', cb, p)])
                    tb = (idx % 4) * 256

                    def mmtr(e):
                        e.transpose(psb[:, tb:tb + 128], ktl[cb][:, 0, :], cstb[:, 2, :])
                        return e.transpose(psb[:, tb + 128:tb + 256], ktl[cb][:, 1, :], cstb[:, 2, :])
                    T.op('pe', mmtr, reads=[('ktl', cb, 0), ('ktl', cb, 1), 'cstb'], writes=[('psb', idx % 4)])
                    T.op('act', lambda e: e.activation(out=ktT[cb][:].rearrange("p a t -> p (a t)"), in_=psb[:, tb:tb + 256], func=AF.Copy),
                         reads=[('psb', idx % 4)], writes=[('ktT', cb)])
                    if need_out:
                        def mmA(e):
                            last_ = None
                            for h in (0, 2, 1, 3):
                                p, r = h // 2, slice((h % 2) * 64, (h % 2) * 64 + 64)
                                last_ = e.matmul(ps[2 + h % 2][:, p * 128:(p + 1) * 128], kt[cb][r, p, :], qt[cb][r, p, :], start=True, stop=True)
                            return last_
                        T.op('pe', mmA, reads=[('kt', cb), ('qt', cb)], writes=[('ps', 2), ('ps', 3)])
                        for par in range(2):
                            T.op('dve', lambda e, par=par: e.tensor_tensor(out=Am[cb][:, par], in0=ps[2 + par][:, 0:256].rearrange("p (h t) -> p h t", h=2), in1=mrep[:, d_, 0:2, :], op=ALU.mult),
                                 reads=[('ps', 2 + par), 'mrep'], writes=[('Am', cb, par)])

                written = set()

                def zero_state(d_):
                    T.op('dve', lambda e: e.memset(Sst[d_][:], 0.0), writes=[('S', d_, h) for h in range(4)])
                    T.op('dve', lambda e: e.memset(Sbf[d_][:], 0.0), writes=[('Sbf', d_)])

                def fin(idx, item):
                    if item[0] == 'reset':
                        zero_state(item[1])
                        return
                    _, tok0, d_, need_out, ocol = item
                    cb = idx % NB
                    last = 127 if d_ == 0 else 0
                    cj = tok0 // 128
                    if need_out:
                        pO = ps[4 + idx % 2]

                        def mmO(e):
                            last_ = None
                            for h in range(4):
                                p, r = h // 2, slice((h % 2) * 64, (h % 2) * 64 + 64)
                                e.matmul(pO[:, h * 128:(h + 1) * 128], v[:, cj, h * 128:(h + 1) * 128], Am[cb][:, h % 2, h // 2, :], start=True, stop=False)
                                last_ = e.matmul(pO[:, h * 128:(h + 1) * 128], Sbf[d_][r, p, :], qt[cb][r, p, :], start=False, stop=True)
                            return last_
                        T.op('pe', mmO, reads=['v', ('Am', cb, 0), ('Am', cb, 1), ('Sbf', d_), ('qt', cb)], writes=[('ps', 4 + idx % 2)])
                        pOv = pO[:, 0:512].rearrange("p (h t) -> p h t", h=4)
                        okey = ('oT', ocol)
                        if ocol not in written:
                            written.add(ocol)
                            T.op('act', lambda e: e.activation(out=oT[:, :, ocol:ocol + 128], in_=pOv, func=AF.Copy), reads=[('ps', 4 + idx % 2)], writes=[okey])
                        else:
                            T.op('dve', lambda e: e.tensor_tensor(out=oT[:, :, ocol:ocol + 128], in0=pOv, in1=oT[:, :, ocol:ocol + 128], op=ALU.add),
                                 reads=[('ps', 4 + idx % 2), okey], writes=[okey])
                    pU = ps[6]

                    def mmU(e):
                        last_ = None
                        for h in range(4):
                            last_ = e.matmul(pU[:, h * 128:(h + 1) * 128], ktT[cb][:, h // 2, :], v[:, cj, h * 128:(h + 1) * 128], start=True, stop=True)
                        return last_
                    T.op('pe', mmU, reads=[('ktT', cb), 'v'], writes=[('ps', 6)])
                    for h in range(4):
                        p, r = h // 2, slice((h % 2) * 64, (h % 2) * 64 + 64)
                        T.op('dve', lambda e, h=h, p=p, r=r: e.scalar_tensor_tensor(out=Sst[d_][r, p, :], in0=Sst[d_][r, p, :], scalar=eb[cb][r, p, last:last + 1],
                                                                                   in1=pU[r, h * 128:(h + 1) * 128], op0=ALU.mult, op1=ALU.add),
                             reads=[('ps', 6), ('eb', cb), ('S', d_, h)], writes=[('S', d_, h)])
                    T.op('act', lambda e: e.activation(out=Sbf[d_][:], in_=Sst[d_][:], func=AF.Copy), reads=[('S', d_, h) for h in range(4)], writes=[('Sbf', d_)])

                n1 = TFULL[l] // 128
                ng = TGLA[l] // 128
                L1 = [('chunk', TL + j * 128, 0, l == 0, 2560 + j * 128) for j in range(2)]
                L1 += [('chunk', j * 128, 0, True, j * 128) for j in range(n1)]
                L2 = [('chunk', j * 128, 1, j < n1, j * 128) for j in range(ng - 1, -1, -1)]
                if l == 0:
                    L2 += [('reset', 1)] + [('chunk', TL + j * 128, 1, True, 2560 + j * 128) for j in (1, 0)]
                seq = []
                for i in range(max(len(L1), len(L2))):
                    if i < len(L2):
                        seq.append(L2[i])
                    if i < len(L1):
                        seq.append(L1[i])
                zero_state(0)
                zero_state(1)
                LA = 2
                for idx in range(len(seq) + LA):
                    if idx < len(seq):
                        prep(idx, seq[idx])
                    if idx - LA >= 0:
                        fin(idx - LA, seq[idx - LA])
                if dbg:
                    T.dma('sp', 'dbg', out=S['oT'].rearrange("(h p) t -> p h t", p=128), in_=oT[:], reads=[('oT', c_) for c_ in range(0, TO, 128)])
                with contextlib.ExitStack() as ph2:
                    br = [sbuf(ph2, f"G_br{i}", [128, 4, 512], BF16) for i in range(2)]
                    sq = [sbuf(ph2, f"G_sq{i}", [128, 512], BF16) for i in range(2)]
                    rs = [sbuf(ph2, f"G_rs{i}", [128, 512], F32) for i in range(2)]
                    yt = [sbuf(ph2, f"G_yt{i}", [128, 512], F32) for i in range(2)]
                    yst = [sbuf(ph2, f"G_yst{i}", [128, 4, 512], BF16) for i in range(2)]
                    blocks = [(t0, 512, t0) for t0 in range(0, TFULL[l], 512)]
                    if l == 0:
                        blocks.append((TL, 256, 2560))
                    it = 0
                    for bi, (t0, n, oc0) in enumerate(blocks):
                        sl = bi % 2
                        T.dma('sp', f'Gbr{sl}', out=br[sl][:, :, 0:n], in_=S['Br'][:, t0:t0 + n].rearrange("(h p) t -> p h t", p=128), writes=[('br', sl)])
                        okeys = [('oT', c_) for c_ in range(oc0, oc0 + n, 128)]
                        for h in range(4):
                            a = it % 2
                            it += 1
                            pi = 2 + a
                            T.op('act', lambda e, a=a, h=h: e.activation(out=sq[a][:, 0:n], in_=oT[:, h, oc0:oc0 + n], func=AF.Square), reads=okeys, writes=[('gsq', a)])
                            T.op('pe', lambda e, a=a, pi=pi: e.matmul(ps[pi][:, 0:n], ones_bf[:], sq[a][:, 0:n], start=True, stop=True), reads=[('gsq', a), 'ones_bf'], writes=[('ps', pi)])
                            T.op('dve', lambda e, a=a, pi=pi: e.tensor_scalar(out=rs[a][:, 0:n], in0=ps[pi][:, 0:n], scalar1=1.0 / 128, scalar2=EPS, op0=ALU.mult, op1=ALU.add),
                                 reads=[('ps', pi)], writes=[('grs', a)])
                            T.op('act', lambda e, a=a: e.activation(out=rs[a][:, 0:n], in_=rs[a][:, 0:n], func=AF.Sqrt), reads=[('grs', a)], writes=[('grs', a)])
                            T.op('dve', lambda e, a=a: e.reciprocal(out=rs[a][:, 0:n], in_=rs[a][:, 0:n]), reads=[('grs', a)], writes=[('grs', a)])
                            T.op('dve', lambda e, a=a, h=h: e.tensor_tensor(out=yt[a][:, 0:n], in0=oT[:, h, oc0:oc0 + n], in1=rs[a][:, 0:n], op=ALU.mult),
                                 reads=okeys + [('grs', a)], writes=[('gyt', a)])
                            T.op('dve', lambda e, a=a, h=h: e.scalar_tensor_tensor(out=yst[sl][:, h, 0:n], in0=yt[a][:, 0:n], scalar=ppv(f'gnorm{l}', 1),
                                                                                   in1=br[sl][:, h, 0:n], op0=ALU.mult, op1=ALU.mult),
                                 reads=[('gyt', a), ('br', sl), 'ppt'], writes=[('yst', sl, h)])
                        T.dma('sp', f'Gyb{sl}', out=S['yb'][:, t0:t0 + n].rearrange("(h p) t -> p h t", p=128), in_=yst[sl][:, :, 0:n],
                              reads=[('yst', sl, h) for h in range(4)])
            T.barrier()

        def phase_A(l):
            with contextlib.ExitStack() as ph:
                qT = sbuf(ph, "A_q", [128, 4, TT], BF16)
                kT = sbuf(ph, "A_k", [128, TT], BF16)
                v = sbuf(ph, "A_v", [128, TT // 128, 256], BF16)
                esk = sbuf(ph, "A_esk", [128, 8], F32)
                mrepb = sbuf(ph, "A_mrep", [128, 2, 4, 128], BF16)
                P = [sbuf(ph, f"A_P{i}", [128, 5, 512], BF16) for i in range(2)]
                den = [sbuf(ph, f"A_den{i}", [128, 512], F32) for i in range(2)]
                yst = [sbuf(ph, f"A_yst{i}", [64, 4, 128], BF16) for i in range(2)]
                ada_todo = []
                if l == 0:
                    wa = [sbuf(ph, f"A_adaw{i}", [128, 8, 1024], F32) for i in range(2)]
                    ada_todo = [(0, j) for j in range(2, 6)] + [(1, j) for j in range(6)]
                T.dma('sp', 'Aq', out=qT[:], in_=S['Aq'].rearrange("(c p) t -> p c t", p=128), writes=['qT'])
                T.dma('sp', 'Ak', out=kT[:], in_=S['Ak'], writes=['kT'])
                T.dma('sp', 'Av', out=v[:], in_=S['Av'].rearrange("(j p) c -> p j c", p=128), writes=['v'])
                T.op('act', lambda e: e.activation(out=esk[:], in_=ppv(f'sink{l}', 8), func=AF.Exp), reads=['ppt'], writes=['esk'])
                for d_ in range(2):
                    for h in range(4):
                        T.op('pool', lambda e, d_=d_, h=h: e.tensor_copy(out=mrepb[:, d_, h, :], in_=cstb[:, d_, :]), reads=['cstb'], writes=['mrepb'])
                qtiles = [(n, False) for n in range(TFULL[l] // 128)]
                if l == 0:
                    qtiles += [(24, True), (25, True)]
                u = 0
                sb_ = 0
                for (n, isctx) in qtiles:
                    if isctx:
                        klist = [(24, None), (25, None)]
                    else:
                        klist = ([(n - 1, 1)] if n > 0 else []) + [(n, None), (n + 1, 0), (24, None), (25, None)]
                    for g in range(2):
                        sl = u % 2
                        u += 1
                        if ada_todo and u % 4 == 1:
                            ada_piece(*ada_todo.pop(0), wa, 5)
                        r = slice(g * 64, g * 64 + 64)
                        for i, (kt_, m) in enumerate(klist):
                            pi = sb_ % 3
                            sb_ += 1
                            T.op('pe', lambda e, pi=pi, kt_=kt_: e.matmul(ps[pi][:, 0:512].rearrange("p (j t) -> p j t", j=4), kT[r, kt_ * 128:(kt_ + 1) * 128],
                                                                         qT[r, :, n * 128:(n + 1) * 128], start=True, stop=True),
                                 reads=['kT', 'qT'], writes=[('ps', pi)])
                            T.op('act', lambda e, pi=pi, i=i: e.activation(out=P[sl][:, i, :], in_=ps[pi][:, 0:512], func=AF.Exp, scale=0.125),
                                 reads=[('ps', pi)], writes=[('P', sl, i)])
                            if m is not None:
                                T.op('pool', lambda e, i=i, m=m: e.tensor_tensor(out=P[sl][:, i, :].rearrange("p (j t) -> p j t", j=4),
                                                                                in0=P[sl][:, i, :].rearrange("p (j t) -> p j t", j=4), in1=mrepb[:, m], op=ALU.mult),
                                     reads=[('P', sl, i), 'mrepb'], writes=[('P', sl, i)])
                        po = 3 + sl

                        def mmO(e, klist=klist, sl=sl, po=po, g=g):
                            last_ = None
                            for i, (kt_, m) in enumerate(klist):
                                last_ = e.matmul(ps[po][:, 0:512], v[:, kt_, g * 128:(g + 1) * 128], P[sl][:, i, :], start=(i == 0), stop=(i == len(klist) - 1))
                            return last_
                        T.op('pe', mmO, reads=['v'] + [('P', sl, i) for i in range(len(klist))], writes=[('ps', po)])
                        for j in range(4):
                            T.op('dve', lambda e, j=j, po=po, sl=sl, g=g: e.tensor_scalar(out=den[sl][64:128, j * 128:(j + 1) * 128], in0=ps[po][64:128, j * 128:(j + 1) * 128],
                                                                                        scalar1=esk[64:128, 4 * g + j:4 * g + j + 1], scalar2=None, op0=ALU.add),
                                 reads=[('ps', po), 'esk'], writes=[('den', sl, j)])
                        T.op('dve', lambda e, sl=sl: e.reciprocal(out=den[sl][64:128, :], in_=den[sl][64:128, :]), reads=[('den', sl, j) for j in range(4)], writes=[('den', sl)])
                        T.op('dve', lambda e, sl=sl, po=po: e.tensor_tensor(out=yst[sl][0:64, :, :], in0=ps[po][0:64, 0:512].rearrange("p (j t) -> p j t", j=4),
                                                                            in1=den[sl][64:128, :].rearrange("p (j t) -> p j t", j=4), op=ALU.mult),
                             reads=[('ps', po), ('den', sl)], writes=[('yst', sl)])
                        T.dma('sp', f'Ayst{sl}', out=S['ya'][g * 256:(g + 1) * 256, n * 128:(n + 1) * 128].rearrange("(j d) t -> d j t", d=64), in_=yst[sl][:],
                              reads=[('yst', sl)])
                assert not ada_todo
                if dbg and l == 0:
                    T.dma('sp', 'dbg', out=S['mod'], in_=modt[:].rearrange("p l a k i -> p (l a k i)"),
                          reads=[('modt', l_, a_) for l_ in range(2) for a_ in range(6)])
            T.barrier()

        def phase_C(l):
            with contextlib.ExitStack() as ph:
                qT = sbuf(ph, "C_q", [128, 4, TT], BF16)
                kT = sbuf(ph, "C_k", [128, 4, TT], BF16)
                v = sbuf(ph, "C_v", [128, TT // 128, 1024], BF16)
                EB = sbuf(ph, "C_EB", [128, NCTAB, 512 * 2], BF16)
                P = [sbuf(ph, f"C_P{i}", [128, 7, 512], BF16) for i in range(2)]
                den = [sbuf(ph, f"C_den{i}", [128, 512], F32) for i in range(2)]
                yst = [sbuf(ph, f"C_yst{i}", [64, 4, 128], BF16) for i in range(2)]
                T.dma('sp', 'Cq', out=qT[:], in_=S['Cq'].rearrange("(c p) t -> p c t", p=128), writes=['qT'])
                T.dma('sp', 'Ck', out=kT[:], in_=S['Ck'].rearrange("(c p) t -> p c t", p=128), writes=['kT'])
                T.dma('sp', 'Cv', out=v[:], in_=S['Cv'].rearrange("(j p) c -> p j c", p=128), writes=['v'])
                T.dma('pool', 'Ctab', out=EB[:].rearrange("p a b -> p (a b)"), in_=ctab_in[l], writes=['EBraw'])
                for a in range(NCTAB):
                    T.op('act', lambda e, a=a: e.activation(out=EB[:, a, :], in_=EB[:, a, :], func=AF.Exp), reads=['EBraw'], writes=[('EB', a)])
                ebkeys = [('EB', a) for a in range(NCTAB)]
                qtiles = [(n, False) for n in range(TFULL[l] // 128)]
                if l == 0:
                    qtiles += [(24, True), (25, True)]
                u = 0
                sb_ = 0
                alt = 0
                for (n, isctx) in qtiles:
                    if isctx:
                        klist = [(24, None), (25, None)]
                    elif n == 0:
                        klist = [(0, 0), (1, 1), (2, 2), (3, 3), (24, None), (25, None)]
                    elif n == 1:
                        klist = [(0, 4), (1, 5), (2, 6), (3, 7), (24, None), (25, None)]
                    else:
                        klist = [(n - 2 + i, 8 + i) for i in range(5)] + [(24, None), (25, None)]
                    for hq in range(2):
                        sl = u % 2
                        u += 1
                        for i, (kt_, ti) in enumerate(klist):
                            pr_ = (sb_ % 2) * 2
                            sb_ += 1

                            def mmS(e, pr_=pr_, kt_=kt_, hq=hq):
                                last_ = None
                                for j in (0, 2, 1, 3):
                                    h = 4 * hq + j
                                    r = slice((h % 2) * 64, (h % 2) * 64 + 64)
                                    last_ = e.matmul(ps[pr_ + j % 2][:, (j // 2) * 128:(j // 2 + 1) * 128], kT[r, h // 2, kt_ * 128:(kt_ + 1) * 128],
                                                     qT[r, h // 2, n * 128:(n + 1) * 128], start=True, stop=True)
                                return last_
                            T.op('pe', mmS, reads=['kT', 'qT'], writes=[('ps', pr_), ('ps', pr_ + 1)])
                            for par in range(2):
                                T.op('act', lambda e, pr_=pr_, i=i, sl=sl, par=par: e.activation(out=P[sl][:, i, par * 256:(par + 1) * 256], in_=ps[pr_ + par][:, 0:256], func=AF.Exp, scale=0.125),
                                     reads=[('ps', pr_ + par)], writes=[('P', sl, i, par)])
                            if ti is not None:
                                alt += 1
                                eng = 'pool' if alt % 2 else 'dve'
                                T.op(eng, lambda e, i=i, ti=ti, sl=sl, hq=hq: e.tensor_tensor(out=P[sl][:, i, :], in0=P[sl][:, i, :], in1=EB[:, ti, hq * 512:(hq + 1) * 512], op=ALU.mult),
                                     reads=[('P', sl, i, 0), ('P', sl, i, 1)] + ebkeys, writes=[('P', sl, i, 0), ('P', sl, i, 1)])
                        po = 4 + sl

                        def mmO(e, klist=klist, sl=sl, po=po, hq=hq):
                            last_ = None
                            for j in range(4):
                                h = 4 * hq + j
                                sj = (j % 2) * 2 + j // 2
                                for i, (kt_, ti) in enumerate(klist):
                                    last_ = e.matmul(ps[po][:, j * 128:(j + 1) * 128], v[:, kt_, h * 128:(h + 1) * 128], P[sl][:, i, sj * 128:(sj + 1) * 128],
                                                     start=(i == 0), stop=(i == len(klist) - 1))
                            return last_
                        T.op('pe', mmO, reads=['v'] + [('P', sl, i, par) for i in range(len(klist)) for par in range(2)], writes=[('ps', po)])
                        T.op('dve', lambda e, sl=sl, po=po: e.reciprocal(out=den[sl][64:128, :], in_=ps[po][64:128, 0:512]), reads=[('ps', po)], writes=[('den', sl)])
                        T.op('dve', lambda e, sl=sl, po=po: e.tensor_tensor(out=yst[sl][0:64, :, :], in0=ps[po][0:64, 0:512].rearrange("p (j t) -> p j t", j=4),
                                                                            in1=den[sl][64:128, :].rearrange("p (j t) -> p j t", j=4), op=ALU.mult),
                             reads=[('ps', po), ('den', sl)], writes=[('yst', sl)])
                        T.dma('sp', f'Cyst{sl}', out=S['yc'][hq * 256:(hq + 1) * 256, n * 128:(n + 1) * 128].rearrange("(j d) t -> d j t", d=64), in_=yst[sl][:],
                              reads=[('yst', sl)])
            T.barrier()

        def phase_M(l):
            with contextlib.ExitStack() as ph:
                wm = sbuf(ph, "M_wm", [128, 8, 3072], BF16)
                wb = sbuf(ph, "M_wb", [128, 3, 4, 1024], BF16)
                wo = sbuf(ph, "M_wo", [128, 8, 1024], BF16)
                hTs = [sbuf(ph, f"M_hT{i}", [128, 8, 512], BF16) for i in range(2)]
                ys = [sbuf(ph, f"M_y{i}", [128, 3, 4, 512], BF16) for i in range(2)]
                xts = [sbuf(ph, f"M_xt{i}", [128, 8, 512], F32) for i in range(2)]
                mix = sbuf(ph, "M_mix", [128, 8, 512], BF16)
                gsb = sbuf(ph, "M_gsb", [128, 3, 512], F32)
                mt = sbuf(ph, "M_mt", [128, 3, 512], F32)
                sq = sbuf(ph, "M_sq", [128, 8, 512], BF16)
                rs = sbuf(ph, "M_rs", [128, 512], F32)
                tmpf = sbuf(ph, "M_tmpf", [128, 2, 512], F32)
                hf = sbuf(ph, "M_hf", [128, 8, 512], BF16)
                for k in range(8):
                    T.dma('pool', f'wM{k}', out=wm[:, k, :], in_=wmerge_in[l, k * 128:(k + 1) * 128, :], writes=[('wm', k)])
                for bi_, wsrc in enumerate((wba_in, wbb_in, wbc_in)):
                    T.dma('pool', f'wMb{bi_}', out=wb[:, bi_], in_=wsrc[l].rearrange("(k p) n -> p k n", p=128), writes=[('wb', bi_)])
                T.dma('pool', 'wMo', out=wo[:], in_=wout_in[l].rearrange("(k p) n -> p k n", p=128), writes=['wo'])
                wmk = [('wm', k) for k in range(8)]
                blocks = [(t0, 512, 0) for t0 in range(0, TFULL[l], 512)]
                if l == 0:
                    blocks.append((TL, 256, 1))
                pr = [0]

                def nextps():
                    i = pr[0] % 6
                    pr[0] += 1
                    return i
                for bi, (t0, n, isctx) in enumerate(blocks):
                    sl = bi % 2
                    hT, y, xt = hTs[sl], ys[sl], xts[sl]
                    key = ('mblk', sl)
                    T.dma('sp', f'MhT{sl}', out=hT[:, :, 0:n], in_=S['hT'][:, t0:t0 + n].rearrange("(k p) t -> p k t", p=128), writes=[(key, 'hTm')])
                    for bi_, nm in enumerate(('ya', 'yb', 'yc')):
                        T.dma('sp', f'My{sl}', out=y[:, bi_, :, 0:n], in_=S[nm][:, t0:t0 + n].rearrange("(k p) t -> p k t", p=128), writes=[(key, 'y', bi_)])
                    if l == 0:
                        src = (ctxT_in if isctx else xT_in[:, t0:t0 + n])
                    else:
                        src = S['xs'][:, t0:t0 + n]
                    T.dma('sp', f'Mx{sl}', out=xt[:, :, 0:n], in_=src.rearrange("(k p) t -> p k t", p=128), reads=[('xsd', t0)], writes=[(key, 'xt')])
                    for oc in range(8):
                        for b_ in range(3):
                            pg = nextps()

                            def mmg(e, pg=pg, b_=b_, oc=oc):
                                last_ = None
                                c0 = b_ * 1024 + oc * 128
                                for k in range(8):
                                    last_ = e.matmul(ps[pg][:, 0:n], wm[:, k, c0:c0 + 128], hT[:, k, 0:n], start=(k == 0), stop=(k == 7))
                                return last_
                            T.op('pe', mmg, reads=wmk + [(key, 'hTm')], writes=[('ps', pg)])
                            T.op('act', lambda e, pg=pg, b_=b_, oc=oc: e.activation(out=gsb[:, b_, 0:n], in_=ps[pg][:, 0:n], func=AF.Sigmoid,
                                                                                  bias=ppt[:, PP[f'bmerge{l}'] + b_ * 8 + oc:PP[f'bmerge{l}'] + b_ * 8 + oc + 1], scale=1.0),
                                 reads=[('ps', pg), 'ppt'], writes=[('gsb', b_)])
                            pp_ = nextps()

                            def mmp(e, pp_=pp_, b_=b_, oc=oc):
                                last_ = None
                                for k in range(4):
                                    last_ = e.matmul(ps[pp_][:, 0:n], wb[:, b_, k, oc * 128:(oc + 1) * 128], y[:, b_, k, 0:n], start=(k == 0), stop=(k == 3))
                                return last_
                            T.op('pe', mmp, reads=[('wb', b_), (key, 'y', b_)], writes=[('ps', pp_)])
                            T.op('dve', lambda e, pp_=pp_, b_=b_: e.tensor_tensor(out=mt[:, b_, 0:n], in0=ps[pp_][:, 0:n], in1=gsb[:, b_, 0:n], op=ALU.mult),
                                 reads=[('ps', pp_), ('gsb', b_)], writes=[('mt', b_)])
                        T.op('pool', lambda e: e.tensor_tensor(out=mt[:, 0, 0:n], in0=mt[:, 0, 0:n], in1=mt[:, 1, 0:n], op=ALU.add),
                             reads=[('mt', 0), ('mt', 1)], writes=[('mt', 0)])
                        T.op('pool', lambda e, oc=oc: e.tensor_tensor(out=mix[:, oc, 0:n], in0=mt[:, 0, 0:n], in1=mt[:, 2, 0:n], op=ALU.add),
                             reads=[('mt', 0), ('mt', 2)], writes=[('mix', oc)])
                    for oc in range(8):
                        po = nextps()

                        def mmo(e, po=po, oc=oc):
                            last_ = None
                            for k in range(8):
                                last_ = e.matmul(ps[po][:, 0:n], wo[:, k, oc * 128:(oc + 1) * 128], mix[:, k, 0:n], start=(k == 0), stop=(k == 7))
                            return last_
                        T.op('pe', mmo, reads=['wo'] + [('mix', k) for k in range(8)], writes=[('ps', po)])
                        T.op('dve', lambda e, po=po, oc=oc: e.scalar_tensor_tensor(out=xt[:, oc, 0:n], in0=ps[po][:, 0:n], scalar=modt[:, l, 2, oc, isctx:isctx + 1],
                                                                                 in1=xt[:, oc, 0:n], op0=ALU.mult, op1=ALU.add),
                             reads=[('ps', po), (key, 'xt')], writes=[(key, 'xt')])
                    T.dma('sp', f'Mxs{sl}', out=S['xs'][:, t0:t0 + n].rearrange("(k p) t -> p k t", p=128), in_=xt[:, :, 0:n], reads=[(key, 'xt')], writes=[('xsd', t0)])
                    norm_block(xt, (key, 'xt'), sq, rs, tmpf, hf, 'hfm', n, l, 3, isctx)
                    T.dma('sp', 'Mhf', out=S['hf'][:, t0:t0 + n].rearrange("(k p) t -> p k t", p=128), in_=hf[:, :, 0:n], reads=[('hfm', 'hT', k) for k in range(8)])
            T.barrier()

        def phase_F(l):
            with contextlib.ExitStack() as ph:
                wi = sbuf(ph, "F_wi", [128, 8, 2 * HID], BF16)
                wo2 = sbuf(ph, "F_wo", [128, NHC, 1024], BF16)
                hfs = [sbuf(ph, f"F_hf{i}", [128, 8, 512], BF16) for i in range(2)]
                act = sbuf(ph, "F_act", [128, NHC, 512], BF16)
                xt = sbuf(ph, "F_xt", [128, 8, 512], F32)
                gs = sbuf(ph, "F_gs", [128, 2, 512], F32)
                for k in range(8):
                    T.dma('pool', f'wF{k}', out=wi[:, k, :], in_=wffi_in[l, k * 128:(k + 1) * 128, :], writes=[('wi', k)])
                for k in range(2):
                    T.dma('pool', f'wFo{k}', out=wo2[:, k * 11:(k + 1) * 11, :], in_=wffo_in[l, k * 1408:(k + 1) * 1408, :].rearrange("(k p) n -> p k n", p=128), writes=[('wo2', k)])
                wik = [('wi', k) for k in range(8)]
                blocks = [(t0, 512, 0) for t0 in range(0, TFULL[l], 512)]
                if l == 0:
                    blocks.append((TL, 256, 1))
                pr = [0]

                def nextps():
                    i = pr[0] % 6
                    pr[0] += 1
                    return i
                for bi, (t0, n, isctx) in enumerate(blocks):
                    sl = bi % 2
                    hf = hfs[sl]
                    T.dma('sp', f'Fhf{sl}', out=hf[:, :, 0:n], in_=S['hf'][:, t0:t0 + n].rearrange("(k p) t -> p k t", p=128), writes=[('hf', sl)])
                    T.dma('sp', 'Fx', out=xt[:, :, 0:n], in_=S['xs'][:, t0:t0 + n].rearrange("(k p) t -> p k t", p=128), reads=[('xsd', t0)], writes=['xt'])
                    for hc in range(NHC):
                        pg = nextps()
                        pu = nextps()

                        def mmg(e, pg=pg, pu=pu, hc=hc):
                            last_ = None
                            for k in range(8):
                                e.matmul(ps[pg][:, 0:n], wi[:, k, hc * 128:(hc + 1) * 128], hf[:, k, 0:n], start=(k == 0), stop=(k == 7))
                            for k in range(8):
                                last_ = e.matmul(ps[pu][:, 0:n], wi[:, k, HID + hc * 128:HID + (hc + 1) * 128], hf[:, k, 0:n], start=(k == 0), stop=(k == 7))
                            return last_
                        T.op('pe', mmg, reads=wik + [('hf', sl)], writes=[('ps', pg), ('ps', pu)])
                        a = hc % 2
                        T.op('act', lambda e, pg=pg, a=a: e.activation(out=gs[:, a, 0:n], in_=ps[pg][:, 0:n], func=AF.Silu), reads=[('ps', pg)], writes=[('gs', a)])
                        T.op('dve', lambda e, pu=pu, a=a, hc=hc: e.tensor_tensor(out=act[:, hc, 0:n], in0=ps[pu][:, 0:n], in1=gs[:, a, 0:n], op=ALU.mult),
                             reads=[('ps', pu), ('gs', a)], writes=[('act', hc)])
                    for oc in range(8):
                        po = nextps()

                        def mmo(e, po=po, oc=oc):
                            last_ = None
                            for hc in range(NHC):
                                last_ = e.matmul(ps[po][:, 0:n], wo2[:, hc, oc * 128:(oc + 1) * 128], act[:, hc, 0:n], start=(hc == 0), stop=(hc == NHC - 1))
                            return last_
                        T.op('pe', mmo, reads=[('wo2', 0), ('wo2', 1)] + [('act', hc) for hc in range(NHC)], writes=[('ps', po)])
                        T.op('dve', lambda e, po=po, oc=oc: e.scalar_tensor_tensor(out=xt[:, oc, 0:n], in0=ps[po][:, 0:n], scalar=modt[:, l, 5, oc, isctx:isctx + 1],
                                                                                 in1=xt[:, oc, 0:n], op0=ALU.mult, op1=ALU.add),
                             reads=[('ps', po), 'xt'], writes=['xt'])
                    if l == 0:
                        T.dma('sp', 'Fxs', out=S['xs'][:, t0:t0 + n].rearrange("(k p) t -> p k t", p=128), in_=xt[:, :, 0:n], reads=['xt'], writes=[('xsd', t0)])
                    else:
                        sqf = act[:, 0:8, :]
                        T.op('act', lambda e: e.activation(out=sqf[:, :, 0:n], in_=xt[:, :, 0:n], func=AF.Square), reads=['xt'] + [('act', hc) for hc in range(8)], writes=[('act', hc) for hc in range(8)])

                        def mms(e):
                            last_ = None
                            for k in range(8):
                                last_ = e.matmul(ps[6][:, 0:n], ones_bf[:], sqf[:, k, 0:n], start=(k == 0), stop=(k == 7))
                            return last_
                        T.op('pe', mms, reads=[('act', hc) for hc in range(8)] + ['ones_bf'], writes=[('ps', 6)])
                        T.op('dve', lambda e: e.tensor_scalar(out=gs[:, 0, 0:n], in0=ps[6][:, 0:n], scalar1=1.0 / D, scalar2=EPS, op0=ALU.mult, op1=ALU.add),
                             reads=[('ps', 6)], writes=[('gs', 0)])
                        T.op('act', lambda e: e.activation(out=gs[:, 0, 0:n], in_=gs[:, 0, 0:n], func=AF.Sqrt), reads=[('gs', 0)], writes=[('gs', 0)])
                        T.op('dve', lambda e: e.reciprocal(out=gs[:, 0, 0:n], in_=gs[:, 0, 0:n]), reads=[('gs', 0)], writes=[('gs', 0)])
                        for k in range(8):
                            T.op('dve', lambda e, k=k: e.scalar_tensor_tensor(out=xt[:, k, 0:n], in0=xt[:, k, 0:n], scalar=ppt[:, PP['fnorm'] + k:PP['fnorm'] + k + 1],
                                                                              in1=gs[:, 0, 0:n], op0=ALU.mult, op1=ALU.mult),
                                 reads=['xt', ('gs', 0), 'ppt'], writes=['xt'])
                        T.dma('sp', 'Fout', out=outT[:, t0:t0 + n].rearrange("(k p) t -> p k t", p=128), in_=xt[:, :, 0:n], reads=['xt'])
            T.barrier()

        phases = [('ada', phase_ada)]
        for l_ in range(2):
            phases += [(f'B{l_}', lambda l_=l_: phase_B(l_)), (f'G{l_}', lambda l_=l_: phase_G(l_)), (f'A{l_}', lambda l_=l_: phase_A(l_)),
                       (f'C{l_}', lambda l_=l_: phase_C(l_)), (f'M{l_}', lambda l_=l_: phase_M(l_)), (f'F{l_}', lambda l_=l_: phase_F(l_))]
        if only is not None:
            phases = [p for p in phases if p[0] in only]
        for name, fn in phases:
            fn()
            if stop_after == name:
                break
        T.final_wait('sp')
        print("instructions emitted:", T.ninst)
    return nc


IN_SIZES = (512, 128, 128, 256, 256, 512, 512, 32, 512, 512, 512)
IN_OFF = np.concatenate([[0], np.cumsum(IN_SIZES)])


def _local_to_global(half):
    tau = np.arange(TL)
    return tau if half == 0 else (SEQ - 1 - tau)


def _winx(w_in, half):
    o = IN_OFF
    aq = w_in[:, o[0]:o[1]]; ak = w_in[:, o[1]:o[2]]; av = w_in[:, o[2]:o[3]]
    bq = w_in[:, o[3]:o[4]]; bk = w_in[:, o[4]:o[5]]; bv = w_in[:, o[5]:o[6]]; br = w_in[:, o[6]:o[7]]
    bz = w_in[:, o[7]:o[8]]
    cq = w_in[:, o[8]:o[9]]; ck = w_in[:, o[9]:o[10]]; cv = w_in[:, o[10]:o[11]]
    out = np.zeros((D, NX), np.float32)
    sw = (np.arange(64) + 32) % 64
    heads = []
    for c in range(4):
        heads += [c, 4 + c]
    idx = np.concatenate([h * 64 + np.arange(64) for h in heads])
    idxs = np.concatenate([h * 64 + sw for h in heads])
    out[:, COL['Aq']:COL['Aq'] + 512] = aq[:, idx]
    out[:, COL['Aqs']:COL['Aqs'] + 512] = aq[:, idxs]
    out[:, COL['Ak']:COL['Ak'] + 128] = ak
    out[:, COL['Aks']:COL['Aks'] + 128] = ak[:, np.concatenate([sw, 64 + sw])]
    out[:, COL['Cq']:COL['Cq'] + 512] = cq
    out[:, COL['Ck']:COL['Ck'] + 512] = ck
    out[:, COL['Bq']:COL['Bq'] + 256] = bq
    out[:, COL['Bk']:COL['Bk'] + 256] = bk
    out[:, COL['Br']:COL['Br'] + 512] = br
    z1, z2 = (bz[:, 0:16], bz[:, 16:32]) if half == 0 else (bz[:, 16:32], bz[:, 0:16])
    out[:, COL['Bz']:COL['Bz'] + 16] = z1
    out[:, COL['Bz'] + 32:COL['Bz'] + 48] = z2
    out[:, COL['Av']:COL['Av'] + 128] = av
    out[:, COL['Bv']:COL['Bv'] + 512] = bv
    out[:, COL['Cv']:COL['Cv'] + 512] = cv
    return out


def _rope_tables(half):
    t = _local_to_global(half)
    row = (t // 64).astype(np.float32)
    col = (t % 64).astype(np.float32)
    inv = (np.float32(10000.0) ** (-np.arange(16, dtype=np.float32) / np.float32(16))).astype(np.float32)
    ang = np.concatenate([row[:, None] * inv[None], col[:, None] * inv[None]], axis=-1).astype(np.float32)
    cos = np.cos(ang).astype(np.float32).T
    sin = np.sin(ang).astype(np.float32).T
    cs = np.zeros((128, 2, TT), np.float32)
    cs[:, 0, TL:] = 1.0
    for rep in range(2):
        b = rep * 64
        cs[b:b + 32, 0, :TL] = cos; cs[b + 32:b + 64, 0, :TL] = cos
        cs[b:b + 32, 1, :TL] = -sin; cs[b + 32:b + 64, 1, :TL] = sin
    return cs


def _ctab(rpb, half):
    tab = np.full((128, NCTAB, 8, 128), -30000.0, np.float32)
    pairs = [(0, 0), (0, 1), (0, 2), (0, 3), (1, 0), (1, 1), (1, 2), (1, 3), (4, 2), (4, 3), (4, 4), (4, 5), (4, 6)]
    loc = np.arange(128)
    for ti, (qn, kn) in enumerate(pairs):
        tq = qn * 128 + loc
        tk = kn * 128 + loc
        if half == 1:
            tq = SEQ - 1 - tq
            tk = SEQ - 1 - tk
        qr, qc = tq // 64, tq % 64
        kr, kc = tk // 64, tk % 64
        rs = np.clip(qr - 4, 0, 64 - 8)
        ws = np.clip(qc - 8, 0, 64 - 16)
        valid = ((kr[:, None] >= rs[None]) & (kr[:, None] < rs[None] + 8) &
                 (kc[:, None] >= ws[None]) & (kc[:, None] < ws[None] + 16))
        dr = np.clip(kr[:, None] - qr[None] + 7, 0, 14)
        dc = np.clip(kc[:, None] - qc[None], -15, 15) + 15
        vals = rpb[:, dr, dc]
        tab[:, ti] = np.where(valid[None], vals, np.float32(-30000.0)).transpose(1, 0, 2)[:, [0, 2, 1, 3, 4, 6, 5, 7], :]
    return tab.reshape(128, NCTAB * 8 * 128)


def _dup2(v):
    a = v.reshape(-1, 128).T
    return np.repeat(a[:, :, None], 2, axis=2).reshape(128, -1)


def prep_core(inputs, core):
    b, half = core // 2, core % 2
    x = inputs['x'][b]
    t = np.arange(TL) if half == 0 else (SEQ - 1 - np.arange(TL))
    m = {}
    m['xT'] = np.ascontiguousarray(x[t].T)
    ctx = inputs['ctx'][b]
    if half == 1:
        ctx = ctx[::-1]
    m['ctxT'] = np.ascontiguousarray(ctx.T)
    pp = np.zeros((128, NPP), np.float32)
    cc = np.stack([inputs['c'][b].reshape(8, 128).T, inputs['c_ctx'].reshape(8, 128).T], axis=2)
    pp[:, PP['c']:PP['c'] + 16] = cc.reshape(128, 16)
    for l in range(2):
        pp[:, PP[f'bada{l}']:PP[f'bada{l}'] + 96] = _dup2(inputs['b_ada'][l])
        pp[:, PP[f'nmix{l}']:PP[f'nmix{l}'] + 16] = _dup2(inputs['norm_mix'][l])
        pp[:, PP[f'nffn{l}']:PP[f'nffn{l}'] + 16] = _dup2(inputs['norm_ffn'][l])
        pp[:, PP[f'bmerge{l}']:PP[f'bmerge{l}'] + 24] = inputs['b_merge'][l].reshape(24, 128).T
        pp[:, PP[f'gnorm{l}']] = inputs['gla_norm'][l]
        pp[:, PP[f'sink{l}']:PP[f'sink{l}'] + 8] = inputs['attn_sink'][l][None, :]
    pp[:, PP['fnorm']:PP['fnorm'] + 8] = inputs['final_norm'].reshape(8, 128).T
    m['pp'] = pp
    m['w_ada'] = inputs['w_ada']
    m['winx'] = np.stack([_winx(inputs['w_in'][l], half) for l in range(2)])
    gw = np.zeros((64, 2, 256), np.float32)
    d1w, d2w = ('gla_gate_w_fwd', 'gla_gate_w_bwd') if half == 0 else ('gla_gate_w_bwd', 'gla_gate_w_fwd')
    d1b, d2b = ('gla_gate_b_fwd', 'gla_gate_b_bwd') if half == 0 else ('gla_gate_b_bwd', 'gla_gate_b_fwd')
    for l in range(2):
        gw[0:16, l] = inputs[d1w][l]; gw[32:48, l] = inputs[d2w][l]
        gw[16, l] = inputs[d1b][l]; gw[48, l] = inputs[d2b][l]
    m['gwp'] = gw
    m['ones_row'] = np.ones((1, TT), np.float32)
    m['cossin'] = _rope_tables(half)
    s_, t_ = np.meshgrid(np.arange(128), np.arange(128), indexing='ij')
    m['cst'] = np.stack([(s_ <= t_), (s_ >= t_), (s_ == t_)], axis=1).astype(np.float32)
    m['ctab'] = np.stack([_ctab(inputs['na_rpb'][l], half) for l in range(2)])
    for k in ('w_branch_a', 'w_branch_b', 'w_branch_c', 'w_merge', 'w_out', 'w_ffn_in', 'w_ffn_out'):
        m[k] = inputs[k]
    return m


_NC_CACHE = {}


def kernel(**inputs):
    inputs = {k: np.asarray(v) for k, v in inputs.items()}
    if 'nc' not in _NC_CACHE:
        _NC_CACHE['nc'] = build()
    nc = _NC_CACHE['nc']
    in_maps = [prep_core(inputs, c) for c in range(8)]
    res = run_bass_kernel_spmd(nc, in_maps, core_ids=list(range(8)))
    out = np.zeros((4, SEQ, D), np.float32)
    for c in range(8):
        b, half = c // 2, c % 2
        o = res.results[c]["outT"].T
        if half == 0:
            out[b, 0:2048] = o
        else:
            out[b, 2048:] = o[::-1]
    return out
```

```python
import contextlib
import numpy as np
import concourse.bass as bass
import concourse.mybir as mybir
from concourse.bass_utils import run_bass_kernel_spmd

F32 = mybir.dt.float32
BF16 = mybir.dt.bfloat16
AF = mybir.ActivationFunctionType
ALU = mybir.AluOpType

D = 1024
KC = 8
SEQ = 4096
TL = 3072
CT = 256
TT = TL + CT
TO = 2560 + CT
EPS = 1e-6
HID = 2816
NHC = 22

FM_GROUPS = [('Aq', 512), ('Aqs', 512), ('Ak', 128), ('Aks', 128), ('Cq', 512), ('Ck', 512),
             ('Bq', 256), ('Bk', 256), ('Br', 512), ('Bz', 64)]
TM_GROUPS = [('Av', 128), ('Bv', 512), ('Cv', 512)]
COL = {}
_o = 0
for _n, _w in FM_GROUPS + TM_GROUPS:
    COL[_n] = _o
    _o += _w
NX = _o

PP = {}
_o = 0
def _pp(name, n):
    global _o
    PP[name] = _o
    _o += n
_pp('c', 16)
for _l in range(2):
    _pp(f'bada{_l}', 96); _pp(f'nmix{_l}', 16); _pp(f'nffn{_l}', 16); _pp(f'bmerge{_l}', 24)
    _pp(f'gnorm{_l}', 1); _pp(f'sink{_l}', 8)
_pp('fnorm', 8)
NPP = _o

BLOCKS = [
    [(0, 512, 'F', 0), (512, 512, 'F', 0), (1024, 512, 'F', 0), (1536, 512, 'F', 0), (2048, 512, 'F', 0),
     (2560, 512, 'KV', 0), (TL, 256, 'F', 1)],
    [(0, 512, 'F', 0), (512, 512, 'F', 0), (1024, 512, 'F', 0), (1536, 512, 'F', 0),
     (2048, 512, 'KV', 0), (TL, 256, 'KV', 1)],
]
TFULL = [2560, 2048]
TGLA = [3072, 2560]
NCTAB = 13


class Trk:
    ENG = ('pe', 'act', 'dve', 'pool', 'sp')

    def __init__(s, nc, es):
        s.nc, s.es = nc, es
        s.E = dict(pe=nc.tensor, act=nc.scalar, dve=nc.vector, pool=nc.gpsimd, sp=nc.sync)
        s.sem = {}
        s.cnt = {}
        s.epoch = 0
        s.seen = {e: {} for e in s.ENG}
        s.lastw = {}
        s.rd = {}
        s._new_compute_sems()
        s.ninst = 0
        s.dmap = {}
        s.dfree = []
        s.nd = 0

    def _new_compute_sems(s):
        for e in ('pe', 'act', 'dve', 'pool'):
            s.sem[e] = s.es.enter_context(s.nc.semaphore(f"c{s.epoch}_{e}"))
            s.cnt[e] = 0
            for w in s.ENG:
                s.seen[w].pop(e, None)

    def dsem(s, name):
        if name in s.dmap:
            return s.dmap[name]
        if s.dfree:
            key = s.dfree.pop()
        else:
            key = 'd_%d' % s.nd
            s.nd += 1
            s.sem[key] = s.es.enter_context(s.nc.semaphore(key))
            s.cnt[key] = 0
        s.dmap[name] = key
        return key

    def _wait(s, e, tok):
        k, v = tok
        if k == e and e == 'pe':
            return
        if s.seen[e].get(k, 0) >= v:
            return
        s.E[e].wait_ge(s.sem[k], v)
        s.seen[e][k] = v
        s.ninst += 1

    def _deps(s, e, reads, writes):
        for k in reads:
            if k in s.lastw:
                s._wait(e, s.lastw[k])
        for k in writes:
            for sk, v in s.rd.get(k, {}).items():
                if sk != e:
                    s._wait(e, (sk, v))
            if k in s.lastw and s.lastw[k][0] != e:
                s._wait(e, s.lastw[k])

    def _post(s, tok, reads, writes):
        for k in reads:
            d = s.rd.setdefault(k, {})
            d[tok[0]] = max(d.get(tok[0], 0), tok[1])
        for k in writes:
            s.lastw[k] = tok
            s.rd[k] = {}

    def op(s, e, fn, reads=(), writes=()):
        s._deps(e, reads, writes)
        inst = fn(s.E[e])
        s.cnt[e] += 1
        inst.then_inc(s.sem[e], 1)
        s._post((e, s.cnt[e]), reads, writes)
        s.ninst += 1

    def dma(s, q, semname, out, in_, reads=(), writes=(), throttle=False):
        key = s.dsem(semname)
        if throttle and s.cnt[key] > 0:
            s._wait(q, (key, s.cnt[key]))
        s._deps(q, reads, writes)
        inst = s.E[q].dma_start(out=out, in_=in_)
        s.cnt[key] += 16
        inst.then_inc(s.sem[key], 16)
        s._post((key, s.cnt[key]), reads, writes)
        s.ninst += 1

    def barrier(s):
        for e in s.ENG:
            for k in list(s.sem):
                if k == e:
                    continue
                if s.cnt[k] > 0:
                    s._wait(e, (k, s.cnt[k]))
        s.lastw.clear()
        s.rd.clear()
        s.dfree += sorted(s.dmap.values())
        s.dmap = {}
        s.epoch += 1
        s._new_compute_sems()

    def final_wait(s, e='sp'):
        for k in list(s.sem):
            if k != e and s.cnt[k] > 0:
                s._wait(e, (k, s.cnt[k]))


def build(stop_after=None, dbg=False, only=None):
    nc = bass.Bass("TRN2", target_bir_lowering=False)
    okind = "ExternalOutput" if dbg else "Internal"

    def din(name, shape, dt=F32):
        return nc.dram_tensor(name, list(shape), dt, kind="ExternalInput").ap()

    def dscr(name, shape, dt=BF16):
        return nc.dram_tensor(name, list(shape), dt, kind=okind).ap()

    xT_in = din("xT", [D, TL])
    ctxT_in = din("ctxT", [D, CT])
    pp_in = din("pp", [128, NPP])
    wada_in = din("w_ada", [2, D, 6 * D])
    winx_in = din("winx", [2, D, NX])
    gwp_in = din("gwp", [64, 2, 256])
    onesrow_in = din("ones_row", [1, TT])
    cs_in = din("cossin", [128, 2, TT])
    cst_in = din("cst", [128, 3, 128])
    ctab_in = din("ctab", [2, 128, NCTAB * 8 * 128])
    wba_in = din("w_branch_a", [2, 512, D])
    wbb_in = din("w_branch_b", [2, 512, D])
    wbc_in = din("w_branch_c", [2, 512, D])
    wmerge_in = din("w_merge", [2, D, 3 * D])
    wout_in = din("w_out", [2, D, D])
    wffi_in = din("w_ffn_in", [2, D, 2 * HID])
    wffo_in = din("w_ffn_out", [2, HID, D])
    outT = nc.dram_tensor("outT", [D, 2048], F32, kind="ExternalOutput").ap()

    S = {}
    S['hT'] = dscr("s_hT", [D, TT])
    S['Aq'] = dscr("s_Aq", [512, TT]); S['Ak'] = dscr("s_Ak", [128, TT])
    S['Cq'] = dscr("s_Cq", [512, TT]); S['Ck'] = dscr("s_Ck", [512, TT])
    S['Bq'] = dscr("s_Bq", [256, TT]); S['Bk'] = dscr("s_Bk", [256, TT]); S['Br'] = dscr("s_Br", [512, TT])
    S['Bz'] = dscr("s_Bz", [64, TT], F32)
    S['Av'] = dscr("s_Av", [TT, 256]); S['Bv'] = dscr("s_Bv", [TT, 512]); S['Cv'] = dscr("s_Cv", [TT, 1024])
    S['ya'] = dscr("s_ya", [512, TT]); S['yb'] = dscr("s_yb", [512, TT]); S['yc'] = dscr("s_yc", [512, TT])
    S['xs'] = dscr("s_xs", [D, TT], F32)
    S['hf'] = dscr("s_hf", [D, TT])
    if dbg:
        S['mod'] = dscr("s_mod", [128, 2 * 6 * 8 * 2], F32)
        S['oT'] = dscr("s_oT", [512, TO], F32)

    with contextlib.ExitStack() as es:
        T = Trk(nc, es)

        ucnt = [0]

        def sbuf(st, name, shape, dt):
            ucnt[0] += 1
            return st.enter_context(nc.sbuf_tensor(f"{name}_{ucnt[0]}", list(shape), dt))

        ps = [es.enter_context(nc.psum_tensor(f"ps{i}", [128, 512], F32)) for i in range(7)]
        psb = es.enter_context(nc.psum_tensor("psb", [128, 1024], BF16))
        ppt = sbuf(es, "ppt", [128, NPP], F32)
        modt = sbuf(es, "modt", [128, 2, 6, 8, 2], F32)
        cst = sbuf(es, "cstf", [128, 3, 128], F32)
        cstb = sbuf(es, "cstb", [128, 3, 128], BF16)
        ones_bf = sbuf(es, "ones_bf", [128, 128], BF16)
        ones_f = sbuf(es, "ones_f", [128, 128], F32)

        T.dma('sp', 'g0', out=ppt[:], in_=pp_in, writes=['ppt'])
        T.dma('sp', 'g1', out=cst[:], in_=cst_in, writes=['cst'])
        T.op('dve', lambda e: e.tensor_copy(out=cstb[:], in_=cst[:]), reads=['cst'], writes=['cstb'])
        T.op('pool', lambda e: e.memset(ones_bf[:], 1.0), writes=['ones_bf'])
        T.op('pool', lambda e: e.memset(ones_f[:], 1.0), writes=['ones_f'])

        def ppv(name, n):
            return ppt[:, PP[name]:PP[name] + n]

        sc = sbuf(es, "ada_sc", [128, 16], F32)
        adatmp = sbuf(es, "ada_tmp", [128, 16], F32)
        T.op('act', lambda e: e.activation(out=sc[:], in_=ppv('c', 16), func=AF.Silu), reads=['ppt'], writes=['sc'])
        scv = sc[:].rearrange("p (k i) -> p k i", i=2)
        adacnt = [0]

        def ada_load(l, j, wa):
            sl = adacnt[0] % len(wa)
            adacnt[0] += 1
            T.dma('sp', f'adaw{sl}', out=wa[sl][:],
                  in_=wada_in[l, :, j * 1024:(j + 1) * 1024].rearrange("(k p) n -> p k n", p=128), writes=[('wa', sl)])
            return sl

        def ada_piece(l, j, wa, pbank, sl=None):
            if sl is None:
                sl = ada_load(l, j, wa)

            def mm(e):
                last = None
                for oc in range(8):
                    for k in range(8):
                        last = e.matmul(ps[pbank][:, oc * 2:oc * 2 + 2], wa[sl][:, k, oc * 128:(oc + 1) * 128], scv[:, k, :],
                                        start=(k == 0), stop=(k == 7))
                return last
            T.op('pe', mm, reads=[('wa', sl), 'sc'], writes=[('ps', pbank)])
            b0 = PP[f'bada{l}'] + j * 16
            T.op('dve', lambda e: e.tensor_tensor(out=adatmp[:], in0=ps[pbank][:, 0:16], in1=ppt[:, b0:b0 + 16], op=ALU.add),
                 reads=[('ps', pbank), 'ppt'], writes=['adatmp'])
            dst = {0: 1, 1: 0, 2: 2, 3: 4, 4: 3, 5: 5}[j]
            tv = adatmp[:].rearrange("p (k i) -> p k i", i=2)
            if j in (1, 4):
                nname = f'nmix{l}' if j == 1 else f'nffn{l}'
                T.op('dve', lambda e: e.tensor_scalar(out=adatmp[:], in0=adatmp[:], scalar1=1.0, scalar2=None, op0=ALU.add), reads=['adatmp'], writes=['adatmp'])
                T.op('dve', lambda e: e.tensor_tensor(out=modt[:, l, dst], in0=tv, in1=ppv(nname, 16).rearrange("p (k i) -> p k i", i=2), op=ALU.mult),
                     reads=['adatmp', 'ppt'], writes=[('modt', l, dst)])
            else:
                T.op('dve', lambda e: e.tensor_copy(out=modt[:, l, dst], in_=tv), reads=['adatmp'], writes=[('modt', l, dst)])

        def phase_ada():
            with contextlib.ExitStack() as ph:
                wa = [sbuf(ph, f"ada_w{i}", [128, 8, 1024], F32) for i in range(2)]
                ada_piece(0, 0, wa, 0)
                ada_piece(0, 1, wa, 1)
            T.barrier()

        def norm_block(xt, xkey, sq, rs, tmpf, hT, hkp, n, l, which, isctx):
            T.op('act', lambda e: e.activation(out=sq[:, :, 0:n], in_=xt[:, :, 0:n], func=AF.Square),
                 reads=[xkey], writes=['nb_sq'])

            def mm(e):
                last = None
                for k in range(8):
                    last = e.matmul(ps[6][:, 0:n], ones_bf[:], sq[:, k, 0:n], start=(k == 0), stop=(k == 7))
                return last
            T.op('pe', mm, reads=['nb_sq', 'ones_bf'], writes=[('ps', 6)])
            T.op('dve', lambda e: e.tensor_scalar(out=rs[:, 0:n], in0=ps[6][:, 0:n], scalar1=1.0 / D, scalar2=EPS, op0=ALU.mult, op1=ALU.add),
                 reads=[('ps', 6)], writes=['nb_rs'])
            T.op('act', lambda e: e.activation(out=rs[:, 0:n], in_=rs[:, 0:n], func=AF.Sqrt), reads=['nb_rs'], writes=['nb_rs'])
            T.op('dve', lambda e: e.reciprocal(out=rs[:, 0:n], in_=rs[:, 0:n]), reads=['nb_rs'], writes=['nb_rs'])
            for k in range(8):
                T.op('dve', lambda e, k=k: e.scalar_tensor_tensor(out=tmpf[:, k % 2, 0:n], in0=xt[:, k, 0:n], scalar=modt[:, l, which, k, isctx:isctx + 1],
                                                                  in1=rs[:, 0:n], op0=ALU.mult, op1=ALU.mult),
                     reads=[xkey, 'nb_rs'], writes=[('nb_tmpf', k % 2)])
                T.op('act', lambda e, k=k: e.activation(out=hT[:, k, 0:n], in_=tmpf[:, k % 2, 0:n], func=AF.Identity,
                                                        bias=modt[:, l, which + 1, k, isctx:isctx + 1], scale=1.0),
                     reads=[('nb_tmpf', k % 2)], writes=[(hkp, 'hT', k)])

        def phase_B(l):
            with contextlib.ExitStack() as ph:
                w = sbuf(ph, "B_w", [128, 8, NX], BF16)
                xts = [sbuf(ph, f"B_xt{i}", [128, 8, 512], F32) for i in range(2)]
                hTs = [sbuf(ph, f"B_hT{i}", [128, 8, 512], BF16) for i in range(2)]
                sq = sbuf(ph, "B_sq", [128, 8, 512], BF16)
                rs = sbuf(ph, "B_rs", [128, 512], F32)
                tmpf = sbuf(ph, "B_tmpf", [128, 2, 512], F32)
                cs = sbuf(ph, "B_cs", [128, 2, 512], F32)
                r1 = sbuf(ph, "B_r1", [128, 2, 512], F32)
                r2 = sbuf(ph, "B_r2", [128, 2, 512], F32)
                fmA = sbuf(ph, "B_fmA", [128, 5, 512], BF16)
                fmC = sbuf(ph, "B_fmC", [128, 8, 512], BF16)
                fmB = sbuf(ph, "B_fmB", [128, 8, 512], BF16)
                fmZ = sbuf(ph, "B_fmZ", [64, 512], F32)
                tmA = sbuf(ph, "B_tmA", [128, 4, 2, 128], BF16)
                tmB = sbuf(ph, "B_tmB", [128, 4, 512], BF16)
                tmC = sbuf(ph, "B_tmC", [128, 4, 8, 128], BF16)
                worder = ([('Aq', c) for c in range(4)] + [('Aqs', c) for c in range(4)] + [('Ak', 0), ('Aks', 0)] + [('Cq', c) for c in range(4)]
                          + [('Ck', c) for c in range(4)] + [('Bq', 0), ('Bq', 1)] + [('Br', c) for c in range(4)] + [('Bk', 0), ('Bk', 1), ('Bz', 0)])
                worder = ([x for c in range(4) for x in (('Aq', c), ('Aqs', c))] + worder[8:])
                for wi_, (grp, c) in enumerate(worder):
                    wd = 64 if grp == 'Bz' else 128
                    c0 = COL[grp] + c * 128
                    T.dma('pool', f'wB{wi_ % 12}', out=w[:, :, c0:c0 + wd], in_=winx_in[l, :, c0:c0 + wd].rearrange("(k p) n -> p k n", p=128), writes=[('w', grp, c)])
                for wi_, (grp, wd) in enumerate(TM_GROUPS):
                    c0 = COL[grp]
                    T.dma('pool', f'wBt{wi_}', out=w[:, :, c0:c0 + wd], in_=winx_in[l, :, c0:c0 + wd].rearrange("(k p) n -> p k n", p=128), writes=[('w', grp, 0)])
                T.op('pool', lambda e: e.memset(tmA[:], 1.0), writes=[('tmA', j) for j in range(4)])
                T.op('pool', lambda e: e.memset(tmC[:], 1.0), writes=[('tmC', j) for j in range(4)])
                psrot = [0]

                def nextps():
                    i = psrot[0] % 6
                    psrot[0] += 1
                    return i

                for bi, (t0, n, kind, isctx) in enumerate(BLOCKS[l]):
                    sl = bi % 2
                    xt, hT = xts[sl], hTs[sl]
                    key = ('blk', sl)
                    if l == 0:
                        src = (ctxT_in if isctx else xT_in[:, t0:t0 + n])
                    else:
                        src = S['xs'][:, t0:t0 + n]
                    T.dma('sp', f'Bx{sl}', out=xt[:, :, 0:n], in_=src.rearrange("(k p) t -> p k t", p=128), writes=[(key, 'xt')])
                    T.dma('sp', 'Bcs', out=cs[:, :, 0:n], in_=cs_in[:, :, t0:t0 + n], writes=['cs'])
                    norm_block(xt, (key, 'xt'), sq, rs, tmpf, hT, key, n, l, 0, isctx)
                    hkeys = [(key, 'hT', k) for k in range(8)]
                    if kind == 'F':
                        T.dma('sp', f'BhT{sl}', out=S['hT'][:, t0:t0 + n].rearrange("(k p) t -> p k t", p=128), in_=hT[:, :, 0:n], reads=hkeys)

                    def proj_fm(grp, c, width=128):
                        pi = nextps()
                        c0 = COL[grp] + c * 128

                        def mm(e):
                            last = None
                            for k in range(8):
                                last = e.matmul(ps[pi][0:width, 0:n], w[:, k, c0:c0 + width], hT[:, k, 0:n], start=(k == 0), stop=(k == 7))
                            return last
                        T.op('pe', mm, reads=[('w', grp, c)] + hkeys, writes=[('ps', pi)])
                        return pi

                    def rope_chunk(grp, grps, c, dst, dkey):
                        p1 = proj_fm(grp, c)
                        p2 = proj_fm(grps, c)
                        rr = (c % 2)
                        T.op('dve', lambda e: e.tensor_tensor(out=r1[:, rr, 0:n], in0=ps[p1][:, 0:n], in1=cs[:, 0, 0:n], op=ALU.mult),
                             reads=[('ps', p1), 'cs'], writes=[('r1', rr)])
                        T.op('dve', lambda e: e.tensor_tensor(out=r2[:, rr, 0:n], in0=ps[p2][:, 0:n], in1=cs[:, 1, 0:n], op=ALU.mult),
                             reads=[('ps', p2), 'cs'], writes=[('r2', rr)])
                        T.op('pool', lambda e: e.tensor_tensor(out=dst, in0=r1[:, rr, 0:n], in1=r2[:, rr, 0:n], op=ALU.add),
                             reads=[('r1', rr), ('r2', rr)], writes=[dkey])

                    evr = [0]

                    def evac(pi, dst, dkey, scale=None, func=None, width=128):
                        src_ = ps[pi][0:width, 0:n]
                        if func is not None:
                            T.op('act', lambda e: e.activation(out=dst, in_=src_, func=func), reads=[('ps', pi)], writes=[dkey])
                        elif scale is not None:
                            T.op('act', lambda e: e.activation(out=dst, in_=src_, func=AF.Copy, scale=scale), reads=[('ps', pi)], writes=[dkey])
                        else:
                            evr[0] += 1
                            if evr[0] % 2:
                                T.op('act', lambda e: e.activation(out=dst, in_=src_, func=AF.Copy), reads=[('ps', pi)], writes=[dkey])
                            else:
                                T.op('dve', lambda e: e.tensor_copy(out=dst, in_=src_), reads=[('ps', pi)], writes=[dkey])

                    def store_fm(name, tile_ap, nchunks, keys, rows=128):
                        T.dma('sp', f'st_{name}', out=S[name][:, t0:t0 + n].rearrange("(c p) t -> p c t", p=rows) if nchunks > 1 else S[name][:, t0:t0 + n],
                              in_=tile_ap, reads=keys)

                    if kind == 'F':
                        for c in range(4):
                            rope_chunk('Aq', 'Aqs', c, fmA[:, c, 0:n], ('fmA', c))
                        store_fm('Aq', fmA[:, 0:4, 0:n], 4, [('fmA', c) for c in range(4)])
                    rope_chunk('Ak', 'Aks', 0, fmA[:, 4, 0:n], ('fmA', 4))
                    store_fm('Ak', fmA[:, 4, 0:n], 1, [('fmA', 4)])
                    if kind == 'F':
                        for c in range(4):
                            evac(proj_fm('Cq', c), fmC[:, c, 0:n], ('fmC', c))
                        store_fm('Cq', fmC[:, 0:4, 0:n], 4, [('fmC', c) for c in range(4)])
                    for c in range(4):
                        evac(proj_fm('Ck', c), fmC[:, 4 + c, 0:n], ('fmC', 4 + c))
                    store_fm('Ck', fmC[:, 4:8, 0:n], 4, [('fmC', 4 + c) for c in range(4)])
                    if kind == 'F':
                        for c in range(2):
                            evac(proj_fm('Bq', c), fmB[:, c, 0:n], ('fmB', c), scale=0.125)
                        store_fm('Bq', fmB[:, 0:2, 0:n], 2, [('fmB', c) for c in range(2)])
                        for c in range(4):
                            evac(proj_fm('Br', c), fmB[:, 4 + c, 0:n], ('fmB', 4 + c), func=AF.Silu)
                        store_fm('Br', fmB[:, 4:8, 0:n], 4, [('fmB', 4 + c) for c in range(4)])
                    for c in range(2):
                        evac(proj_fm('Bk', c), fmB[:, 2 + c, 0:n], ('fmB', 2 + c))
                    store_fm('Bk', fmB[:, 2:4, 0:n], 2, [('fmB', 2 + c) for c in range(2)])
                    pz = proj_fm('Bz', 0, width=64)
                    T.op('dve', lambda e: e.tensor_copy(out=fmZ[:, 0:n], in_=ps[pz][0:64, 0:n]), reads=[('ps', pz)], writes=['fmZ'])
                    T.dma('sp', 'st_Bz', out=S['Bz'][:, t0:t0 + n], in_=fmZ[:, 0:n], reads=['fmZ'])
                    nj = n // 128
                    for j in range(nj):
                        def proj_tm(grp, ncols, j=j):
                            pi = nextps()
                            c0 = COL[grp]

                            def mm(e):
                                last = None
                                for k in range(8):
                                    last = e.matmul(ps[pi][:, 0:ncols], hT[:, k, j * 128:(j + 1) * 128], w[:, k, c0:c0 + ncols], start=(k == 0), stop=(k == 7))
                                return last
                            T.op('pe', mm, reads=[('w', grp, 0)] + hkeys, writes=[('ps', pi)])
                            return pi
                        pa = proj_tm('Av', 128)
                        T.op('dve', lambda e, pa=pa, j=j: e.tensor_copy(out=tmA[:, j, :, 0:64], in_=ps[pa][:, 0:128].rearrange("p (g d) -> p g d", g=2)),
                             reads=[('ps', pa)], writes=[('tmA', j)])
                        pb = proj_tm('Bv', 512)
                        T.op('act', lambda e, pb=pb, j=j: e.activation(out=tmB[:, j, :], in_=ps[pb][:, 0:512], func=AF.Copy),
                             reads=[('ps', pb)], writes=[('tmB', j)])
                        pc = proj_tm('Cv', 512)
                        T.op('dve', lambda e, pc=pc, j=j: e.tensor_copy(out=tmC[:, j, :, 0:64], in_=ps[pc][:, 0:512].rearrange("p (g d) -> p g d", g=8)),
                             reads=[('ps', pc)], writes=[('tmC', j)])
                    T.dma('sp', 'st_Av', out=S['Av'][t0:t0 + n, :].rearrange("(j p) c -> p j c", p=128),
                          in_=tmA[:, 0:nj].rearrange("p j g d -> p j (g d)"), reads=[('tmA', j) for j in range(nj)])
                    T.dma('sp', 'st_Bv', out=S['Bv'][t0:t0 + n, :].rearrange("(j p) c -> p j c", p=128),
                          in_=tmB[:, 0:nj, :], reads=[('tmB', j) for j in range(nj)])
                    T.dma('sp', 'st_Cv', out=S['Cv'][t0:t0 + n, :].rearrange("(j p) c -> p j c", p=128),
                          in_=tmC[:, 0:nj].rearrange("p j g d -> p j (g d)"), reads=[('tmC', j) for j in range(nj)])
            T.barrier()


        def phase_G(l):
            with contextlib.ExitStack() as ph:
                qT = sbuf(ph, "G_q", [128, 2, TT], BF16)
                kT = sbuf(ph, "G_k", [128, 2, TT], BF16)
                v = sbuf(ph, "G_v", [128, TT // 128, 512], BF16)
                zT = sbuf(ph, "G_z", [64, TT], F32)
                oT = sbuf(ph, "G_o", [128, 4, TO], F32)
                gw = sbuf(ph, "G_gw", [64, 256], F32)
                mrep = sbuf(ph, "G_mrep", [128, 2, 4, 128], F32)
                NB = 3
                Sst = [sbuf(ph, f"G_S{i}", [128, 2, 128], F32) for i in range(2)]
                Sbf = [sbuf(ph, f"G_Sbf{i}", [128, 2, 128], BF16) for i in range(2)]
                e1 = [sbuf(ph, f"G_e1{i}", [128, 256], F32) for i in range(NB)]
                gpos = [sbuf(ph, f"G_gp{i}", [128, 256], F32) for i in range(NB)]
                eb = [sbuf(ph, f"G_eb{i}", [128, 2, 128], F32) for i in range(NB)]
                enb = [sbuf(ph, f"G_enb{i}", [128, 2, 128], F32) for i in range(NB)]
                qt = [sbuf(ph, f"G_qt{i}", [128, 2, 128], BF16) for i in range(NB)]
                kt = [sbuf(ph, f"G_kt{i}", [128, 2, 128], BF16) for i in range(NB)]
                ktl = [sbuf(ph, f"G_ktl{i}", [128, 2, 128], BF16) for i in range(NB)]
                ktT = [sbuf(ph, f"G_ktT{i}", [128, 2, 128], BF16) for i in range(NB)]
                Am = [sbuf(ph, f"G_Am{i}", [128, 2, 2, 128], BF16) for i in range(NB)]
                T.dma('sp', 'Gq', out=qT[:], in_=S['Bq'].rearrange("(c p) t -> p c t", p=128), writes=['qT'])
                T.dma('sp', 'Gk', out=kT[:], in_=S['Bk'].rearrange("(c p) t -> p c t", p=128), writes=['kT'])
                T.dma('sp', 'Gv', out=v[:], in_=S['Bv'].rearrange("(j p) c -> p j c", p=128), writes=['v'])
                T.dma('sp', 'Gz', out=zT[:], in_=S['Bz'], writes=['zT'])
                T.dma('sp', 'Ggw', out=gw[:], in_=gwp_in[:, l, :], writes=['gw'])
                T.dma('sp', 'Gz', out=zT[16:17, :], in_=onesrow_in, writes=['zT'])
                T.dma('sp', 'Gz', out=zT[48:49, :], in_=onesrow_in, writes=['zT'])
                for d_ in range(2):
                    for h in range(4):
                        T.op('pool', lambda e, d_=d_, h=h: e.tensor_copy(out=mrep[:, d_, h, :], in_=cst[:, d_, :]), reads=['cst'], writes=['mrep'])

                def prep(idx, item):
                    if item[0] != 'chunk':
                        return
                    _, tok0, d_, need_out, ocol = item
                    cb = idx % NB
                    pz = idx % 2
                    rows = slice(0, 17) if d_ == 0 else slice(32, 49)
                    last = 127 if d_ == 0 else 0
                    bz = ps[pz]
                    kz, kcs = ('psz', pz), ('pscs', pz)
                    T.op('pe', lambda e: e.matmul(bz[:, 0:256], zT[rows, tok0:tok0 + 128], gw[rows, :], start=True, stop=True), reads=['zT', 'gw'], writes=[kz])
                    T.op('act', lambda e: e.activation(out=e1[cb][:], in_=bz[:, 0:256], func=AF.Exp, scale=-1.0), reads=[kz], writes=[('e1', cb)])
                    T.op('act', lambda e: e.activation(out=gpos[cb][:], in_=e1[cb][:], func=AF.Ln, bias=1.0, scale=1.0), reads=[('e1', cb)], writes=[('gpos', cb)])

                    def mmcs(e):
                        e.matmul(bz[:, 256:384], gpos[cb][:, 0:128], cst[:, d_, :], start=True, stop=True)
                        return e.matmul(bz[:, 384:512], gpos[cb][:, 128:256], cst[:, d_, :], start=True, stop=True)
                    T.op('pe', mmcs, reads=[('gpos', cb), 'cst'], writes=[kcs])
                    csv = bz[:, 256:512].rearrange("p (a t) -> p a t", a=2)
                    T.op('act', lambda e: e.activation(out=eb[cb][:], in_=csv, func=AF.Exp, scale=-1.0 / 16), reads=[kcs], writes=[('eb', cb)])
                    T.op('act', lambda e: e.activation(out=enb[cb][:], in_=csv, func=AF.Exp, scale=1.0 / 16), reads=[kcs], writes=[('enb', cb)])
                    if need_out:
                        T.op('dve', lambda e: e.tensor_tensor(out=qt[cb][:], in0=qT[:, :, tok0:tok0 + 128], in1=eb[cb][:], op=ALU.mult),
                             reads=['qT', ('eb', cb)], writes=[('qt', cb)])
                    T.op('dve', lambda e: e.tensor_tensor(out=kt[cb][:], in0=kT[:, :, tok0:tok0 + 128], in1=enb[cb][:], op=ALU.mult),
                         reads=['kT', ('enb', cb)], writes=[('kt', cb)])
                    for p in range(2):
                        T.op('pool', lambda e, p=p: e.tensor_scalar(out=ktl[cb][:, p, :], in0=kt[cb][:, p, :], scalar1=eb[cb][:, p, last:last + 1], scalar2=None, op0=ALU.mult),
                             reads=[('kt', cb), ('eb', cb)], writes=[('ktl', cb, p)])
                    tb = (idx % 4) * 256

                    def mmtr(e):
                        e.transpose(psb[:, tb:tb + 128], ktl[cb][:, 0, :], cstb[:, 2, :])
                        return e.transpose(psb[:, tb + 128:tb + 256], ktl[cb][:, 1, :], cstb[:, 2, :])
                    T.op('pe', mmtr, reads=[('ktl', cb, 0), ('ktl', cb, 1), 'cstb'], writes=[('psb', idx % 4)])
                    T.op('act', lambda e: e.activation(out=ktT[cb][:].rearrange("p a t -> p (a t)"), in_=psb[:, tb:tb + 256], func=AF.Copy),
                         reads=[('psb', idx % 4)], writes=[('ktT', cb)])
                    if need_out:
                        def mmA(e):
                            last_ = None
                            for h in (0, 2, 1, 3):
                                p, r = h // 2, slice((h % 2) * 64, (h % 2) * 64 + 64)
                                last_ = e.matmul(ps[2 + h % 2][:, p * 128:(p + 1) * 128], kt[cb][r, p, :], qt[cb][r, p, :], start=True, stop=True)
                            return last_
                        T.op('pe', mmA, reads=[('kt', cb), ('qt', cb)], writes=[('ps', 2), ('ps', 3)])
                        for par in range(2):
                            T.op('dve', lambda e, par=par: e.tensor_tensor(out=Am[cb][:, par], in0=ps[2 + par][:, 0:256].rearrange("p (h t) -> p h t", h=2), in1=mrep[:, d_, 0:2, :], op=ALU.mult),
                                 reads=[('ps', 2 + par), 'mrep'], writes=[('Am', cb, par)])

                written = set()

                def zero_state(d_):
                    T.op('dve', lambda e: e.memset(Sst[d_][:], 0.0), writes=[('S', d_, h) for h in range(4)])
                    T.op('dve', lambda e: e.memset(Sbf[d_][:], 0.0), writes=[('Sbf', d_)])

                def fin(idx, item):
                    if item[0] == 'reset':
                        zero_state(item[1])
                        return
                    _, tok0, d_, need_out, ocol = item
                    cb = idx % NB
                    last = 127 if d_ == 0 else 0
                    cj = tok0 // 128
                    if need_out:
                        pO = ps[4 + idx % 2]

                        def mmO(e):
                            last_ = None
                            for h in range(4):
                                p, r = h // 2, slice((h % 2) * 64, (h % 2) * 64 + 64)
                                e.matmul(pO[:, h * 128:(h + 1) * 128], v[:, cj, h * 128:(h + 1) * 128], Am[cb][:, h % 2, h // 2, :], start=True, stop=False)
                                last_ = e.matmul(pO[:, h * 128:(h + 1) * 128], Sbf[d_][r, p, :], qt[cb][r, p, :], start=False, stop=True)
                            return last_
                        T.op('pe', mmO, reads=['v', ('Am', cb, 0), ('Am', cb, 1), ('Sbf', d_), ('qt', cb)], writes=[('ps', 4 + idx % 2)])
                        pOv = pO[:, 0:512].rearrange("p (h t) -> p h t", h=4)
                        okey = ('oT', ocol)
                        if ocol not in written:
                            written.add(ocol)
                            T.op('act', lambda e: e.activation(out=oT[:, :, ocol:ocol + 128], in_=pOv, func=AF.Copy), reads=[('ps', 4 + idx % 2)], writes=[okey])
                        else:
                            T.op('dve', lambda e: e.tensor_tensor(out=oT[:, :, ocol:ocol + 128], in0=pOv, in1=oT[:, :, ocol:ocol + 128], op=ALU.add),
                                 reads=[('ps', 4 + idx % 2), okey], writes=[okey])
                    pU = ps[6]

                    def mmU(e):
                        last_ = None
                        for h in range(4):
                            last_ = e.matmul(pU[:, h * 128:(h + 1) * 128], ktT[cb][:, h // 2, :], v[:, cj, h * 128:(h + 1) * 128], start=True, stop=True)
                        return last_
                    T.op('pe', mmU, reads=[('ktT', cb), 'v'], writes=[('ps', 6)])
                    for h in range(4):
                        p, r = h // 2, slice((h % 2) * 64, (h % 2) * 64 + 64)
                        T.op('dve', lambda e, h=h, p=p, r=r: e.scalar_tensor_tensor(out=Sst[d_][r, p, :], in0=Sst[d_][r, p, :], scalar=eb[cb][r, p, last:last + 1],
                                                                                   in1=pU[r, h * 128:(h + 1) * 128], op0=ALU.mult, op1=ALU.add),
                             reads=[('ps', 6), ('eb', cb), ('S', d_, h)], writes=[('S', d_, h)])
                    T.op('act', lambda e: e.activation(out=Sbf[d_][:], in_=Sst[d_][:], func=AF.Copy), reads=[('S', d_, h) for h in range(4)], writes=[('Sbf', d_)])

                n1 = TFULL[l] // 128
                ng = TGLA[l] // 128
                L1 = [('chunk', TL + j * 128, 0, l == 0, 2560 + j * 128) for j in range(2)]
                L1 += [('chunk', j * 128, 0, True, j * 128) for j in range(n1)]
                L2 = [('chunk', j * 128, 1, j < n1, j * 128) for j in range(ng - 1, -1, -1)]
                if l == 0:
                    L2 += [('reset', 1)] + [('chunk', TL + j * 128, 1, True, 2560 + j * 128) for j in (1, 0)]
                seq = []
                for i in range(max(len(L1), len(L2))):
                    if i < len(L2):
                        seq.append(L2[i])
                    if i < len(L1):
                        seq.append(L1[i])
                zero_state(0)
                zero_state(1)
                LA = 2
                for idx in range(len(seq) + LA):
                    if idx < len(seq):
                        prep(idx, seq[idx])
                    if idx - LA >= 0:
                        fin(idx - LA, seq[idx - LA])
                if dbg:
                    T.dma('sp', 'dbg', out=S['oT'].rearrange("(h p) t -> p h t", p=128), in_=oT[:], reads=[('oT', c_) for c_ in range(0, TO, 128)])
                with contextlib.ExitStack() as ph2:
                    br = [sbuf(ph2, f"G_br{i}", [128, 4, 512], BF16) for i in range(2)]
                    sq = [sbuf(ph2, f"G_sq{i}", [128, 512], BF16) for i in range(2)]
                    rs = [sbuf(ph2, f"G_rs{i}", [128, 512], F32) for i in range(2)]
                    yt = [sbuf(ph2, f"G_yt{i}", [128, 512], F32) for i in range(2)]
                    yst = [sbuf(ph2, f"G_yst{i}", [128, 4, 512], BF16) for i in range(2)]
                    blocks = [(t0, 512, t0) for t0 in range(0, TFULL[l], 512)]
                    if l == 0:
                        blocks.append((TL, 256, 2560))
                    it = 0
                    for bi, (t0, n, oc0) in enumerate(blocks):
                        sl = bi % 2
                        T.dma('sp', f'Gbr{sl}', out=br[sl][:, :, 0:n], in_=S['Br'][:, t0:t0 + n].rearrange("(h p) t -> p h t", p=128), writes=[('br', sl)])
                        okeys = [('oT', c_) for c_ in range(oc0, oc0 + n, 128)]
                        for h in range(4):
                            a = it % 2
                            it += 1
                            pi = 2 + a
                            T.op('act', lambda e, a=a, h=h: e.activation(out=sq[a][:, 0:n], in_=oT[:, h, oc0:oc0 + n], func=AF.Square), reads=okeys, writes=[('gsq', a)])
                            T.op('pe', lambda e, a=a, pi=pi: e.matmul(ps[pi][:, 0:n], ones_bf[:], sq[a][:, 0:n], start=True, stop=True), reads=[('gsq', a), 'ones_bf'], writes=[('ps', pi)])
                            T.op('dve', lambda e, a=a, pi=pi: e.tensor_scalar(out=rs[a][:, 0:n], in0=ps[pi][:, 0:n], scalar1=1.0 / 128, scalar2=EPS, op0=ALU.mult, op1=ALU.add),
                                 reads=[('ps', pi)], writes=[('grs', a)])
                            T.op('act', lambda e, a=a: e.activation(out=rs[a][:, 0:n], in_=rs[a][:, 0:n], func=AF.Sqrt), reads=[('grs', a)], writes=[('grs', a)])
                            T.op('dve', lambda e, a=a: e.reciprocal(out=rs[a][:, 0:n], in_=rs[a][:, 0:n]), reads=[('grs', a)], writes=[('grs', a)])
                            T.op('dve', lambda e, a=a, h=h: e.tensor_tensor(out=yt[a][:, 0:n], in0=oT[:, h, oc0:oc0 + n], in1=rs[a][:, 0:n], op=ALU.mult),
                                 reads=okeys + [('grs', a)], writes=[('gyt', a)])
                            T.op('dve', lambda e, a=a, h=h: e.scalar_tensor_tensor(out=yst[sl][:, h, 0:n], in0=yt[a][:, 0:n], scalar=ppv(f'gnorm{l}', 1),
                                                                                   in1=br[sl][:, h, 0:n], op0=ALU.mult, op1=ALU.mult),
                                 reads=[('gyt', a), ('br', sl), 'ppt'], writes=[('yst', sl, h)])
                        T.dma('sp', f'Gyb{sl}', out=S['yb'][:, t0:t0 + n].rearrange("(h p) t -> p h t", p=128), in_=yst[sl][:, :, 0:n],
                              reads=[('yst', sl, h) for h in range(4)])
            T.barrier()

        def phase_A(l):
            with contextlib.ExitStack() as ph:
                qT = sbuf(ph, "A_q", [128, 4, TT], BF16)
                kT = sbuf(ph, "A_k", [128, TT], BF16)
                v = sbuf(ph, "A_v", [128, TT // 128, 256], BF16)
                esk = sbuf(ph, "A_esk", [128, 8], F32)
                mrepb = sbuf(ph, "A_mrep", [128, 2, 4, 128], BF16)
                P = [sbuf(ph, f"A_P{i}", [128, 5, 512], BF16) for i in range(2)]
                den = [sbuf(ph, f"A_den{i}", [128, 512], F32) for i in range(2)]
                yst = [sbuf(ph, f"A_yst{i}", [64, 4, 128], BF16) for i in range(2)]
                ada_todo = []
                ada_pending = None
                if l == 0:
                    wa = [sbuf(ph, f"A_adaw{i}", [128, 8, 1024], F32) for i in range(2)]
                    ada_todo = [(0, j) for j in range(2, 6)] + [(1, j) for j in range(6)]
                T.dma('sp', 'Aq', out=qT[:], in_=S['Aq'].rearrange("(c p) t -> p c t", p=128), writes=['qT'])
                T.dma('sp', 'Ak', out=kT[:], in_=S['Ak'], writes=['kT'])
                T.dma('sp', 'Av', out=v[:], in_=S['Av'].rearrange("(j p) c -> p j c", p=128), writes=['v'])
                T.op('act', lambda e: e.activation(out=esk[:], in_=ppv(f'sink{l}', 8), func=AF.Exp), reads=['ppt'], writes=['esk'])
                for d_ in range(2):
                    for h in range(4):
                        T.op('pool', lambda e, d_=d_, h=h: e.tensor_copy(out=mrepb[:, d_, h, :], in_=cstb[:, d_, :]), reads=['cstb'], writes=['mrepb'])
                qtiles = [(n, False) for n in range(TFULL[l] // 128)]
                if l == 0:
                    qtiles += [(24, True), (25, True)]
                u = 0
                sb_ = 0
                for (n, isctx) in qtiles:
                    if isctx:
                        klist = [(24, None), (25, None)]
                    else:
                        klist = ([(n - 1, 1)] if n > 0 else []) + [(n, None), (n + 1, 0), (24, None), (25, None)]
                    for g in range(2):
                        sl = u % 2
                        u += 1
                        if l == 0 and u % 4 == 1:
                            if ada_pending is not None:
                                ada_piece(*ada_pending[0], wa, 5, sl=ada_pending[1])
                                ada_pending = None
                            if ada_todo:
                                lj = ada_todo.pop(0)
                                ada_pending = (lj, ada_load(*lj, wa))
                        r = slice(g * 64, g * 64 + 64)
                        for i, (kt_, m) in enumerate(klist):
                            pi = sb_ % 3
                            sb_ += 1
                            T.op('pe', lambda e, pi=pi, kt_=kt_: e.matmul(ps[pi][:, 0:512].rearrange("p (j t) -> p j t", j=4), kT[r, kt_ * 128:(kt_ + 1) * 128],
                                                                         qT[r, :, n * 128:(n + 1) * 128], start=True, stop=True),
                                 reads=['kT', 'qT'], writes=[('ps', pi)])
                            T.op('act', lambda e, pi=pi, i=i: e.activation(out=P[sl][:, i, :], in_=ps[pi][:, 0:512], func=AF.Exp, scale=0.125),
                                 reads=[('ps', pi)], writes=[('P', sl, i)])
                            if m is not None:
                                T.op('pool', lambda e, i=i, m=m: e.tensor_tensor(out=P[sl][:, i, :].rearrange("p (j t) -> p j t", j=4),
                                                                                in0=P[sl][:, i, :].rearrange("p (j t) -> p j t", j=4), in1=mrepb[:, m], op=ALU.mult),
                                     reads=[('P', sl, i), 'mrepb'], writes=[('P', sl, i)])
                        po = 3 + sl

                        def mmO(e, klist=klist, sl=sl, po=po, g=g):
                            last_ = None
                            for i, (kt_, m) in enumerate(klist):
                                last_ = e.matmul(ps[po][:, 0:512], v[:, kt_, g * 128:(g + 1) * 128], P[sl][:, i, :], start=(i == 0), stop=(i == len(klist) - 1))
                            return last_
                        T.op('pe', mmO, reads=['v'] + [('P', sl, i) for i in range(len(klist))], writes=[('ps', po)])
                        for j in range(4):
                            T.op('dve', lambda e, j=j, po=po, sl=sl, g=g: e.tensor_scalar(out=den[sl][64:128, j * 128:(j + 1) * 128], in0=ps[po][64:128, j * 128:(j + 1) * 128],
                                                                                        scalar1=esk[64:128, 4 * g + j:4 * g + j + 1], scalar2=None, op0=ALU.add),
                                 reads=[('ps', po), 'esk'], writes=[('den', sl, j)])
                        T.op('dve', lambda e, sl=sl: e.reciprocal(out=den[sl][64:128, :], in_=den[sl][64:128, :]), reads=[('den', sl, j) for j in range(4)], writes=[('den', sl)])
                        T.op('dve', lambda e, sl=sl, po=po: e.tensor_tensor(out=yst[sl][0:64, :, :], in0=ps[po][0:64, 0:512].rearrange("p (j t) -> p j t", j=4),
                                                                            in1=den[sl][64:128, :].rearrange("p (j t) -> p j t", j=4), op=ALU.mult),
                             reads=[('ps', po), ('den', sl)], writes=[('yst', sl)])
                        T.dma('sp', f'Ayst{sl}', out=S['ya'][g * 256:(g + 1) * 256, n * 128:(n + 1) * 128].rearrange("(j d) t -> d j t", d=64), in_=yst[sl][:],
                              reads=[('yst', sl)])
                if ada_pending is not None:
                    ada_piece(*ada_pending[0], wa, 5, sl=ada_pending[1])
                assert not ada_todo
                if dbg and l == 0:
                    T.dma('sp', 'dbg', out=S['mod'], in_=modt[:].rearrange("p l a k i -> p (l a k i)"),
                          reads=[('modt', l_, a_) for l_ in range(2) for a_ in range(6)])
            T.barrier()

        def phase_C(l):
            with contextlib.ExitStack() as ph:
                qT = sbuf(ph, "C_q", [128, 4, TT], BF16)
                kT = sbuf(ph, "C_k", [128, 4, TT], BF16)
                v = sbuf(ph, "C_v", [128, TT // 128, 1024], BF16)
                EB = sbuf(ph, "C_EB", [128, NCTAB, 512 * 2], BF16)
                P = [sbuf(ph, f"C_P{i}", [128, 7, 512], BF16) for i in range(2)]
                den = [sbuf(ph, f"C_den{i}", [128, 512], F32) for i in range(2)]
                yst = [sbuf(ph, f"C_yst{i}", [64, 4, 128], BF16) for i in range(2)]
                T.dma('sp', 'Cq', out=qT[:], in_=S['Cq'].rearrange("(c p) t -> p c t", p=128), writes=['qT'])
                T.dma('sp', 'Ck', out=kT[:], in_=S['Ck'].rearrange("(c p) t -> p c t", p=128), writes=['kT'])
                T.dma('sp', 'Cv', out=v[:], in_=S['Cv'].rearrange("(j p) c -> p j c", p=128), writes=['v'])
                T.dma('pool', 'Ctab', out=EB[:].rearrange("p a b -> p (a b)"), in_=ctab_in[l], writes=['EBraw'])
                for a in range(NCTAB):
                    T.op('act', lambda e, a=a: e.activation(out=EB[:, a, :], in_=EB[:, a, :], func=AF.Exp), reads=['EBraw'], writes=[('EB', a)])
                ebkeys = [('EB', a) for a in range(NCTAB)]
                qtiles = [(n, False) for n in range(TFULL[l] // 128)]
                if l == 0:
                    qtiles += [(24, True), (25, True)]
                u = 0
                sb_ = 0
                alt = 0
                for (n, isctx) in qtiles:
                    if isctx:
                        klist = [(24, None), (25, None)]
                    elif n == 0:
                        klist = [(0, 0), (1, 1), (2, 2), (3, 3), (24, None), (25, None)]
                    elif n == 1:
                        klist = [(0, 4), (1, 5), (2, 6), (3, 7), (24, None), (25, None)]
                    else:
                        klist = [(n - 2 + i, 8 + i) for i in range(5)] + [(24, None), (25, None)]
                    for hq in range(2):
                        sl = u % 2
                        u += 1
                        for i, (kt_, ti) in enumerate(klist):
                            pr_ = (sb_ % 2) * 2
                            sb_ += 1

                            def mmS(e, pr_=pr_, kt_=kt_, hq=hq):
                                last_ = None
                                for j in (0, 2, 1, 3):
                                    h = 4 * hq + j
                                    r = slice((h % 2) * 64, (h % 2) * 64 + 64)
                                    last_ = e.matmul(ps[pr_ + j % 2][:, (j // 2) * 128:(j // 2 + 1) * 128], kT[r, h // 2, kt_ * 128:(kt_ + 1) * 128],
                                                     qT[r, h // 2, n * 128:(n + 1) * 128], start=True, stop=True)
                                return last_
                            T.op('pe', mmS, reads=['kT', 'qT'], writes=[('ps', pr_), ('ps', pr_ + 1)])
                            for par in range(2):
                                T.op('act', lambda e, pr_=pr_, i=i, sl=sl, par=par: e.activation(out=P[sl][:, i, par * 256:(par + 1) * 256], in_=ps[pr_ + par][:, 0:256], func=AF.Exp, scale=0.125),
                                     reads=[('ps', pr_ + par)], writes=[('P', sl, i, par)])
                            if ti is not None:
                                alt += 1
                                eng = 'pool' if alt % 2 else 'dve'
                                T.op(eng, lambda e, i=i, ti=ti, sl=sl, hq=hq: e.tensor_tensor(out=P[sl][:, i, :], in0=P[sl][:, i, :], in1=EB[:, ti, hq * 512:(hq + 1) * 512], op=ALU.mult),
                                     reads=[('P', sl, i, 0), ('P', sl, i, 1)] + ebkeys, writes=[('P', sl, i, 0), ('P', sl, i, 1)])
                        po = 4 + sl

                        def mmO(e, klist=klist, sl=sl, po=po, hq=hq):
                            last_ = None
                            for j in range(4):
                                h = 4 * hq + j
                                sj = (j % 2) * 2 + j // 2
                                for i, (kt_, ti) in enumerate(klist):
                                    last_ = e.matmul(ps[po][:, j * 128:(j + 1) * 128], v[:, kt_, h * 128:(h + 1) * 128], P[sl][:, i, sj * 128:(sj + 1) * 128],
                                                     start=(i == 0), stop=(i == len(klist) - 1))
                            return last_
                        T.op('pe', mmO, reads=['v'] + [('P', sl, i, par) for i in range(len(klist)) for par in range(2)], writes=[('ps', po)])
                        T.op('dve', lambda e, sl=sl, po=po: e.reciprocal(out=den[sl][64:128, :], in_=ps[po][64:128, 0:512]), reads=[('ps', po)], writes=[('den', sl)])
                        T.op('dve', lambda e, sl=sl, po=po: e.tensor_tensor(out=yst[sl][0:64, :, :], in0=ps[po][0:64, 0:512].rearrange("p (j t) -> p j t", j=4),
                                                                            in1=den[sl][64:128, :].rearrange("p (j t) -> p j t", j=4), op=ALU.mult),
                             reads=[('ps', po), ('den', sl)], writes=[('yst', sl)])
                        T.dma('sp', f'Cyst{sl}', out=S['yc'][hq * 256:(hq + 1) * 256, n * 128:(n + 1) * 128].rearrange("(j d) t -> d j t", d=64), in_=yst[sl][:],
                              reads=[('yst', sl)])
            T.barrier()

        def phase_M(l):
            with contextlib.ExitStack() as ph:
                wm = sbuf(ph, "M_wm", [128, 8, 3072], BF16)
                wb = sbuf(ph, "M_wb", [128, 3, 4, 1024], BF16)
                wo = sbuf(ph, "M_wo", [128, 8, 1024], BF16)
                hTs = [sbuf(ph, f"M_hT{i}", [128, 8, 512], BF16) for i in range(2)]
                ys = [sbuf(ph, f"M_y{i}", [128, 3, 4, 512], BF16) for i in range(2)]
                xts = [sbuf(ph, f"M_xt{i}", [128, 8, 512], F32) for i in range(2)]
                mix = sbuf(ph, "M_mix", [128, 8, 512], BF16)
                gsb = sbuf(ph, "M_gsb", [128, 3, 512], F32)
                mt = sbuf(ph, "M_mt", [128, 3, 512], F32)
                sq = sbuf(ph, "M_sq", [128, 8, 512], BF16)
                rs = sbuf(ph, "M_rs", [128, 512], F32)
                tmpf = sbuf(ph, "M_tmpf", [128, 2, 512], F32)
                hf = sbuf(ph, "M_hf", [128, 8, 512], BF16)
                wsrcs = (wba_in, wbb_in, wbc_in)
                di = 0
                for oc in range(8):
                    for b_ in range(3):
                        c0 = b_ * 1024 + oc * 128
                        T.dma('pool', f'wM{di % 16}', out=wm[:, :, c0:c0 + 128], in_=wmerge_in[l, :, c0:c0 + 128].rearrange("(k p) n -> p k n", p=128), writes=[('wm', b_, oc)], throttle=True)
                        di += 1
                        T.dma('pool', f'wM{di % 16}', out=wb[:, b_, :, oc * 128:(oc + 1) * 128], in_=wsrcs[b_][l, :, oc * 128:(oc + 1) * 128].rearrange("(k p) n -> p k n", p=128), writes=[('wb', b_, oc)], throttle=True)
                        di += 1
                for oc in range(8):
                    T.dma('pool', f'wM{di % 16}', out=wo[:, :, oc * 128:(oc + 1) * 128], in_=wout_in[l, :, oc * 128:(oc + 1) * 128].rearrange("(k p) n -> p k n", p=128), writes=[('wo', oc)], throttle=True)
                    di += 1
                blocks = [(t0, 512, 0) for t0 in range(0, TFULL[l], 512)]
                if l == 0:
                    blocks.append((TL, 256, 1))
                pr = [0]

                def nextps():
                    i = pr[0] % 6
                    pr[0] += 1
                    return i
                for bi, (t0, n, isctx) in enumerate(blocks):
                    sl = bi % 2
                    hT, y, xt = hTs[sl], ys[sl], xts[sl]
                    key = ('mblk', sl)
                    T.dma('sp', f'MhT{sl}', out=hT[:, :, 0:n], in_=S['hT'][:, t0:t0 + n].rearrange("(k p) t -> p k t", p=128), writes=[(key, 'hTm')])
                    for bi_, nm in enumerate(('ya', 'yb', 'yc')):
                        T.dma('sp', f'My{sl}', out=y[:, bi_, :, 0:n], in_=S[nm][:, t0:t0 + n].rearrange("(k p) t -> p k t", p=128), writes=[(key, 'y', bi_)])
                    if l == 0:
                        src = (ctxT_in if isctx else xT_in[:, t0:t0 + n])
                    else:
                        src = S['xs'][:, t0:t0 + n]
                    T.dma('sp', f'Mx{sl}', out=xt[:, :, 0:n], in_=src.rearrange("(k p) t -> p k t", p=128), reads=[('xsd', t0)], writes=[(key, 'xt')])
                    for oc in range(8):
                        for b_ in range(3):
                            pg = nextps()

                            def mmg(e, pg=pg, b_=b_, oc=oc):
                                last_ = None
                                c0 = b_ * 1024 + oc * 128
                                for k in range(8):
                                    last_ = e.matmul(ps[pg][:, 0:n], wm[:, k, c0:c0 + 128], hT[:, k, 0:n], start=(k == 0), stop=(k == 7))
                                return last_
                            T.op('pe', mmg, reads=[('wm', b_, oc), (key, 'hTm')], writes=[('ps', pg)])
                            T.op('act', lambda e, pg=pg, b_=b_, oc=oc: e.activation(out=gsb[:, b_, 0:n], in_=ps[pg][:, 0:n], func=AF.Sigmoid,
                                                                                  bias=ppt[:, PP[f'bmerge{l}'] + b_ * 8 + oc:PP[f'bmerge{l}'] + b_ * 8 + oc + 1], scale=1.0),
                                 reads=[('ps', pg), 'ppt'], writes=[('gsb', b_)])
                            pp_ = nextps()

                            def mmp(e, pp_=pp_, b_=b_, oc=oc):
                                last_ = None
                                for k in range(4):
                                    last_ = e.matmul(ps[pp_][:, 0:n], wb[:, b_, k, oc * 128:(oc + 1) * 128], y[:, b_, k, 0:n], start=(k == 0), stop=(k == 3))
                                return last_
                            T.op('pe', mmp, reads=[('wb', b_, oc), (key, 'y', b_)], writes=[('ps', pp_)])
                            T.op('dve', lambda e, pp_=pp_, b_=b_: e.tensor_tensor(out=mt[:, b_, 0:n], in0=ps[pp_][:, 0:n], in1=gsb[:, b_, 0:n], op=ALU.mult),
                                 reads=[('ps', pp_), ('gsb', b_)], writes=[('mt', b_)])
                        T.op('dve', lambda e: e.tensor_tensor(out=mt[:, 0, 0:n], in0=mt[:, 0, 0:n], in1=mt[:, 1, 0:n], op=ALU.add),
                             reads=[('mt', 0), ('mt', 1)], writes=[('mt', 0)])
                        T.op('dve', lambda e, oc=oc: e.tensor_tensor(out=mix[:, oc, 0:n], in0=mt[:, 0, 0:n], in1=mt[:, 2, 0:n], op=ALU.add),
                             reads=[('mt', 0), ('mt', 2)], writes=[('mix', oc)])
                    for oc in range(8):
                        po = nextps()

                        def mmo(e, po=po, oc=oc):
                            last_ = None
                            for k in range(8):
                                last_ = e.matmul(ps[po][:, 0:n], wo[:, k, oc * 128:(oc + 1) * 128], mix[:, k, 0:n], start=(k == 0), stop=(k == 7))
                            return last_
                        T.op('pe', mmo, reads=[('wo', oc)] + [('mix', k) for k in range(8)], writes=[('ps', po)])
                        T.op('dve', lambda e, po=po, oc=oc: e.scalar_tensor_tensor(out=xt[:, oc, 0:n], in0=ps[po][:, 0:n], scalar=modt[:, l, 2, oc, isctx:isctx + 1],
                                                                                 in1=xt[:, oc, 0:n], op0=ALU.mult, op1=ALU.add),
                             reads=[('ps', po), (key, 'xt')], writes=[(key, 'xt')])
                    T.dma('sp', f'Mxs{sl}', out=S['xs'][:, t0:t0 + n].rearrange("(k p) t -> p k t", p=128), in_=xt[:, :, 0:n], reads=[(key, 'xt')], writes=[('xsd', t0)])
                    norm_block(xt, (key, 'xt'), sq, rs, tmpf, hf, 'hfm', n, l, 3, isctx)
                    T.dma('sp', 'Mhf', out=S['hf'][:, t0:t0 + n].rearrange("(k p) t -> p k t", p=128), in_=hf[:, :, 0:n], reads=[('hfm', 'hT', k) for k in range(8)])
            T.barrier()

        def phase_F(l):
            with contextlib.ExitStack() as ph:
                wi = sbuf(ph, "F_wi", [128, 8, 2 * HID], BF16)
                wo2 = sbuf(ph, "F_wo", [128, NHC, 1024], BF16)
                hfs = [sbuf(ph, f"F_hf{i}", [128, 8, 512], BF16) for i in range(2)]
                act = sbuf(ph, "F_act", [128, NHC, 512], BF16)
                xt = sbuf(ph, "F_xt", [128, 8, 512], F32)
                gs = sbuf(ph, "F_gs", [128, 2, 512], F32)
                di = 0
                for hc in range(NHC):
                    for half in range(2):
                        c0 = half * HID + hc * 128
                        T.dma('pool', f'wF{di % 16}', out=wi[:, :, c0:c0 + 128], in_=wffi_in[l, :, c0:c0 + 128].rearrange("(k p) n -> p k n", p=128), writes=[('wi', half, hc)], throttle=True)
                        di += 1
                for k in range(2):
                    T.dma('pool', f'wF{di % 16}', out=wo2[:, k * 11:(k + 1) * 11, :], in_=wffo_in[l, k * 1408:(k + 1) * 1408, :].rearrange("(k p) n -> p k n", p=128), writes=[('wo2', k)], throttle=True)
                    di += 1
                blocks = [(t0, 512, 0) for t0 in range(0, TFULL[l], 512)]
                if l == 0:
                    blocks.append((TL, 256, 1))
                pr = [0]

                def nextps():
                    i = pr[0] % 6
                    pr[0] += 1
                    return i
                for bi, (t0, n, isctx) in enumerate(blocks):
                    sl = bi % 2
                    hf = hfs[sl]
                    T.dma('sp', f'Fhf{sl}', out=hf[:, :, 0:n], in_=S['hf'][:, t0:t0 + n].rearrange("(k p) t -> p k t", p=128), writes=[('hf', sl)])
                    T.dma('sp', 'Fx', out=xt[:, :, 0:n], in_=S['xs'][:, t0:t0 + n].rearrange("(k p) t -> p k t", p=128), reads=[('xsd', t0)], writes=['xt'])
                    for hc in range(NHC):
                        pg = nextps()
                        pu = nextps()

                        def mmg(e, pg=pg, pu=pu, hc=hc):
                            last_ = None
                            for k in range(8):
                                e.matmul(ps[pg][:, 0:n], wi[:, k, hc * 128:(hc + 1) * 128], hf[:, k, 0:n], start=(k == 0), stop=(k == 7))
                            for k in range(8):
                                last_ = e.matmul(ps[pu][:, 0:n], wi[:, k, HID + hc * 128:HID + (hc + 1) * 128], hf[:, k, 0:n], start=(k == 0), stop=(k == 7))
                            return last_
                        T.op('pe', mmg, reads=[('wi', 0, hc), ('wi', 1, hc), ('hf', sl)], writes=[('ps', pg), ('ps', pu)])
                        a = hc % 2
                        T.op('act', lambda e, pg=pg, a=a: e.activation(out=gs[:, a, 0:n], in_=ps[pg][:, 0:n], func=AF.Silu), reads=[('ps', pg)], writes=[('gs', a)])
                        T.op('dve', lambda e, pu=pu, a=a, hc=hc: e.tensor_tensor(out=act[:, hc, 0:n], in0=ps[pu][:, 0:n], in1=gs[:, a, 0:n], op=ALU.mult),
                             reads=[('ps', pu), ('gs', a)], writes=[('act', hc)])
                    for oc in range(8):
                        po = nextps()

                        def mmo(e, po=po, oc=oc):
                            last_ = None
                            for hc in range(NHC):
                                last_ = e.matmul(ps[po][:, 0:n], wo2[:, hc, oc * 128:(oc + 1) * 128], act[:, hc, 0:n], start=(hc == 0), stop=(hc == NHC - 1))
                            return last_
                        T.op('pe', mmo, reads=[('wo2', 0), ('wo2', 1)] + [('act', hc) for hc in range(NHC)], writes=[('ps', po)])
                        T.op('dve', lambda e, po=po, oc=oc: e.scalar_tensor_tensor(out=xt[:, oc, 0:n], in0=ps[po][:, 0:n], scalar=modt[:, l, 5, oc, isctx:isctx + 1],
                                                                                 in1=xt[:, oc, 0:n], op0=ALU.mult, op1=ALU.add),
                             reads=[('ps', po), 'xt'], writes=['xt'])
                    if l == 0:
                        T.dma('sp', 'Fxs', out=S['xs'][:, t0:t0 + n].rearrange("(k p) t -> p k t", p=128), in_=xt[:, :, 0:n], reads=['xt'], writes=[('xsd', t0)])
                    else:
                        sqf = act[:, 0:8, :]
                        T.op('act', lambda e: e.activation(out=sqf[:, :, 0:n], in_=xt[:, :, 0:n], func=AF.Square), reads=['xt'] + [('act', hc) for hc in range(8)], writes=[('act', hc) for hc in range(8)])

                        def mms(e):
                            last_ = None
                            for k in range(8):
                                last_ = e.matmul(ps[6][:, 0:n], ones_bf[:], sqf[:, k, 0:n], start=(k == 0), stop=(k == 7))
                            return last_
                        T.op('pe', mms, reads=[('act', hc) for hc in range(8)] + ['ones_bf'], writes=[('ps', 6)])
                        T.op('dve', lambda e: e.tensor_scalar(out=gs[:, 0, 0:n], in0=ps[6][:, 0:n], scalar1=1.0 / D, scalar2=EPS, op0=ALU.mult, op1=ALU.add),
                             reads=[('ps', 6)], writes=[('gs', 0)])
                        T.op('act', lambda e: e.activation(out=gs[:, 0, 0:n], in_=gs[:, 0, 0:n], func=AF.Sqrt), reads=[('gs', 0)], writes=[('gs', 0)])
                        T.op('dve', lambda e: e.reciprocal(out=gs[:, 0, 0:n], in_=gs[:, 0, 0:n]), reads=[('gs', 0)], writes=[('gs', 0)])
                        for k in range(8):
                            T.op('dve', lambda e, k=k: e.scalar_tensor_tensor(out=xt[:, k, 0:n], in0=xt[:, k, 0:n], scalar=ppt[:, PP['fnorm'] + k:PP['fnorm'] + k + 1],
                                                                              in1=gs[:, 0, 0:n], op0=ALU.mult, op1=ALU.mult),
                                 reads=['xt', ('gs', 0), 'ppt'], writes=['xt'])
                        T.dma('sp', 'Fout', out=outT[:, t0:t0 + n].rearrange("(k p) t -> p k t", p=128), in_=xt[:, :, 0:n], reads=['xt'])
            T.barrier()

        phases = [('ada', phase_ada)]
        for l_ in range(2):
            phases += [(f'B{l_}', lambda l_=l_: phase_B(l_)), (f'G{l_}', lambda l_=l_: phase_G(l_)), (f'A{l_}', lambda l_=l_: phase_A(l_)),
                       (f'C{l_}', lambda l_=l_: phase_C(l_)), (f'M{l_}', lambda l_=l_: phase_M(l_)), (f'F{l_}', lambda l_=l_: phase_F(l_))]
        if only is not None:
            phases = [p for p in phases if p[0] in only]
        for name, fn in phases:
            fn()
            if stop_after == name:
                break
        T.final_wait('sp')
        print("instructions emitted:", T.ninst)
    return nc


IN_SIZES = (512, 128, 128, 256, 256, 512, 512, 32, 512, 512, 512)
IN_OFF = np.concatenate([[0], np.cumsum(IN_SIZES)])


def _local_to_global(half):
    tau = np.arange(TL)
    return tau if half == 0 else (SEQ - 1 - tau)


def _winx(w_in, half):
    o = IN_OFF
    aq = w_in[:, o[0]:o[1]]; ak = w_in[:, o[1]:o[2]]; av = w_in[:, o[2]:o[3]]
    bq = w_in[:, o[3]:o[4]]; bk = w_in[:, o[4]:o[5]]; bv = w_in[:, o[5]:o[6]]; br = w_in[:, o[6]:o[7]]
    bz = w_in[:, o[7]:o[8]]
    cq = w_in[:, o[8]:o[9]]; ck = w_in[:, o[9]:o[10]]; cv = w_in[:, o[10]:o[11]]
    out = np.zeros((D, NX), np.float32)
    sw = (np.arange(64) + 32) % 64
    heads = []
    for c in range(4):
        heads += [c, 4 + c]
    idx = np.concatenate([h * 64 + np.arange(64) for h in heads])
    idxs = np.concatenate([h * 64 + sw for h in heads])
    out[:, COL['Aq']:COL['Aq'] + 512] = aq[:, idx]
    out[:, COL['Aqs']:COL['Aqs'] + 512] = aq[:, idxs]
    out[:, COL['Ak']:COL['Ak'] + 128] = ak
    out[:, COL['Aks']:COL['Aks'] + 128] = ak[:, np.concatenate([sw, 64 + sw])]
    out[:, COL['Cq']:COL['Cq'] + 512] = cq
    out[:, COL['Ck']:COL['Ck'] + 512] = ck
    out[:, COL['Bq']:COL['Bq'] + 256] = bq
    out[:, COL['Bk']:COL['Bk'] + 256] = bk
    out[:, COL['Br']:COL['Br'] + 512] = br
    z1, z2 = (bz[:, 0:16], bz[:, 16:32]) if half == 0 else (bz[:, 16:32], bz[:, 0:16])
    out[:, COL['Bz']:COL['Bz'] + 16] = z1
    out[:, COL['Bz'] + 32:COL['Bz'] + 48] = z2
    out[:, COL['Av']:COL['Av'] + 128] = av
    out[:, COL['Bv']:COL['Bv'] + 512] = bv
    out[:, COL['Cv']:COL['Cv'] + 512] = cv
    return out


def _rope_tables(half):
    t = _local_to_global(half)
    row = (t // 64).astype(np.float32)
    col = (t % 64).astype(np.float32)
    inv = (np.float32(10000.0) ** (-np.arange(16, dtype=np.float32) / np.float32(16))).astype(np.float32)
    ang = np.concatenate([row[:, None] * inv[None], col[:, None] * inv[None]], axis=-1).astype(np.float32)
    cos = np.cos(ang).astype(np.float32).T
    sin = np.sin(ang).astype(np.float32).T
    cs = np.zeros((128, 2, TT), np.float32)
    cs[:, 0, TL:] = 1.0
    for rep in range(2):
        b = rep * 64
        cs[b:b + 32, 0, :TL] = cos; cs[b + 32:b + 64, 0, :TL] = cos
        cs[b:b + 32, 1, :TL] = -sin; cs[b + 32:b + 64, 1, :TL] = sin
    return cs


def _ctab(rpb, half):
    tab = np.full((128, NCTAB, 8, 128), -30000.0, np.float32)
    pairs = [(0, 0), (0, 1), (0, 2), (0, 3), (1, 0), (1, 1), (1, 2), (1, 3), (4, 2), (4, 3), (4, 4), (4, 5), (4, 6)]
    loc = np.arange(128)
    for ti, (qn, kn) in enumerate(pairs):
        tq = qn * 128 + loc
        tk = kn * 128 + loc
        if half == 1:
            tq = SEQ - 1 - tq
            tk = SEQ - 1 - tk
        qr, qc = tq // 64, tq % 64
        kr, kc = tk // 64, tk % 64
        rs = np.clip(qr - 4, 0, 64 - 8)
        ws = np.clip(qc - 8, 0, 64 - 16)
        valid = ((kr[:, None] >= rs[None]) & (kr[:, None] < rs[None] + 8) &
                 (kc[:, None] >= ws[None]) & (kc[:, None] < ws[None] + 16))
        dr = np.clip(kr[:, None] - qr[None] + 7, 0, 14)
        dc = np.clip(kc[:, None] - qc[None], -15, 15) + 15
        vals = rpb[:, dr, dc]
        tab[:, ti] = np.where(valid[None], vals, np.float32(-30000.0)).transpose(1, 0, 2)[:, [0, 2, 1, 3, 4, 6, 5, 7], :]
    return tab.reshape(128, NCTAB * 8 * 128)


def _dup2(v):
    a = v.reshape(-1, 128).T
    return np.repeat(a[:, :, None], 2, axis=2).reshape(128, -1)


def prep_core(inputs, core):
    b, half = core // 2, core % 2
    x = inputs['x'][b]
    t = np.arange(TL) if half == 0 else (SEQ - 1 - np.arange(TL))
    m = {}
    m['xT'] = np.ascontiguousarray(x[t].T)
    ctx = inputs['ctx'][b]
    if half == 1:
        ctx = ctx[::-1]
    m['ctxT'] = np.ascontiguousarray(ctx.T)
    pp = np.zeros((128, NPP), np.float32)
    cc = np.stack([inputs['c'][b].reshape(8, 128).T, inputs['c_ctx'].reshape(8, 128).T], axis=2)
    pp[:, PP['c']:PP['c'] + 16] = cc.reshape(128, 16)
    for l in range(2):
        pp[:, PP[f'bada{l}']:PP[f'bada{l}'] + 96] = _dup2(inputs['b_ada'][l])
        pp[:, PP[f'nmix{l}']:PP[f'nmix{l}'] + 16] = _dup2(inputs['norm_mix'][l])
        pp[:, PP[f'nffn{l}']:PP[f'nffn{l}'] + 16] = _dup2(inputs['norm_ffn'][l])
        pp[:, PP[f'bmerge{l}']:PP[f'bmerge{l}'] + 24] = inputs['b_merge'][l].reshape(24, 128).T
        pp[:, PP[f'gnorm{l}']] = inputs['gla_norm'][l]
        pp[:, PP[f'sink{l}']:PP[f'sink{l}'] + 8] = inputs['attn_sink'][l][None, :]
    pp[:, PP['fnorm']:PP['fnorm'] + 8] = inputs['final_norm'].reshape(8, 128).T
    m['pp'] = pp
    m['w_ada'] = inputs['w_ada']
    m['winx'] = np.stack([_winx(inputs['w_in'][l], half) for l in range(2)])
    gw = np.zeros((64, 2, 256), np.float32)
    d1w, d2w = ('gla_gate_w_fwd', 'gla_gate_w_bwd') if half == 0 else ('gla_gate_w_bwd', 'gla_gate_w_fwd')
    d1b, d2b = ('gla_gate_b_fwd', 'gla_gate_b_bwd') if half == 0 else ('gla_gate_b_bwd', 'gla_gate_b_fwd')
    for l in range(2):
        gw[0:16, l] = inputs[d1w][l]; gw[32:48, l] = inputs[d2w][l]
        gw[16, l] = inputs[d1b][l]; gw[48, l] = inputs[d2b][l]
    m['gwp'] = gw
    m['ones_row'] = np.ones((1, TT), np.float32)
    m['cossin'] = _rope_tables(half)
    s_, t_ = np.meshgrid(np.arange(128), np.arange(128), indexing='ij')
    m['cst'] = np.stack([(s_ <= t_), (s_ >= t_), (s_ == t_)], axis=1).astype(np.float32)
    m['ctab'] = np.stack([_ctab(inputs['na_rpb'][l], half) for l in range(2)])
    for k in ('w_branch_a', 'w_branch_b', 'w_branch_c', 'w_merge', 'w_out', 'w_ffn_in', 'w_ffn_out'):
        m[k] = inputs[k]
    return m


_NC_CACHE = {}


def kernel(**inputs):
    inputs = {k: np.asarray(v) for k, v in inputs.items()}
    if 'nc' not in _NC_CACHE:
        _NC_CACHE['nc'] = build()
    nc = _NC_CACHE['nc']
    in_maps = [prep_core(inputs, c) for c in range(8)]
    res = run_bass_kernel_spmd(nc, in_maps, core_ids=list(range(8)))
    out = np.zeros((4, SEQ, D), np.float32)
    for c in range(8):
        b, half = c // 2, c % 2
        o = res.results[c]["outT"].T
        if half == 0:
            out[b, 0:2048] = o
        else:
            out[b, 2048:] = o[::-1]
    return out
```

```python
import contextlib
import numpy as np
import concourse.bass as bass
import concourse.mybir as mybir
from concourse.bass_utils import run_bass_kernel_spmd

F32 = mybir.dt.float32
BF16 = mybir.dt.bfloat16
AF = mybir.ActivationFunctionType
ALU = mybir.AluOpType

D = 1024
KC = 8
SEQ = 4096
TL = 3072
CT = 256
TT = TL + CT
TO = 2560 + CT
EPS = 1e-6
HID = 2816
NHC = 22

FM_GROUPS = [('Aq', 512), ('Aqs', 512), ('Ak', 128), ('Aks', 128), ('Cq', 512), ('Ck', 512),
             ('Bq', 256), ('Bk', 256), ('Br', 512), ('Bz', 64)]
TM_GROUPS = [('Av', 128), ('Bv', 512), ('Cv', 512)]
COL = {}
_o = 0
for _n, _w in FM_GROUPS + TM_GROUPS:
    COL[_n] = _o
    _o += _w
NX = _o

PP = {}
_o = 0
def _pp(name, n):
    global _o
    PP[name] = _o
    _o += n
_pp('c', 16)
for _l in range(2):
    _pp(f'bada{_l}', 96); _pp(f'nmix{_l}', 16); _pp(f'nffn{_l}', 16); _pp(f'bmerge{_l}', 24)
    _pp(f'gnorm{_l}', 1); _pp(f'sink{_l}', 8)
_pp('fnorm', 8)
NPP = _o

BLOCKS = [
    [(0, 512, 'F', 0), (512, 512, 'F', 0), (1024, 512, 'F', 0), (1536, 512, 'F', 0), (2048, 512, 'F', 0),
     (2560, 512, 'KV', 0), (TL, 256, 'F', 1)],
    [(0, 512, 'F', 0), (512, 512, 'F', 0), (1024, 512, 'F', 0), (1536, 512, 'F', 0),
     (2048, 512, 'KV', 0), (TL, 256, 'KV', 1)],
]
TFULL = [2560, 2048]
TGLA = [3072, 2560]
NCTAB = 13


class Trk:
    ENG = ('pe', 'act', 'dve', 'pool', 'sp')

    def __init__(s, nc, es):
        s.nc, s.es = nc, es
        s.E = dict(pe=nc.tensor, act=nc.scalar, dve=nc.vector, pool=nc.gpsimd, sp=nc.sync)
        s.sem = {}
        s.cnt = {}
        s.epoch = 0
        s.seen = {e: {} for e in s.ENG}
        s.lastw = {}
        s.rd = {}
        s._new_compute_sems()
        s.ninst = 0
        s.dmap = {}
        s.dfree = []
        s.nd = 0

    def _new_compute_sems(s):
        for e in ('pe', 'act', 'dve', 'pool'):
            s.sem[e] = s.es.enter_context(s.nc.semaphore(f"c{s.epoch}_{e}"))
            s.cnt[e] = 0
            for w in s.ENG:
                s.seen[w].pop(e, None)

    def dsem(s, name):
        if name in s.dmap:
            return s.dmap[name]
        if s.dfree:
            key = s.dfree.pop()
        else:
            key = 'd_%d' % s.nd
            s.nd += 1
            s.sem[key] = s.es.enter_context(s.nc.semaphore(key))
            s.cnt[key] = 0
        s.dmap[name] = key
        return key

    def _wait(s, e, tok):
        k, v = tok
        if k == e and e == 'pe':
            return
        if s.seen[e].get(k, 0) >= v:
            return
        s.E[e].wait_ge(s.sem[k], v)
        s.seen[e][k] = v
        s.ninst += 1

    def _deps(s, e, reads, writes):
        for k in reads:
            if k in s.lastw:
                s._wait(e, s.lastw[k])
        for k in writes:
            for sk, v in s.rd.get(k, {}).items():
                if sk != e:
                    s._wait(e, (sk, v))
            if k in s.lastw and s.lastw[k][0] != e:
                s._wait(e, s.lastw[k])

    def _post(s, tok, reads, writes):
        for k in reads:
            d = s.rd.setdefault(k, {})
            d[tok[0]] = max(d.get(tok[0], 0), tok[1])
        for k in writes:
            s.lastw[k] = tok
            s.rd[k] = {}

    def op(s, e, fn, reads=(), writes=()):
        s._deps(e, reads, writes)
        inst = fn(s.E[e])
        s.cnt[e] += 1
        inst.then_inc(s.sem[e], 1)
        s._post((e, s.cnt[e]), reads, writes)
        s.ninst += 1

    def dma(s, q, semname, out, in_, reads=(), writes=(), throttle=False):
        key = s.dsem(semname)
        if throttle and s.cnt[key] > 0:
            s._wait(q, (key, s.cnt[key]))
        s._deps(q, reads, writes)
        inst = s.E[q].dma_start(out=out, in_=in_)
        s.cnt[key] += 16
        inst.then_inc(s.sem[key], 16)
        s._post((key, s.cnt[key]), reads, writes)
        s.ninst += 1

    def barrier(s):
        for e in s.ENG:
            for k in list(s.sem):
                if k == e:
                    continue
                if s.cnt[k] > 0:
                    s._wait(e, (k, s.cnt[k]))
        s.lastw.clear()
        s.rd.clear()
        s.dfree += sorted(s.dmap.values())
        s.dmap = {}
        s.epoch += 1
        s._new_compute_sems()

    def final_wait(s, e='sp'):
        for k in list(s.sem):
            if k != e and s.cnt[k] > 0:
                s._wait(e, (k, s.cnt[k]))


def build(stop_after=None, dbg=False, only=None):
    nc = bass.Bass("TRN2", target_bir_lowering=False)
    okind = "ExternalOutput" if dbg else "Internal"

    def din(name, shape, dt=F32):
        return nc.dram_tensor(name, list(shape), dt, kind="ExternalInput").ap()

    def dscr(name, shape, dt=BF16):
        return nc.dram_tensor(name, list(shape), dt, kind=okind).ap()

    xT_in = din("xT", [D, TL])
    ctxT_in = din("ctxT", [D, CT])
    pp_in = din("pp", [128, NPP])
    wada_in = din("w_ada", [2, D, 6 * D])
    winx_in = din("winx", [2, D, NX])
    gwp_in = din("gwp", [64, 2, 256])
    onesrow_in = din("ones_row", [1, TT])
    cs_in = din("cossin", [128, 2, TT])
    cst_in = din("cst", [128, 3, 128])
    ctab_in = din("ctab", [2, 128, NCTAB * 8 * 128])
    wba_in = din("w_branch_a", [2, 512, D])
    wbb_in = din("w_branch_b", [2, 512, D])
    wbc_in = din("w_branch_c", [2, 512, D])
    wmerge_in = din("w_merge", [2, D, 3 * D])
    wout_in = din("w_out", [2, D, D])
    wffi_in = din("w_ffn_in", [2, D, 2 * HID])
    wffo_in = din("w_ffn_out", [2, HID, D])
    outT = nc.dram_tensor("outT", [D, 2048], F32, kind="ExternalOutput").ap()

    S = {}
    S['hT'] = dscr("s_hT", [D, TT])
    S['Aq'] = dscr("s_Aq", [512, TT]); S['Ak'] = dscr("s_Ak", [128, TT])
    S['Cq'] = dscr("s_Cq", [512, TT]); S['Ck'] = dscr("s_Ck", [512, TT])
    S['Bq'] = dscr("s_Bq", [256, TT]); S['Bk'] = dscr("s_Bk", [256, TT]); S['Br'] = dscr("s_Br", [512, TT])
    S['Bz'] = dscr("s_Bz", [64, TT], F32)
    S['Av'] = dscr("s_Av", [TT, 256]); S['Bv'] = dscr("s_Bv", [TT, 512]); S['Cv'] = dscr("s_Cv", [TT, 1024])
    S['ya'] = dscr("s_ya", [512, TT]); S['yb'] = dscr("s_yb", [512, TT]); S['yc'] = dscr("s_yc", [512, TT])
    S['xs'] = dscr("s_xs", [D, TT], F32)
    S['hf'] = dscr("s_hf", [D, TT])
    if dbg:
        S['mod'] = dscr("s_mod", [128, 2 * 6 * 8 * 2], F32)
        S['oT'] = dscr("s_oT", [512, TO], F32)

    with contextlib.ExitStack() as es:
        T = Trk(nc, es)

        ucnt = [0]

        def sbuf(st, name, shape, dt):
            ucnt[0] += 1
            return st.enter_context(nc.sbuf_tensor(f"{name}_{ucnt[0]}", list(shape), dt))

        ps = [es.enter_context(nc.psum_tensor(f"ps{i}", [128, 512], F32)) for i in range(7)]
        psb = es.enter_context(nc.psum_tensor("psb", [128, 1024], BF16))
        ppt = sbuf(es, "ppt", [128, NPP], F32)
        modt = sbuf(es, "modt", [128, 2, 6, 8, 2], F32)
        cst = sbuf(es, "cstf", [128, 3, 128], F32)
        cstb = sbuf(es, "cstb", [128, 3, 128], BF16)
        ones_bf = sbuf(es, "ones_bf", [128, 128], BF16)
        ones_f = sbuf(es, "ones_f", [128, 128], F32)

        T.dma('sp', 'g0', out=ppt[:], in_=pp_in, writes=['ppt'])
        T.dma('sp', 'g1', out=cst[:], in_=cst_in, writes=['cst'])
        T.op('dve', lambda e: e.tensor_copy(out=cstb[:], in_=cst[:]), reads=['cst'], writes=['cstb'])
        T.op('pool', lambda e: e.memset(ones_bf[:], 1.0), writes=['ones_bf'])
        T.op('pool', lambda e: e.memset(ones_f[:], 1.0), writes=['ones_f'])

        def ppv(name, n):
            return ppt[:, PP[name]:PP[name] + n]

        sc = sbuf(es, "ada_sc", [128, 16], F32)
        adatmp = sbuf(es, "ada_tmp", [128, 16], F32)
        T.op('act', lambda e: e.activation(out=sc[:], in_=ppv('c', 16), func=AF.Silu), reads=['ppt'], writes=['sc'])
        scb = sbuf(es, "ada_scb", [128, 16], BF16)
        T.op('dve', lambda e: e.tensor_copy(out=scb[:], in_=sc[:]), reads=['sc'], writes=['sc'])
        scv = scb[:].rearrange("p (k i) -> p k i", i=2)
        adacnt = [0]

        def ada_load(l, j, wa):
            sl = adacnt[0] % len(wa)
            adacnt[0] += 1
            T.dma('pool', f'adaw{sl}', out=wa[sl][:],
                  in_=wada_in[l, :, j * 1024:(j + 1) * 1024].rearrange("(k p) n -> p k n", p=128), writes=[('wa', sl)])
            return sl

        def ada_piece(l, j, wa, pbank, sl=None):
            if sl is None:
                sl = ada_load(l, j, wa)

            def mm(e):
                last = None
                for oc in range(8):
                    for k in range(8):
                        last = e.matmul(ps[pbank][:, oc * 2:oc * 2 + 2], wa[sl][:, k, oc * 128:(oc + 1) * 128], scv[:, k, :],
                                        start=(k == 0), stop=(k == 7))
                return last
            T.op('pe', mm, reads=[('wa', sl), 'sc'], writes=[('ps', pbank)])
            b0 = PP[f'bada{l}'] + j * 16
            T.op('dve', lambda e: e.tensor_tensor(out=adatmp[:], in0=ps[pbank][:, 0:16], in1=ppt[:, b0:b0 + 16], op=ALU.add),
                 reads=[('ps', pbank), 'ppt'], writes=['adatmp'])
            dst = {0: 1, 1: 0, 2: 2, 3: 4, 4: 3, 5: 5}[j]
            tv = adatmp[:].rearrange("p (k i) -> p k i", i=2)
            if j in (1, 4):
                nname = f'nmix{l}' if j == 1 else f'nffn{l}'
                T.op('dve', lambda e: e.tensor_scalar(out=adatmp[:], in0=adatmp[:], scalar1=1.0, scalar2=None, op0=ALU.add), reads=['adatmp'], writes=['adatmp'])
                T.op('dve', lambda e: e.tensor_tensor(out=modt[:, l, dst], in0=tv, in1=ppv(nname, 16).rearrange("p (k i) -> p k i", i=2), op=ALU.mult),
                     reads=['adatmp', 'ppt'], writes=[('modt', l, dst)])
            else:
                T.op('dve', lambda e: e.tensor_copy(out=modt[:, l, dst], in_=tv), reads=['adatmp'], writes=[('modt', l, dst)])

        def phase_ada():
            with contextlib.ExitStack() as ph:
                wa = [sbuf(ph, f"ada_w{i}", [128, 8, 1024], BF16) for i in range(2)]
                ada_piece(0, 0, wa, 0)
                ada_piece(0, 1, wa, 1)
            T.barrier()

        def norm_block(xt, xkey, sq, rs, tmpf, hT, hkp, n, l, which, isctx, part='all'):
            if part in ('all', 'a'):
                T.op('act', lambda e: e.activation(out=sq[:, :, 0:n], in_=xt[:, :, 0:n], func=AF.Square),
                     reads=[xkey], writes=['nb_sq'])
            if part == 'a':
                return

            def mm(e):
                last = None
                for k in range(8):
                    last = e.matmul(ps[6][:, 0:n], ones_bf[:], sq[:, k, 0:n], start=(k == 0), stop=(k == 7))
                return last
            T.op('pe', mm, reads=['nb_sq', 'ones_bf'], writes=[('ps', 6)])
            T.op('dve', lambda e: e.tensor_scalar(out=rs[:, 0:n], in0=ps[6][:, 0:n], scalar1=1.0 / D, scalar2=EPS, op0=ALU.mult, op1=ALU.add),
                 reads=[('ps', 6)], writes=['nb_rs'])
            T.op('act', lambda e: e.activation(out=rs[:, 0:n], in_=rs[:, 0:n], func=AF.Sqrt), reads=['nb_rs'], writes=['nb_rs'])
            T.op('dve', lambda e: e.reciprocal(out=rs[:, 0:n], in_=rs[:, 0:n]), reads=['nb_rs'], writes=['nb_rs'])
            for k in range(8):
                T.op('dve', lambda e, k=k: e.scalar_tensor_tensor(out=tmpf[:, k % 2, 0:n], in0=xt[:, k, 0:n], scalar=modt[:, l, which, k, isctx:isctx + 1],
                                                                  in1=rs[:, 0:n], op0=ALU.mult, op1=ALU.mult),
                     reads=[xkey, 'nb_rs'], writes=[('nb_tmpf', k % 2)])
                T.op('act', lambda e, k=k: e.activation(out=hT[:, k, 0:n], in_=tmpf[:, k % 2, 0:n], func=AF.Identity,
                                                        bias=modt[:, l, which + 1, k, isctx:isctx + 1], scale=1.0),
                     reads=[('nb_tmpf', k % 2)], writes=[(hkp, 'hT', k)])

        def phase_B(l):
            with contextlib.ExitStack() as ph:
                w = sbuf(ph, "B_w", [128, 8, NX], BF16)
                xts = [sbuf(ph, f"B_xt{i}", [128, 8, 512], F32) for i in range(2)]
                hTs = [sbuf(ph, f"B_hT{i}", [128, 8, 512], BF16) for i in range(2)]
                sq = sbuf(ph, "B_sq", [128, 8, 512], BF16)
                rs = sbuf(ph, "B_rs", [128, 512], F32)
                tmpf = sbuf(ph, "B_tmpf", [128, 2, 512], F32)
                css = [sbuf(ph, f"B_cs{i}", [128, 2, 512], F32) for i in range(2)]
                r1 = sbuf(ph, "B_r1", [128, 2, 512], F32)
                r2 = sbuf(ph, "B_r2", [128, 2, 512], F32)
                fmA = sbuf(ph, "B_fmA", [128, 5, 512], BF16)
                fmC = sbuf(ph, "B_fmC", [128, 8, 512], BF16)
                fmB = sbuf(ph, "B_fmB", [128, 8, 512], BF16)
                fmZ = sbuf(ph, "B_fmZ", [64, 512], F32)
                tmA = sbuf(ph, "B_tmA", [128, 4, 2, 128], BF16)
                tmB = sbuf(ph, "B_tmB", [128, 4, 512], BF16)
                tmC = sbuf(ph, "B_tmC", [128, 4, 8, 128], BF16)
                worder = ([('Aq', c) for c in range(4)] + [('Aqs', c) for c in range(4)] + [('Ak', 0), ('Aks', 0)] + [('Cq', c) for c in range(4)]
                          + [('Ck', c) for c in range(4)] + [('Bq', 0), ('Bq', 1)] + [('Br', c) for c in range(4)] + [('Bk', 0), ('Bk', 1), ('Bz', 0)])
                worder = ([x for c in range(4) for x in (('Aq', c), ('Aqs', c))] + worder[8:])
                for wi_, (grp, c) in enumerate(worder):
                    wd = 64 if grp == 'Bz' else 128
                    c0 = COL[grp] + c * 128
                    T.dma('pool', f'wB{wi_ % 12}', out=w[:, :, c0:c0 + wd], in_=winx_in[l, :, c0:c0 + wd].rearrange("(k p) n -> p k n", p=128), writes=[('w', grp, c)])
                for wi_, (grp, wd) in enumerate(TM_GROUPS):
                    c0 = COL[grp]
                    T.dma('pool', f'wBt{wi_}', out=w[:, :, c0:c0 + wd], in_=winx_in[l, :, c0:c0 + wd].rearrange("(k p) n -> p k n", p=128), writes=[('w', grp, 0)])
                T.op('pool', lambda e: e.memset(tmA[:], 1.0), writes=[('tmA', j) for j in range(4)])
                T.op('pool', lambda e: e.memset(tmC[:], 1.0), writes=[('tmC', j) for j in range(4)])
                psrot = [0]

                def nextps():
                    i = psrot[0] % 6
                    psrot[0] += 1
                    return i

                def blk_loads(bi):
                    t0, n, kind, isctx = BLOCKS[l][bi]
                    sl = bi % 2
                    key = ('blk', sl)
                    if l == 0:
                        src = (ctxT_in if isctx else xT_in[:, t0:t0 + n])
                    else:
                        src = S['xs'][:, t0:t0 + n]
                    T.dma('sp', f'Bx{sl}', out=xts[sl][:, :, 0:n], in_=src.rearrange("(k p) t -> p k t", p=128), writes=[(key, 'xt')])
                    T.dma('sp', f'Bcs{sl}', out=css[sl][:, :, 0:n], in_=cs_in[:, :, t0:t0 + n], writes=[('cs', sl)])

                def blk_norm(bi, part):
                    t0, n, kind, isctx = BLOCKS[l][bi]
                    sl = bi % 2
                    key = ('blk', sl)
                    norm_block(xts[sl], (key, 'xt'), sq, rs, tmpf, hTs[sl], key, n, l, 0, isctx, part=part)
                    if part != 'a' and kind == 'F':
                        T.dma('sp', f'BhT{sl}', out=S['hT'][:, t0:t0 + n].rearrange("(k p) t -> p k t", p=128), in_=hTs[sl][:, :, 0:n],
                              reads=[(key, 'hT', k) for k in range(8)])

                blk_loads(0)
                blk_norm(0, 'all')
                for bi, (t0, n, kind, isctx) in enumerate(BLOCKS[l]):
                    sl = bi % 2
                    xt, hT = xts[sl], hTs[sl]
                    cs = css[sl]
                    key = ('blk', sl)
                    hkeys = [(key, 'hT', k) for k in range(8)]
                    if bi + 1 < len(BLOCKS[l]):
                        blk_loads(bi + 1)

                    def proj_fm(grp, c, width=128):
                        pi = nextps()
                        c0 = COL[grp] + c * 128

                        def mm(e):
                            last = None
                            for k in range(8):
                                last = e.matmul(ps[pi][0:width, 0:n], w[:, k, c0:c0 + width], hT[:, k, 0:n], start=(k == 0), stop=(k == 7))
                            return last
                        T.op('pe', mm, reads=[('w', grp, c)] + hkeys, writes=[('ps', pi)])
                        return pi

                    def rope_chunk(grp, grps, c, dst, dkey):
                        p1 = proj_fm(grp, c)
                        p2 = proj_fm(grps, c)
                        rr = (c % 2)
                        T.op('dve', lambda e: e.tensor_tensor(out=r1[:, rr, 0:n], in0=ps[p1][:, 0:n], in1=cs[:, 0, 0:n], op=ALU.mult),
                             reads=[('ps', p1), ('cs', sl)], writes=[('r1', rr)])
                        T.op('dve', lambda e: e.tensor_tensor(out=r2[:, rr, 0:n], in0=ps[p2][:, 0:n], in1=cs[:, 1, 0:n], op=ALU.mult),
                             reads=[('ps', p2), ('cs', sl)], writes=[('r2', rr)])
                        T.op('pool', lambda e: e.tensor_tensor(out=dst, in0=r1[:, rr, 0:n], in1=r2[:, rr, 0:n], op=ALU.add),
                             reads=[('r1', rr), ('r2', rr)], writes=[dkey])

                    evr = [0]

                    def evac(pi, dst, dkey, scale=None, func=None, width=128):
                        src_ = ps[pi][0:width, 0:n]
                        if func is not None:
                            T.op('act', lambda e: e.activation(out=dst, in_=src_, func=func), reads=[('ps', pi)], writes=[dkey])
                        elif scale is not None:
                            T.op('act', lambda e: e.activation(out=dst, in_=src_, func=AF.Copy, scale=scale), reads=[('ps', pi)], writes=[dkey])
                        else:
                            evr[0] += 1
                            if evr[0] % 2:
                                T.op('act', lambda e: e.activation(out=dst, in_=src_, func=AF.Copy), reads=[('ps', pi)], writes=[dkey])
                            else:
                                T.op('dve', lambda e: e.tensor_copy(out=dst, in_=src_), reads=[('ps', pi)], writes=[dkey])

                    def store_fm(name, tile_ap, nchunks, keys, rows=128):
                        T.dma('sp', f'st_{name}', out=S[name][:, t0:t0 + n].rearrange("(c p) t -> p c t", p=rows) if nchunks > 1 else S[name][:, t0:t0 + n],
                              in_=tile_ap, reads=keys)

                    if kind == 'F':
                        for c in range(4):
                            rope_chunk('Aq', 'Aqs', c, fmA[:, c, 0:n], ('fmA', c))
                        store_fm('Aq', fmA[:, 0:4, 0:n], 4, [('fmA', c) for c in range(4)])
                    rope_chunk('Ak', 'Aks', 0, fmA[:, 4, 0:n], ('fmA', 4))
                    store_fm('Ak', fmA[:, 4, 0:n], 1, [('fmA', 4)])
                    if kind == 'F':
                        for c in range(4):
                            evac(proj_fm('Cq', c), fmC[:, c, 0:n], ('fmC', c))
                        store_fm('Cq', fmC[:, 0:4, 0:n], 4, [('fmC', c) for c in range(4)])
                    for c in range(4):
                        evac(proj_fm('Ck', c), fmC[:, 4 + c, 0:n], ('fmC', 4 + c))
                    store_fm('Ck', fmC[:, 4:8, 0:n], 4, [('fmC', 4 + c) for c in range(4)])
                    if bi + 1 < len(BLOCKS[l]):
                        blk_norm(bi + 1, 'a')
                    if kind == 'F':
                        for c in range(2):
                            evac(proj_fm('Bq', c), fmB[:, c, 0:n], ('fmB', c), scale=0.125)
                        store_fm('Bq', fmB[:, 0:2, 0:n], 2, [('fmB', c) for c in range(2)])
                        for c in range(4):
                            evac(proj_fm('Br', c), fmB[:, 4 + c, 0:n], ('fmB', 4 + c), func=AF.Silu)
                        store_fm('Br', fmB[:, 4:8, 0:n], 4, [('fmB', 4 + c) for c in range(4)])
                    for c in range(2):
                        evac(proj_fm('Bk', c), fmB[:, 2 + c, 0:n], ('fmB', 2 + c))
                    store_fm('Bk', fmB[:, 2:4, 0:n], 2, [('fmB', 2 + c) for c in range(2)])
                    pz = proj_fm('Bz', 0, width=64)
                    T.op('dve', lambda e: e.tensor_copy(out=fmZ[:, 0:n], in_=ps[pz][0:64, 0:n]), reads=[('ps', pz)], writes=['fmZ'])
                    T.dma('sp', 'st_Bz', out=S['Bz'][:, t0:t0 + n], in_=fmZ[:, 0:n], reads=['fmZ'])
                    if bi + 1 < len(BLOCKS[l]):
                        blk_norm(bi + 1, 'b')
                    nj = n // 128
                    for j in range(nj):
                        def proj_tm(grp, ncols, j=j):
                            pi = nextps()
                            c0 = COL[grp]

                            def mm(e):
                                last = None
                                for k in range(8):
                                    last = e.matmul(ps[pi][:, 0:ncols], hT[:, k, j * 128:(j + 1) * 128], w[:, k, c0:c0 + ncols], start=(k == 0), stop=(k == 7))
                                return last
                            T.op('pe', mm, reads=[('w', grp, 0)] + hkeys, writes=[('ps', pi)])
                            return pi
                        pa = proj_tm('Av', 128)
                        T.op('dve', lambda e, pa=pa, j=j: e.tensor_copy(out=tmA[:, j, :, 0:64], in_=ps[pa][:, 0:128].rearrange("p (g d) -> p g d", g=2)),
                             reads=[('ps', pa)], writes=[('tmA', j)])
                        pb = proj_tm('Bv', 512)
                        T.op('act', lambda e, pb=pb, j=j: e.activation(out=tmB[:, j, :], in_=ps[pb][:, 0:512], func=AF.Copy),
                             reads=[('ps', pb)], writes=[('tmB', j)])
                        pc = proj_tm('Cv', 512)
                        T.op('dve', lambda e, pc=pc, j=j: e.tensor_copy(out=tmC[:, j, :, 0:64], in_=ps[pc][:, 0:512].rearrange("p (g d) -> p g d", g=8)),
                             reads=[('ps', pc)], writes=[('tmC', j)])
                    T.dma('sp', 'st_Av', out=S['Av'][t0:t0 + n, :].rearrange("(j p) c -> p j c", p=128),
                          in_=tmA[:, 0:nj].rearrange("p j g d -> p j (g d)"), reads=[('tmA', j) for j in range(nj)])
                    T.dma('sp', 'st_Bv', out=S['Bv'][t0:t0 + n, :].rearrange("(j p) c -> p j c", p=128),
                          in_=tmB[:, 0:nj, :], reads=[('tmB', j) for j in range(nj)])
                    T.dma('sp', 'st_Cv', out=S['Cv'][t0:t0 + n, :].rearrange("(j p) c -> p j c", p=128),
                          in_=tmC[:, 0:nj].rearrange("p j g d -> p j (g d)"), reads=[('tmC', j) for j in range(nj)])
            T.barrier()


        def phase_G(l):
            with contextlib.ExitStack() as ph:
                qT = sbuf(ph, "G_q", [128, 2, TT], BF16)
                kT = sbuf(ph, "G_k", [128, 2, TT], BF16)
                v = sbuf(ph, "G_v", [128, TT // 128, 512], BF16)
                zT = sbuf(ph, "G_z", [64, TT], F32)
                oT = sbuf(ph, "G_o", [128, 4, TO], F32)
                gw = sbuf(ph, "G_gw", [64, 256], F32)
                mrep = sbuf(ph, "G_mrep", [128, 2, 4, 128], F32)
                NB = 3
                Sst = [sbuf(ph, f"G_S{i}", [128, 2, 128], F32) for i in range(2)]
                Sbf = [sbuf(ph, f"G_Sbf{i}", [128, 2, 128], BF16) for i in range(2)]
                e1 = [sbuf(ph, f"G_e1{i}", [128, 256], F32) for i in range(NB)]
                gpos = [sbuf(ph, f"G_gp{i}", [128, 256], BF16) for i in range(NB)]
                eb = [sbuf(ph, f"G_eb{i}", [128, 2, 128], F32) for i in range(NB)]
                enb = [sbuf(ph, f"G_enb{i}", [128, 2, 128], F32) for i in range(NB)]
                qt = [sbuf(ph, f"G_qt{i}", [128, 2, 128], BF16) for i in range(NB)]
                kt = [sbuf(ph, f"G_kt{i}", [128, 2, 128], BF16) for i in range(NB)]
                ktl = [sbuf(ph, f"G_ktl{i}", [128, 2, 128], BF16) for i in range(NB)]
                ktT = [sbuf(ph, f"G_ktT{i}", [128, 2, 128], BF16) for i in range(NB)]
                Am = [sbuf(ph, f"G_Am{i}", [128, 2, 2, 128], BF16) for i in range(NB)]
                T.dma('sp', 'Gq', out=qT[:], in_=S['Bq'].rearrange("(c p) t -> p c t", p=128), writes=['qT'])
                T.dma('sp', 'Gk', out=kT[:], in_=S['Bk'].rearrange("(c p) t -> p c t", p=128), writes=['kT'])
                T.dma('sp', 'Gv', out=v[:], in_=S['Bv'].rearrange("(j p) c -> p j c", p=128), writes=['v'])
                T.dma('sp', 'Gz', out=zT[:], in_=S['Bz'], writes=['zT'])
                T.dma('sp', 'Ggw', out=gw[:], in_=gwp_in[:, l, :], writes=['gw'])
                T.dma('sp', 'Gz', out=zT[16:17, :], in_=onesrow_in, writes=['zT'])
                T.dma('sp', 'Gz', out=zT[48:49, :], in_=onesrow_in, writes=['zT'])
                for d_ in range(2):
                    for h in range(4):
                        T.op('pool', lambda e, d_=d_, h=h: e.tensor_copy(out=mrep[:, d_, h, :], in_=cst[:, d_, :]), reads=['cst'], writes=['mrep'])

                def prep(idx, item):
                    if item[0] != 'chunk':
                        return
                    _, tok0, d_, need_out, ocol = item
                    cb = idx % NB
                    pz = idx % 2
                    rows = slice(0, 17) if d_ == 0 else slice(32, 49)
                    last = 127 if d_ == 0 else 0
                    bz = ps[pz]
                    kz, kcs = ('psz', pz), ('pscs', pz)
                    T.op('pe', lambda e: e.matmul(bz[:, 0:256], zT[rows, tok0:tok0 + 128], gw[rows, :], start=True, stop=True), reads=['zT', 'gw'], writes=[kz])
                    T.op('act', lambda e: e.activation(out=e1[cb][:], in_=bz[:, 0:256], func=AF.Exp, scale=-1.0), reads=[kz], writes=[('e1', cb)])
                    T.op('act', lambda e: e.activation(out=gpos[cb][:], in_=e1[cb][:], func=AF.Ln, bias=1.0, scale=1.0), reads=[('e1', cb)], writes=[('gpos', cb)])

                    def mmcs(e):
                        e.matmul(bz[:, 256:384], gpos[cb][:, 0:128], cstb[:, d_, :], start=True, stop=True)
                        return e.matmul(bz[:, 384:512], gpos[cb][:, 128:256], cstb[:, d_, :], start=True, stop=True)
                    T.op('pe', mmcs, reads=[('gpos', cb), 'cstb'], writes=[kcs])
                    csv = bz[:, 256:512].rearrange("p (a t) -> p a t", a=2)
                    T.op('act', lambda e: e.activation(out=eb[cb][:], in_=csv, func=AF.Exp, scale=-1.0 / 16), reads=[kcs], writes=[('eb', cb)])
                    T.op('act', lambda e: e.activation(out=enb[cb][:], in_=csv, func=AF.Exp, scale=1.0 / 16), reads=[kcs], writes=[('enb', cb)])
                    if need_out:
                        T.op('dve', lambda e: e.tensor_tensor(out=qt[cb][:], in0=qT[:, :, tok0:tok0 + 128], in1=eb[cb][:], op=ALU.mult),
                             reads=['qT', ('eb', cb)], writes=[('qt', cb)])
                    T.op('dve', lambda e: e.tensor_tensor(out=kt[cb][:], in0=kT[:, :, tok0:tok0 + 128], in1=enb[cb][:], op=ALU.mult),
                         reads=['kT', ('enb', cb)], writes=[('kt', cb)])
                    for p in range(2):
                        T.op('dve', lambda e, p=p: e.tensor_scalar(out=ktl[cb][:, p, :], in0=kt[cb][:, p, :], scalar1=eb[cb][:, p, last:last + 1], scalar2=None, op0=ALU.mult),
                             reads=[('kt', cb), ('eb', cb)], writes=[('ktl', cb, p)])
                    tb = (idx % 4) * 256

                    def mmtr(e):
                        e.transpose(psb[:, tb:tb + 128], ktl[cb][:, 0, :], cstb[:, 2, :])
                        return e.transpose(psb[:, tb + 128:tb + 256], ktl[cb][:, 1, :], cstb[:, 2, :])
                    T.op('pe', mmtr, reads=[('ktl', cb, 0), ('ktl', cb, 1), 'cstb'], writes=[('psb', idx % 4)])
                    T.op('act', lambda e: e.activation(out=ktT[cb][:].rearrange("p a t -> p (a t)"), in_=psb[:, tb:tb + 256], func=AF.Copy),
                         reads=[('psb', idx % 4)], writes=[('ktT', cb)])
                    if need_out:
                        def mmA(e):
                            last_ = None
                            for h in (0, 2, 1, 3):
                                p, r = h // 2, slice((h % 2) * 64, (h % 2) * 64 + 64)
                                last_ = e.matmul(ps[2 + h % 2][:, p * 128:(p + 1) * 128], kt[cb][r, p, :], qt[cb][r, p, :], start=True, stop=True)
                            return last_
                        T.op('pe', mmA, reads=[('kt', cb), ('qt', cb)], writes=[('ps', 2), ('ps', 3)])
                        for par in range(2):
                            T.op('dve', lambda e, par=par: e.tensor_tensor(out=Am[cb][:, par], in0=ps[2 + par][:, 0:256].rearrange("p (h t) -> p h t", h=2), in1=mrep[:, d_, 0:2, :], op=ALU.mult),
                                 reads=[('ps', 2 + par), 'mrep'], writes=[('Am', cb, par)])

                written = set()

                def zero_state(d_):
                    T.op('dve', lambda e: e.memset(Sst[d_][:], 0.0), writes=[('S', d_, h) for h in range(4)])
                    T.op('dve', lambda e: e.memset(Sbf[d_][:], 0.0), writes=[('Sbf', d_)])

                def fin(idx, item):
                    if item[0] == 'reset':
                        zero_state(item[1])
                        return
                    _, tok0, d_, need_out, ocol = item
                    cb = idx % NB
                    last = 127 if d_ == 0 else 0
                    cj = tok0 // 128
                    if need_out:
                        pO = ps[4 + idx % 2]

                        def mmO(e):
                            last_ = None
                            for h in range(4):
                                p, r = h // 2, slice((h % 2) * 64, (h % 2) * 64 + 64)
                                e.matmul(pO[:, h * 128:(h + 1) * 128], v[:, cj, h * 128:(h + 1) * 128], Am[cb][:, h % 2, h // 2, :], start=True, stop=False)
                                last_ = e.matmul(pO[:, h * 128:(h + 1) * 128], Sbf[d_][r, p, :], qt[cb][r, p, :], start=False, stop=True)
                            return last_
                        T.op('pe', mmO, reads=['v', ('Am', cb, 0), ('Am', cb, 1), ('Sbf', d_), ('qt', cb)], writes=[('ps', 4 + idx % 2)])
                        pOv = pO[:, 0:512].rearrange("p (h t) -> p h t", h=4)
                        okey = ('oT', ocol)
                        if ocol not in written:
                            written.add(ocol)
                            T.op('act', lambda e: e.activation(out=oT[:, :, ocol:ocol + 128], in_=pOv, func=AF.Copy), reads=[('ps', 4 + idx % 2)], writes=[okey])
                        else:
                            T.op('dve', lambda e: e.tensor_tensor(out=oT[:, :, ocol:ocol + 128], in0=pOv, in1=oT[:, :, ocol:ocol + 128], op=ALU.add),
                                 reads=[('ps', 4 + idx % 2), okey], writes=[okey])
                    pU = ps[6]

                    def mmU(e):
                        last_ = None
                        for h in range(4):
                            last_ = e.matmul(pU[:, h * 128:(h + 1) * 128], ktT[cb][:, h // 2, :], v[:, cj, h * 128:(h + 1) * 128], start=True, stop=True)
                        return last_
                    T.op('pe', mmU, reads=[('ktT', cb), 'v'], writes=[('ps', 6)])
                    for h in range(4):
                        p, r = h // 2, slice((h % 2) * 64, (h % 2) * 64 + 64)
                        T.op('dve', lambda e, h=h, p=p, r=r: e.scalar_tensor_tensor(out=Sst[d_][r, p, :], in0=Sst[d_][r, p, :], scalar=eb[cb][r, p, last:last + 1],
                                                                                   in1=pU[r, h * 128:(h + 1) * 128], op0=ALU.mult, op1=ALU.add),
                             reads=[('ps', 6), ('eb', cb), ('S', d_, h)], writes=[('S', d_, h)])
                    T.op('act', lambda e: e.activation(out=Sbf[d_][:], in_=Sst[d_][:], func=AF.Copy), reads=[('S', d_, h) for h in range(4)], writes=[('Sbf', d_)])

                n1 = TFULL[l] // 128
                ng = TGLA[l] // 128
                L1 = [('chunk', TL + j * 128, 0, l == 0, 2560 + j * 128) for j in range(2)]
                L1 += [('chunk', j * 128, 0, True, j * 128) for j in range(n1)]
                L2 = [('chunk', j * 128, 1, j < n1, j * 128) for j in range(ng - 1, -1, -1)]
                if l == 0:
                    L2 += [('reset', 1)] + [('chunk', TL + j * 128, 1, True, 2560 + j * 128) for j in (1, 0)]
                seq = []
                for i in range(max(len(L1), len(L2))):
                    if i < len(L2):
                        seq.append(L2[i])
                    if i < len(L1):
                        seq.append(L1[i])
                zero_state(0)
                zero_state(1)
                LA = 2
                for idx in range(len(seq) + LA):
                    if idx < len(seq):
                        prep(idx, seq[idx])
                    if idx - LA >= 0:
                        fin(idx - LA, seq[idx - LA])
                if dbg:
                    T.dma('sp', 'dbg', out=S['oT'].rearrange("(h p) t -> p h t", p=128), in_=oT[:], reads=[('oT', c_) for c_ in range(0, TO, 128)])
                with contextlib.ExitStack() as ph2:
                    br = [sbuf(ph2, f"G_br{i}", [128, 4, 512], BF16) for i in range(2)]
                    sq = [sbuf(ph2, f"G_sq{i}", [128, 512], BF16) for i in range(2)]
                    rs = [sbuf(ph2, f"G_rs{i}", [128, 512], F32) for i in range(2)]
                    yt = [sbuf(ph2, f"G_yt{i}", [128, 512], F32) for i in range(2)]
                    yst = [sbuf(ph2, f"G_yst{i}", [128, 4, 512], BF16) for i in range(2)]
                    blocks = [(t0, 512, t0) for t0 in range(0, TFULL[l], 512)]
                    if l == 0:
                        blocks.append((TL, 256, 2560))
                    it = 0
                    for bi, (t0, n, oc0) in enumerate(blocks):
                        sl = bi % 2
                        T.dma('sp', f'Gbr{sl}', out=br[sl][:, :, 0:n], in_=S['Br'][:, t0:t0 + n].rearrange("(h p) t -> p h t", p=128), writes=[('br', sl)])
                        okeys = [('oT', c_) for c_ in range(oc0, oc0 + n, 128)]
                        for h in range(4):
                            a = it % 2
                            it += 1
                            pi = 2 + a
                            T.op('act', lambda e, a=a, h=h: e.activation(out=sq[a][:, 0:n], in_=oT[:, h, oc0:oc0 + n], func=AF.Square), reads=okeys, writes=[('gsq', a)])
                            T.op('pe', lambda e, a=a, pi=pi: e.matmul(ps[pi][:, 0:n], ones_bf[:], sq[a][:, 0:n], start=True, stop=True), reads=[('gsq', a), 'ones_bf'], writes=[('ps', pi)])
                            T.op('act', lambda e, a=a, pi=pi: e.activation(out=rs[a][:, 0:n], in_=ps[pi][:, 0:n], func=AF.Sqrt, bias=EPS, scale=1.0 / 128),
                                 reads=[('ps', pi)], writes=[('grs', a)])
                            T.op('dve', lambda e, a=a: e.reciprocal(out=rs[a][:, 0:n], in_=rs[a][:, 0:n]), reads=[('grs', a)], writes=[('grs', a)])
                            T.op('dve', lambda e, a=a, h=h: e.tensor_tensor(out=yt[a][:, 0:n], in0=oT[:, h, oc0:oc0 + n], in1=rs[a][:, 0:n], op=ALU.mult),
                                 reads=okeys + [('grs', a)], writes=[('gyt', a)])
                            T.op('dve', lambda e, a=a, h=h: e.scalar_tensor_tensor(out=yst[sl][:, h, 0:n], in0=yt[a][:, 0:n], scalar=ppv(f'gnorm{l}', 1),
                                                                                   in1=br[sl][:, h, 0:n], op0=ALU.mult, op1=ALU.mult),
                                 reads=[('gyt', a), ('br', sl), 'ppt'], writes=[('yst', sl, h)])
                        T.dma('sp', f'Gyb{sl}', out=S['yb'][:, t0:t0 + n].rearrange("(h p) t -> p h t", p=128), in_=yst[sl][:, :, 0:n],
                              reads=[('yst', sl, h) for h in range(4)])
            T.barrier()

        def phase_A(l):
            with contextlib.ExitStack() as ph:
                qT = sbuf(ph, "A_q", [128, 4, TT], BF16)
                kT = sbuf(ph, "A_k", [128, TT], BF16)
                v = sbuf(ph, "A_v", [128, TT // 128, 256], BF16)
                esk = sbuf(ph, "A_esk", [128, 8], F32)
                mrepb = sbuf(ph, "A_mrep", [128, 2, 4, 128], BF16)
                P = [sbuf(ph, f"A_P{i}", [128, 5, 512], BF16) for i in range(2)]
                den = [sbuf(ph, f"A_den{i}", [128, 512], F32) for i in range(2)]
                yst = [sbuf(ph, f"A_yst{i}", [64, 4, 128], BF16) for i in range(2)]
                ada_todo = []
                ada_pending = None
                if l == 0:
                    wa = [sbuf(ph, f"A_adaw{i}", [128, 8, 1024], BF16) for i in range(2)]
                    ada_todo = [(0, j) for j in range(2, 6)] + [(1, j) for j in range(6)]
                T.dma('sp', 'Aq', out=qT[:], in_=S['Aq'].rearrange("(c p) t -> p c t", p=128), writes=['qT'])
                T.dma('sp', 'Ak', out=kT[:], in_=S['Ak'], writes=['kT'])
                T.dma('sp', 'Av', out=v[:], in_=S['Av'].rearrange("(j p) c -> p j c", p=128), writes=['v'])
                T.op('act', lambda e: e.activation(out=esk[:], in_=ppv(f'sink{l}', 8), func=AF.Exp), reads=['ppt'], writes=['esk'])
                for d_ in range(2):
                    for h in range(4):
                        T.op('pool', lambda e, d_=d_, h=h: e.tensor_copy(out=mrepb[:, d_, h, :], in_=cstb[:, d_, :]), reads=['cstb'], writes=['mrepb'])
                qtiles = [(n, False) for n in range(TFULL[l] // 128)]
                if l == 0:
                    qtiles += [(24, True), (25, True)]
                units = []
                for (n, isctx) in qtiles:
                    if isctx:
                        klist = [(24, None), (25, None)]
                    else:
                        klist = ([(n - 1, 1)] if n > 0 else []) + [(n, None), (n + 1, 0), (24, None), (25, None)]
                    for g in range(2):
                        units.append((n, klist, g))
                sbc = [0]
                adast = [ada_pending]

                def emit_S(ui):
                    n, klist, g = units[ui]
                    sl = ui % 2
                    if l == 0 and ui % 4 == 0:
                        if adast[0] is not None:
                            ada_piece(*adast[0][0], wa, 5, sl=adast[0][1])
                            adast[0] = None
                        if ada_todo:
                            lj = ada_todo.pop(0)
                            adast[0] = (lj, ada_load(*lj, wa))
                    r = slice(g * 64, g * 64 + 64)
                    for i, (kt_, m) in enumerate(klist):
                        pi = sbc[0] % 3
                        sbc[0] += 1
                        T.op('pe', lambda e, pi=pi, kt_=kt_: e.matmul(ps[pi][:, 0:512].rearrange("p (j t) -> p j t", j=4), kT[r, kt_ * 128:(kt_ + 1) * 128],
                                                                     qT[r, :, n * 128:(n + 1) * 128], start=True, stop=True),
                             reads=['kT', 'qT'], writes=[('ps', pi)])
                        T.op('act', lambda e, pi=pi, i=i: e.activation(out=P[sl][:, i, :], in_=ps[pi][:, 0:512], func=AF.Exp, scale=0.125),
                             reads=[('ps', pi)], writes=[('P', sl, i)])
                        if m is not None:
                            T.op('pool', lambda e, i=i, m=m: e.tensor_tensor(out=P[sl][:, i, :].rearrange("p (j t) -> p j t", j=4),
                                                                            in0=P[sl][:, i, :].rearrange("p (j t) -> p j t", j=4), in1=mrepb[:, m], op=ALU.mult),
                                 reads=[('P', sl, i), 'mrepb'], writes=[('P', sl, i)])

                def emit_O(ui):
                    n, klist, g = units[ui]
                    sl = ui % 2
                    po = 3 + sl

                    def mmO(e):
                        last_ = None
                        for i, (kt_, m) in enumerate(klist):
                            last_ = e.matmul(ps[po][:, 0:512], v[:, kt_, g * 128:(g + 1) * 128], P[sl][:, i, :], start=(i == 0), stop=(i == len(klist) - 1))
                        return last_
                    T.op('pe', mmO, reads=['v'] + [('P', sl, i) for i in range(len(klist))], writes=[('ps', po)])
                    for j in range(4):
                        T.op('dve', lambda e, j=j: e.tensor_scalar(out=den[sl][64:128, j * 128:(j + 1) * 128], in0=ps[po][64:128, j * 128:(j + 1) * 128],
                                                                  scalar1=esk[64:128, 4 * g + j:4 * g + j + 1], scalar2=None, op0=ALU.add),
                             reads=[('ps', po), 'esk'], writes=[('den', sl, j)])
                    T.op('dve', lambda e: e.reciprocal(out=den[sl][64:128, :], in_=den[sl][64:128, :]), reads=[('den', sl, j) for j in range(4)], writes=[('den', sl)])
                    T.op('dve', lambda e: e.tensor_tensor(out=yst[sl][0:64, :, :], in0=ps[po][0:64, 0:512].rearrange("p (j t) -> p j t", j=4),
                                                          in1=den[sl][64:128, :].rearrange("p (j t) -> p j t", j=4), op=ALU.mult),
                         reads=[('ps', po), ('den', sl)], writes=[('yst', sl)])
                    T.dma('sp', f'Ayst{sl}', out=S['ya'][g * 256:(g + 1) * 256, n * 128:(n + 1) * 128].rearrange("(j d) t -> d j t", d=64), in_=yst[sl][:],
                          reads=[('yst', sl)])

                emit_S(0)
                for ui in range(len(units)):
                    if ui + 1 < len(units):
                        emit_S(ui + 1)
                    emit_O(ui)
                ada_pending = adast[0]
                if ada_pending is not None:
                    ada_piece(*ada_pending[0], wa, 5, sl=ada_pending[1])
                assert not ada_todo
                if dbg and l == 0:
                    T.dma('sp', 'dbg', out=S['mod'], in_=modt[:].rearrange("p l a k i -> p (l a k i)"),
                          reads=[('modt', l_, a_) for l_ in range(2) for a_ in range(6)])
            T.barrier()

        def phase_C(l):
            with contextlib.ExitStack() as ph:
                qT = sbuf(ph, "C_q", [128, 4, TT], BF16)
                kT = sbuf(ph, "C_k", [128, 4, TT], BF16)
                v = sbuf(ph, "C_v", [128, TT // 128, 1024], BF16)
                EB = sbuf(ph, "C_EB", [128, NCTAB, 512 * 2], BF16)
                P = [sbuf(ph, f"C_P{i}", [128, 7, 512], BF16) for i in range(2)]
                den = [sbuf(ph, f"C_den{i}", [128, 512], F32) for i in range(2)]
                yst = [sbuf(ph, f"C_yst{i}", [64, 4, 128], BF16) for i in range(2)]
                T.dma('sp', 'Cq', out=qT[:], in_=S['Cq'].rearrange("(c p) t -> p c t", p=128), writes=['qT'])
                T.dma('sp', 'Ck', out=kT[:], in_=S['Ck'].rearrange("(c p) t -> p c t", p=128), writes=['kT'])
                T.dma('sp', 'Cv', out=v[:], in_=S['Cv'].rearrange("(j p) c -> p j c", p=128), writes=['v'])
                T.dma('pool', 'Ctab', out=EB[:].rearrange("p a b -> p (a b)"), in_=ctab_in[l], writes=['EBraw'])
                for a in range(NCTAB):
                    T.op('act', lambda e, a=a: e.activation(out=EB[:, a, :], in_=EB[:, a, :], func=AF.Exp), reads=['EBraw'], writes=[('EB', a)])
                ebkeys = [('EB', a) for a in range(NCTAB)]
                qtiles = [(n, False) for n in range(TFULL[l] // 128)]
                if l == 0:
                    qtiles += [(24, True), (25, True)]
                units = []
                for (n, isctx) in qtiles:
                    if isctx:
                        klist = [(24, None), (25, None)]
                    elif n == 0:
                        klist = [(0, 0), (1, 1), (2, 2), (3, 3), (24, None), (25, None)]
                    elif n == 1:
                        klist = [(0, 4), (1, 5), (2, 6), (3, 7), (24, None), (25, None)]
                    else:
                        klist = [(n - 2 + i, 8 + i) for i in range(5)] + [(24, None), (25, None)]
                    for hq in range(2):
                        units.append((n, klist, hq))
                sbc = [0]
                altc = [0]

                def emit_S(ui):
                    n, klist, hq = units[ui]
                    sl = ui % 2
                    for i, (kt_, ti) in enumerate(klist):
                        pr_ = (sbc[0] % 2) * 2
                        sbc[0] += 1

                        def mmS(e, pr_=pr_, kt_=kt_, hq=hq):
                            last_ = None
                            for j in (0, 2, 1, 3):
                                h = 4 * hq + j
                                r = slice((h % 2) * 64, (h % 2) * 64 + 64)
                                last_ = e.matmul(ps[pr_ + j % 2][:, (j // 2) * 128:(j // 2 + 1) * 128], kT[r, h // 2, kt_ * 128:(kt_ + 1) * 128],
                                                 qT[r, h // 2, n * 128:(n + 1) * 128], start=True, stop=True)
                            return last_
                        T.op('pe', mmS, reads=['kT', 'qT'], writes=[('ps', pr_), ('ps', pr_ + 1)])
                        for par in range(2):
                            T.op('act', lambda e, pr_=pr_, i=i, sl=sl, par=par: e.activation(out=P[sl][:, i, par * 256:(par + 1) * 256], in_=ps[pr_ + par][:, 0:256], func=AF.Exp, scale=0.125),
                                 reads=[('ps', pr_ + par)], writes=[('P', sl, i, par)])
                        if ti is not None:
                            altc[0] += 1
                            eng = 'pool' if altc[0] % 2 else 'dve'
                            T.op(eng, lambda e, i=i, ti=ti, sl=sl, hq=hq: e.tensor_tensor(out=P[sl][:, i, :], in0=P[sl][:, i, :], in1=EB[:, ti, hq * 512:(hq + 1) * 512], op=ALU.mult),
                                 reads=[('P', sl, i, 0), ('P', sl, i, 1)] + ebkeys, writes=[('P', sl, i, 0), ('P', sl, i, 1)])

                def emit_O(ui):
                    n, klist, hq = units[ui]
                    sl = ui % 2
                    po = 4 + sl

                    def mmO(e, klist=klist, sl=sl, po=po, hq=hq):
                        last_ = None
                        for j in range(4):
                            h = 4 * hq + j
                            sj = (j % 2) * 2 + j // 2
                            for i, (kt_, ti) in enumerate(klist):
                                last_ = e.matmul(ps[po][:, j * 128:(j + 1) * 128], v[:, kt_, h * 128:(h + 1) * 128], P[sl][:, i, sj * 128:(sj + 1) * 128],
                                                 start=(i == 0), stop=(i == len(klist) - 1))
                        return last_
                    T.op('pe', mmO, reads=['v'] + [('P', sl, i, par) for i in range(len(klist)) for par in range(2)], writes=[('ps', po)])
                    T.op('dve', lambda e, sl=sl, po=po: e.reciprocal(out=den[sl][64:128, :], in_=ps[po][64:128, 0:512]), reads=[('ps', po)], writes=[('den', sl)])
                    T.op('dve', lambda e, sl=sl, po=po: e.tensor_tensor(out=yst[sl][0:64, :, :], in0=ps[po][0:64, 0:512].rearrange("p (j t) -> p j t", j=4),
                                                                        in1=den[sl][64:128, :].rearrange("p (j t) -> p j t", j=4), op=ALU.mult),
                         reads=[('ps', po), ('den', sl)], writes=[('yst', sl)])
                    T.dma('sp', f'Cyst{sl}', out=S['yc'][hq * 256:(hq + 1) * 256, n * 128:(n + 1) * 128].rearrange("(j d) t -> d j t", d=64), in_=yst[sl][:],
                          reads=[('yst', sl)])
                emit_S(0)
                for ui in range(len(units)):
                    if ui + 1 < len(units):
                        emit_S(ui + 1)
                    emit_O(ui)
            T.barrier()

        def phase_M(l):
            with contextlib.ExitStack() as ph:
                wm = sbuf(ph, "M_wm", [128, 8, 3072], BF16)
                wb = sbuf(ph, "M_wb", [128, 3, 4, 1024], BF16)
                wo = sbuf(ph, "M_wo", [128, 8, 1024], BF16)
                hTs = [sbuf(ph, f"M_hT{i}", [128, 8, 512], BF16) for i in range(2)]
                ys = [sbuf(ph, f"M_y{i}", [128, 3, 4, 512], BF16) for i in range(2)]
                xts = [sbuf(ph, f"M_xt{i}", [128, 8, 512], F32) for i in range(2)]
                mix = sbuf(ph, "M_mix", [128, 8, 512], BF16)
                gsb = sbuf(ph, "M_gsb", [128, 3, 512], F32)
                mt = sbuf(ph, "M_mt", [128, 3, 512], F32)
                sq = sbuf(ph, "M_sq", [128, 8, 512], BF16)
                rs = sbuf(ph, "M_rs", [128, 512], F32)
                tmpf = sbuf(ph, "M_tmpf", [128, 2, 512], F32)
                hf = sbuf(ph, "M_hf", [128, 8, 512], BF16)
                wsrcs = (wba_in, wbb_in, wbc_in)
                di = 0
                for oc in range(8):
                    for b_ in range(3):
                        c0 = b_ * 1024 + oc * 128
                        T.dma('pool', f'wM{di % 16}', out=wm[:, :, c0:c0 + 128], in_=wmerge_in[l, :, c0:c0 + 128].rearrange("(k p) n -> p k n", p=128), writes=[('wm', b_, oc)], throttle=True)
                        di += 1
                        T.dma('pool', f'wM{di % 16}', out=wb[:, b_, :, oc * 128:(oc + 1) * 128], in_=wsrcs[b_][l, :, oc * 128:(oc + 1) * 128].rearrange("(k p) n -> p k n", p=128), writes=[('wb', b_, oc)], throttle=True)
                        di += 1
                for oc in range(8):
                    T.dma('pool', f'wM{di % 16}', out=wo[:, :, oc * 128:(oc + 1) * 128], in_=wout_in[l, :, oc * 128:(oc + 1) * 128].rearrange("(k p) n -> p k n", p=128), writes=[('wo', oc)], throttle=True)
                    di += 1
                blocks = [(t0, 512, 0) for t0 in range(0, TFULL[l], 512)]
                if l == 0:
                    blocks.append((TL, 256, 1))
                pr = [0]

                def nextps():
                    i = pr[0] % 6
                    pr[0] += 1
                    return i
                def m_loads(bi):
                    t0, n, isctx = blocks[bi]
                    sl = bi % 2
                    hT, y, xt = hTs[sl], ys[sl], xts[sl]
                    key = ('mblk', sl)
                    T.dma('sp', f'MhT{sl}', out=hT[:, :, 0:n], in_=S['hT'][:, t0:t0 + n].rearrange("(k p) t -> p k t", p=128), writes=[(key, 'hTm')])
                    for bi_, nm in enumerate(('ya', 'yb', 'yc')):
                        T.dma('sp', f'My{sl}', out=y[:, bi_, :, 0:n], in_=S[nm][:, t0:t0 + n].rearrange("(k p) t -> p k t", p=128), writes=[(key, 'y', bi_)])
                    if l == 0:
                        src = (ctxT_in if isctx else xT_in[:, t0:t0 + n])
                    else:
                        src = S['xs'][:, t0:t0 + n]
                    T.dma('sp', f'Mx{sl}', out=xt[:, :, 0:n], in_=src.rearrange("(k p) t -> p k t", p=128), reads=[('xsd', t0)], writes=[(key, 'xt')])

                m_loads(0)
                for bi, (t0, n, isctx) in enumerate(blocks):
                    sl = bi % 2
                    hT, y, xt = hTs[sl], ys[sl], xts[sl]
                    key = ('mblk', sl)
                    if bi + 1 < len(blocks):
                        m_loads(bi + 1)
                    for oc in range(8):
                        for b_ in range(3):
                            pg = nextps()

                            def mmg(e, pg=pg, b_=b_, oc=oc):
                                last_ = None
                                c0 = b_ * 1024 + oc * 128
                                for k in range(8):
                                    last_ = e.matmul(ps[pg][:, 0:n], wm[:, k, c0:c0 + 128], hT[:, k, 0:n], start=(k == 0), stop=(k == 7))
                                return last_
                            T.op('pe', mmg, reads=[('wm', b_, oc), (key, 'hTm')], writes=[('ps', pg)])
                            T.op('act', lambda e, pg=pg, b_=b_, oc=oc: e.activation(out=gsb[:, b_, 0:n], in_=ps[pg][:, 0:n], func=AF.Sigmoid,
                                                                                  bias=ppt[:, PP[f'bmerge{l}'] + b_ * 8 + oc:PP[f'bmerge{l}'] + b_ * 8 + oc + 1], scale=1.0),
                                 reads=[('ps', pg), 'ppt'], writes=[('gsb', b_)])
                            pp_ = nextps()

                            def mmp(e, pp_=pp_, b_=b_, oc=oc):
                                last_ = None
                                for k in range(4):
                                    last_ = e.matmul(ps[pp_][:, 0:n], wb[:, b_, k, oc * 128:(oc + 1) * 128], y[:, b_, k, 0:n], start=(k == 0), stop=(k == 3))
                                return last_
                            T.op('pe', mmp, reads=[('wb', b_, oc), (key, 'y', b_)], writes=[('ps', pp_)])
                            T.op('dve', lambda e, pp_=pp_, b_=b_: e.tensor_tensor(out=mt[:, b_, 0:n], in0=ps[pp_][:, 0:n], in1=gsb[:, b_, 0:n], op=ALU.mult),
                                 reads=[('ps', pp_), ('gsb', b_)], writes=[('mt', b_)])
                        T.op('dve', lambda e: e.tensor_tensor(out=mt[:, 0, 0:n], in0=mt[:, 0, 0:n], in1=mt[:, 1, 0:n], op=ALU.add),
                             reads=[('mt', 0), ('mt', 1)], writes=[('mt', 0)])
                        T.op('dve', lambda e, oc=oc: e.tensor_tensor(out=mix[:, oc, 0:n], in0=mt[:, 0, 0:n], in1=mt[:, 2, 0:n], op=ALU.add),
                             reads=[('mt', 0), ('mt', 2)], writes=[('mix', oc)])
                    for oc in range(8):
                        po = nextps()

                        def mmo(e, po=po, oc=oc):
                            last_ = None
                            for k in range(8):
                                last_ = e.matmul(ps[po][:, 0:n], wo[:, k, oc * 128:(oc + 1) * 128], mix[:, k, 0:n], start=(k == 0), stop=(k == 7))
                            return last_
                        T.op('pe', mmo, reads=[('wo', oc)] + [('mix', k) for k in range(8)], writes=[('ps', po)])
                        T.op('dve', lambda e, po=po, oc=oc: e.scalar_tensor_tensor(out=xt[:, oc, 0:n], in0=ps[po][:, 0:n], scalar=modt[:, l, 2, oc, isctx:isctx + 1],
                                                                                 in1=xt[:, oc, 0:n], op0=ALU.mult, op1=ALU.add),
                             reads=[('ps', po), (key, 'xt')], writes=[(key, 'xt')])
                    T.dma('sp', f'Mxs{sl}', out=S['xs'][:, t0:t0 + n].rearrange("(k p) t -> p k t", p=128), in_=xt[:, :, 0:n], reads=[(key, 'xt')], writes=[('xsd', t0)])
                    norm_block(xt, (key, 'xt'), sq, rs, tmpf, hf, 'hfm', n, l, 3, isctx)
                    T.dma('sp', 'Mhf', out=S['hf'][:, t0:t0 + n].rearrange("(k p) t -> p k t", p=128), in_=hf[:, :, 0:n], reads=[('hfm', 'hT', k) for k in range(8)])
            T.barrier()

        def phase_F(l):
            with contextlib.ExitStack() as ph:
                wi = sbuf(ph, "F_wi", [128, 8, 2 * HID], BF16)
                wo2 = sbuf(ph, "F_wo", [128, NHC, 1024], BF16)
                hfs = [sbuf(ph, f"F_hf{i}", [128, 8, 512], BF16) for i in range(2)]
                act = sbuf(ph, "F_act", [128, NHC, 512], BF16)
                xt = sbuf(ph, "F_xt", [128, 8, 512], F32)
                gs = sbuf(ph, "F_gs", [128, 2, 512], F32)
                di = 0
                for hc in range(NHC):
                    for half in range(2):
                        c0 = half * HID + hc * 128
                        T.dma('pool', f'wF{di % 16}', out=wi[:, :, c0:c0 + 128], in_=wffi_in[l, :, c0:c0 + 128].rearrange("(k p) n -> p k n", p=128), writes=[('wi', half, hc)], throttle=True)
                        di += 1
                for k in range(2):
                    T.dma('pool', f'wF{di % 16}', out=wo2[:, k * 11:(k + 1) * 11, :], in_=wffo_in[l, k * 1408:(k + 1) * 1408, :].rearrange("(k p) n -> p k n", p=128), writes=[('wo2', k)], throttle=True)
                    di += 1
                blocks = [(t0, 512, 0) for t0 in range(0, TFULL[l], 512)]
                if l == 0:
                    blocks.append((TL, 256, 1))
                pr = [0]

                def nextps():
                    i = pr[0] % 6
                    pr[0] += 1
                    return i
                for bi, (t0, n, isctx) in enumerate(blocks):
                    sl = bi % 2
                    hf = hfs[sl]
                    if bi == 0:
                        T.dma('sp', f'Fhf{sl}', out=hf[:, :, 0:n], in_=S['hf'][:, t0:t0 + n].rearrange("(k p) t -> p k t", p=128), writes=[('hf', sl)])
                    if bi + 1 < len(blocks):
                        t0n, nn, _ = blocks[bi + 1]
                        T.dma('sp', f'Fhf{1 - sl}', out=hfs[1 - sl][:, :, 0:nn], in_=S['hf'][:, t0n:t0n + nn].rearrange("(k p) t -> p k t", p=128), writes=[('hf', 1 - sl)])
                    T.dma('sp', 'Fx', out=xt[:, :, 0:n], in_=S['xs'][:, t0:t0 + n].rearrange("(k p) t -> p k t", p=128), reads=[('xsd', t0)], writes=['xt'])
                    for hc in range(NHC):
                        pg = nextps()
                        pu = nextps()

                        def mmg(e, pg=pg, pu=pu, hc=hc):
                            last_ = None
                            for k in range(8):
                                e.matmul(ps[pg][:, 0:n], wi[:, k, hc * 128:(hc + 1) * 128], hf[:, k, 0:n], start=(k == 0), stop=(k == 7))
                            for k in range(8):
                                last_ = e.matmul(ps[pu][:, 0:n], wi[:, k, HID + hc * 128:HID + (hc + 1) * 128], hf[:, k, 0:n], start=(k == 0), stop=(k == 7))
                            return last_
                        T.op('pe', mmg, reads=[('wi', 0, hc), ('wi', 1, hc), ('hf', sl)], writes=[('ps', pg), ('ps', pu)])
                        a = hc % 2
                        T.op('act', lambda e, pg=pg, a=a: e.activation(out=gs[:, a, 0:n], in_=ps[pg][:, 0:n], func=AF.Silu), reads=[('ps', pg)], writes=[('gs', a)])
                        T.op('dve', lambda e, pu=pu, a=a, hc=hc: e.tensor_tensor(out=act[:, hc, 0:n], in0=ps[pu][:, 0:n], in1=gs[:, a, 0:n], op=ALU.mult),
                             reads=[('ps', pu), ('gs', a)], writes=[('act', hc)])
                    for oc in range(8):
                        po = nextps()

                        def mmo(e, po=po, oc=oc):
                            last_ = None
                            for hc in range(NHC):
                                last_ = e.matmul(ps[po][:, 0:n], wo2[:, hc, oc * 128:(oc + 1) * 128], act[:, hc, 0:n], start=(hc == 0), stop=(hc == NHC - 1))
                            return last_
                        T.op('pe', mmo, reads=[('wo2', 0), ('wo2', 1)] + [('act', hc) for hc in range(NHC)], writes=[('ps', po)])
                        T.op('dve', lambda e, po=po, oc=oc: e.scalar_tensor_tensor(out=xt[:, oc, 0:n], in0=ps[po][:, 0:n], scalar=modt[:, l, 5, oc, isctx:isctx + 1],
                                                                                 in1=xt[:, oc, 0:n], op0=ALU.mult, op1=ALU.add),
                             reads=[('ps', po), 'xt'], writes=['xt'])
                    if l == 0:
                        T.dma('sp', 'Fxs', out=S['xs'][:, t0:t0 + n].rearrange("(k p) t -> p k t", p=128), in_=xt[:, :, 0:n], reads=['xt'], writes=[('xsd', t0)])
                    else:
                        sqf = act[:, 0:8, :]
                        T.op('act', lambda e: e.activation(out=sqf[:, :, 0:n], in_=xt[:, :, 0:n], func=AF.Square), reads=['xt'] + [('act', hc) for hc in range(8)], writes=[('act', hc) for hc in range(8)])

                        def mms(e):
                            last_ = None
                            for k in range(8):
                                last_ = e.matmul(ps[6][:, 0:n], ones_bf[:], sqf[:, k, 0:n], start=(k == 0), stop=(k == 7))
                            return last_
                        T.op('pe', mms, reads=[('act', hc) for hc in range(8)] + ['ones_bf'], writes=[('ps', 6)])
                        T.op('dve', lambda e: e.tensor_scalar(out=gs[:, 0, 0:n], in0=ps[6][:, 0:n], scalar1=1.0 / D, scalar2=EPS, op0=ALU.mult, op1=ALU.add),
                             reads=[('ps', 6)], writes=[('gs', 0)])
                        T.op('act', lambda e: e.activation(out=gs[:, 0, 0:n], in_=gs[:, 0, 0:n], func=AF.Sqrt), reads=[('gs', 0)], writes=[('gs', 0)])
                        T.op('dve', lambda e: e.reciprocal(out=gs[:, 0, 0:n], in_=gs[:, 0, 0:n]), reads=[('gs', 0)], writes=[('gs', 0)])
                        for k in range(8):
                            T.op('dve', lambda e, k=k: e.scalar_tensor_tensor(out=xt[:, k, 0:n], in0=xt[:, k, 0:n], scalar=ppt[:, PP['fnorm'] + k:PP['fnorm'] + k + 1],
                                                                              in1=gs[:, 0, 0:n], op0=ALU.mult, op1=ALU.mult),
                                 reads=['xt', ('gs', 0), 'ppt'], writes=['xt'])
                        T.dma('sp', 'Fout', out=outT[:, t0:t0 + n].rearrange("(k p) t -> p k t", p=128), in_=xt[:, :, 0:n], reads=['xt'])
            T.barrier()

        phases = [('ada', phase_ada)]
        for l_ in range(2):
            phases += [(f'B{l_}', lambda l_=l_: phase_B(l_)), (f'G{l_}', lambda l_=l_: phase_G(l_)), (f'A{l_}', lambda l_=l_: phase_A(l_)),
                       (f'C{l_}', lambda l_=l_: phase_C(l_)), (f'M{l_}', lambda l_=l_: phase_M(l_)), (f'F{l_}', lambda l_=l_: phase_F(l_))]
        if only is not None:
            phases = [p for p in phases if p[0] in only]
        for name, fn in phases:
            fn()
            if stop_after == name:
                break
        T.final_wait('sp')
        print("instructions emitted:", T.ninst)
    return nc


IN_SIZES = (512, 128, 128, 256, 256, 512, 512, 32, 512, 512, 512)
IN_OFF = np.concatenate([[0], np.cumsum(IN_SIZES)])


def _local_to_global(half):
    tau = np.arange(TL)
    return tau if half == 0 else (SEQ - 1 - tau)


def _winx(w_in, half):
    o = IN_OFF
    aq = w_in[:, o[0]:o[1]]; ak = w_in[:, o[1]:o[2]]; av = w_in[:, o[2]:o[3]]
    bq = w_in[:, o[3]:o[4]]; bk = w_in[:, o[4]:o[5]]; bv = w_in[:, o[5]:o[6]]; br = w_in[:, o[6]:o[7]]
    bz = w_in[:, o[7]:o[8]]
    cq = w_in[:, o[8]:o[9]]; ck = w_in[:, o[9]:o[10]]; cv = w_in[:, o[10]:o[11]]
    out = np.zeros((D, NX), np.float32)
    sw = (np.arange(64) + 32) % 64
    heads = []
    for c in range(4):
        heads += [c, 4 + c]
    idx = np.concatenate([h * 64 + np.arange(64) for h in heads])
    idxs = np.concatenate([h * 64 + sw for h in heads])
    out[:, COL['Aq']:COL['Aq'] + 512] = aq[:, idx]
    out[:, COL['Aqs']:COL['Aqs'] + 512] = aq[:, idxs]
    out[:, COL['Ak']:COL['Ak'] + 128] = ak
    out[:, COL['Aks']:COL['Aks'] + 128] = ak[:, np.concatenate([sw, 64 + sw])]
    out[:, COL['Cq']:COL['Cq'] + 512] = cq
    out[:, COL['Ck']:COL['Ck'] + 512] = ck
    out[:, COL['Bq']:COL['Bq'] + 256] = bq
    out[:, COL['Bk']:COL['Bk'] + 256] = bk
    out[:, COL['Br']:COL['Br'] + 512] = br
    z1, z2 = (bz[:, 0:16], bz[:, 16:32]) if half == 0 else (bz[:, 16:32], bz[:, 0:16])
    out[:, COL['Bz']:COL['Bz'] + 16] = z1
    out[:, COL['Bz'] + 32:COL['Bz'] + 48] = z2
    out[:, COL['Av']:COL['Av'] + 128] = av
    out[:, COL['Bv']:COL['Bv'] + 512] = bv
    out[:, COL['Cv']:COL['Cv'] + 512] = cv
    return out


def _rope_tables(half):
    t = _local_to_global(half)
    row = (t // 64).astype(np.float32)
    col = (t % 64).astype(np.float32)
    inv = (np.float32(10000.0) ** (-np.arange(16, dtype=np.float32) / np.float32(16))).astype(np.float32)
    ang = np.concatenate([row[:, None] * inv[None], col[:, None] * inv[None]], axis=-1).astype(np.float32)
    cos = np.cos(ang).astype(np.float32).T
    sin = np.sin(ang).astype(np.float32).T
    cs = np.zeros((128, 2, TT), np.float32)
    cs[:, 0, TL:] = 1.0
    for rep in range(2):
        b = rep * 64
        cs[b:b + 32, 0, :TL] = cos; cs[b + 32:b + 64, 0, :TL] = cos
        cs[b:b + 32, 1, :TL] = -sin; cs[b + 32:b + 64, 1, :TL] = sin
    return cs


def _ctab(rpb, half):
    tab = np.full((128, NCTAB, 8, 128), -30000.0, np.float32)
    pairs = [(0, 0), (0, 1), (0, 2), (0, 3), (1, 0), (1, 1), (1, 2), (1, 3), (4, 2), (4, 3), (4, 4), (4, 5), (4, 6)]
    loc = np.arange(128)
    for ti, (qn, kn) in enumerate(pairs):
        tq = qn * 128 + loc
        tk = kn * 128 + loc
        if half == 1:
            tq = SEQ - 1 - tq
            tk = SEQ - 1 - tk
        qr, qc = tq // 64, tq % 64
        kr, kc = tk // 64, tk % 64
        rs = np.clip(qr - 4, 0, 64 - 8)
        ws = np.clip(qc - 8, 0, 64 - 16)
        valid = ((kr[:, None] >= rs[None]) & (kr[:, None] < rs[None] + 8) &
                 (kc[:, None] >= ws[None]) & (kc[:, None] < ws[None] + 16))
        dr = np.clip(kr[:, None] - qr[None] + 7, 0, 14)
        dc = np.clip(kc[:, None] - qc[None], -15, 15) + 15
        vals = rpb[:, dr, dc]
        tab[:, ti] = np.where(valid[None], vals, np.float32(-30000.0)).transpose(1, 0, 2)[:, [0, 2, 1, 3, 4, 6, 5, 7], :]
    return tab.reshape(128, NCTAB * 8 * 128)


def _dup2(v):
    a = v.reshape(-1, 128).T
    return np.repeat(a[:, :, None], 2, axis=2).reshape(128, -1)


def prep_core(inputs, core):
    b, half = core // 2, core % 2
    x = inputs['x'][b]
    t = np.arange(TL) if half == 0 else (SEQ - 1 - np.arange(TL))
    m = {}
    m['xT'] = np.ascontiguousarray(x[t].T)
    ctx = inputs['ctx'][b]
    if half == 1:
        ctx = ctx[::-1]
    m['ctxT'] = np.ascontiguousarray(ctx.T)
    pp = np.zeros((128, NPP), np.float32)
    cc = np.stack([inputs['c'][b].reshape(8, 128).T, inputs['c_ctx'].reshape(8, 128).T], axis=2)
    pp[:, PP['c']:PP['c'] + 16] = cc.reshape(128, 16)
    for l in range(2):
        pp[:, PP[f'bada{l}']:PP[f'bada{l}'] + 96] = _dup2(inputs['b_ada'][l])
        pp[:, PP[f'nmix{l}']:PP[f'nmix{l}'] + 16] = _dup2(inputs['norm_mix'][l])
        pp[:, PP[f'nffn{l}']:PP[f'nffn{l}'] + 16] = _dup2(inputs['norm_ffn'][l])
        pp[:, PP[f'bmerge{l}']:PP[f'bmerge{l}'] + 24] = inputs['b_merge'][l].reshape(24, 128).T
        pp[:, PP[f'gnorm{l}']] = inputs['gla_norm'][l]
        pp[:, PP[f'sink{l}']:PP[f'sink{l}'] + 8] = inputs['attn_sink'][l][None, :]
    pp[:, PP['fnorm']:PP['fnorm'] + 8] = inputs['final_norm'].reshape(8, 128).T
    m['pp'] = pp
    m['w_ada'] = inputs['w_ada']
    m['winx'] = np.stack([_winx(inputs['w_in'][l], half) for l in range(2)])
    gw = np.zeros((64, 2, 256), np.float32)
    d1w, d2w = ('gla_gate_w_fwd', 'gla_gate_w_bwd') if half == 0 else ('gla_gate_w_bwd', 'gla_gate_w_fwd')
    d1b, d2b = ('gla_gate_b_fwd', 'gla_gate_b_bwd') if half == 0 else ('gla_gate_b_bwd', 'gla_gate_b_fwd')
    for l in range(2):
        gw[0:16, l] = inputs[d1w][l]; gw[32:48, l] = inputs[d2w][l]
        gw[16, l] = inputs[d1b][l]; gw[48, l] = inputs[d2b][l]
    m['gwp'] = gw
    m['ones_row'] = np.ones((1, TT), np.float32)
    m['cossin'] = _rope_tables(half)
    s_, t_ = np.meshgrid(np.arange(128), np.arange(128), indexing='ij')
    m['cst'] = np.stack([(s_ <= t_), (s_ >= t_), (s_ == t_)], axis=1).astype(np.float32)
    m['ctab'] = np.stack([_ctab(inputs['na_rpb'][l], half) for l in range(2)])
    for k in ('w_branch_a', 'w_branch_b', 'w_branch_c', 'w_merge', 'w_out', 'w_ffn_in', 'w_ffn_out'):
        m[k] = inputs[k]
    return m


_NC_CACHE = {}


def kernel(**inputs):
    inputs = {k: np.asarray(v) for k, v in inputs.items()}
    if 'nc' not in _NC_CACHE:
        _NC_CACHE['nc'] = build()
    nc = _NC_CACHE['nc']
    in_maps = [prep_core(inputs, c) for c in range(8)]
    res = run_bass_kernel_spmd(nc, in_maps, core_ids=list(range(8)))
    out = np.zeros((4, SEQ, D), np.float32)
    for c in range(8):
        b, half = c // 2, c % 2
        o = res.results[c]["outT"].T
        if half == 0:
            out[b, 0:2048] = o
        else:
            out[b, 2048:] = o[::-1]
    return out
```

```python
import contextlib
import numpy as np
import concourse.bass as bass
import concourse.mybir as mybir
from concourse.bass_utils import run_bass_kernel_spmd

F32 = mybir.dt.float32
BF16 = mybir.dt.bfloat16
AF = mybir.ActivationFunctionType
ALU = mybir.AluOpType

D = 1024
KC = 8
SEQ = 4096
TL = 3072
CT = 256
TT = TL + CT
TO = 2560 + CT
EPS = 1e-6
HID = 2816
NHC = 22

FM_GROUPS = [('Aq', 512), ('Aqs', 512), ('Ak', 128), ('Aks', 128), ('Cq', 512), ('Ck', 512),
             ('Bq', 256), ('Bk', 256), ('Br', 512), ('Bz', 64)]
TM_GROUPS = [('Av', 128), ('Bv', 512), ('Cv', 512)]
COL = {}
_o = 0
for _n, _w in FM_GROUPS + TM_GROUPS:
    COL[_n] = _o
    _o += _w
NX = _o

PP = {}
_o = 0
def _pp(name, n):
    global _o
    PP[name] = _o
    _o += n
_pp('c', 16)
for _l in range(2):
    _pp(f'bada{_l}', 96); _pp(f'nmix{_l}', 16); _pp(f'nffn{_l}', 16); _pp(f'bmerge{_l}', 24)
    _pp(f'gnorm{_l}', 1); _pp(f'sink{_l}', 8)
_pp('fnorm', 8)
NPP = _o

BLOCKS = [
    [(0, 512, 'F', 0), (512, 512, 'F', 0), (1024, 512, 'F', 0), (1536, 512, 'F', 0), (2048, 512, 'F', 0),
     (2560, 512, 'KV', 0), (TL, 256, 'F', 1)],
    [(0, 512, 'F', 0), (512, 512, 'F', 0), (1024, 512, 'F', 0), (1536, 512, 'F', 0),
     (2048, 512, 'KV', 0), (TL, 256, 'KV', 1)],
]
TFULL = [2560, 2048]
TGLA = [3072, 2560]
NCTAB = 13


class Trk:
    ENG = ('pe', 'act', 'dve', 'pool', 'sp')

    def __init__(s, nc, es):
        s.nc, s.es = nc, es
        s.E = dict(pe=nc.tensor, act=nc.scalar, dve=nc.vector, pool=nc.gpsimd, sp=nc.sync)
        s.sem = {}
        s.cnt = {}
        s.epoch = 0
        s.seen = {e: {} for e in s.ENG}
        s.lastw = {}
        s.rd = {}
        s._new_compute_sems()
        s.ninst = 0
        s.dmap = {}
        s.dfree = []
        s.nd = 0

    def _new_compute_sems(s):
        for e in ('pe', 'act', 'dve', 'pool'):
            s.sem[e] = s.es.enter_context(s.nc.semaphore(f"c{s.epoch}_{e}"))
            s.cnt[e] = 0
            for w in s.ENG:
                s.seen[w].pop(e, None)

    def dsem(s, name):
        if name in s.dmap:
            return s.dmap[name]
        if s.dfree:
            key = s.dfree.pop()
        else:
            key = 'd_%d' % s.nd
            s.nd += 1
            s.sem[key] = s.es.enter_context(s.nc.semaphore(key))
            s.cnt[key] = 0
        s.dmap[name] = key
        return key

    def _wait(s, e, tok):
        k, v = tok
        if k == e and e == 'pe':
            return
        if s.seen[e].get(k, 0) >= v:
            return
        s.E[e].wait_ge(s.sem[k], v)
        s.seen[e][k] = v
        s.ninst += 1

    def _deps(s, e, reads, writes):
        for k in reads:
            if k in s.lastw:
                s._wait(e, s.lastw[k])
        for k in writes:
            for sk, v in s.rd.get(k, {}).items():
                if sk != e:
                    s._wait(e, (sk, v))
            if k in s.lastw and s.lastw[k][0] != e:
                s._wait(e, s.lastw[k])

    def _post(s, tok, reads, writes):
        for k in reads:
            d = s.rd.setdefault(k, {})
            d[tok[0]] = max(d.get(tok[0], 0), tok[1])
        for k in writes:
            s.lastw[k] = tok
            s.rd[k] = {}

    def op(s, e, fn, reads=(), writes=()):
        s._deps(e, reads, writes)
        inst = fn(s.E[e])
        s.cnt[e] += 1
        inst.then_inc(s.sem[e], 1)
        s._post((e, s.cnt[e]), reads, writes)
        s.ninst += 1

    def dma(s, q, semname, out, in_, reads=(), writes=(), throttle=False):
        key = s.dsem(semname)
        if throttle and s.cnt[key] > 0:
            s._wait(q, (key, s.cnt[key]))
        s._deps(q, reads, writes)
        inst = s.E[q].dma_start(out=out, in_=in_)
        s.cnt[key] += 16
        inst.then_inc(s.sem[key], 16)
        s._post((key, s.cnt[key]), reads, writes)
        s.ninst += 1

    def barrier(s):
        for e in s.ENG:
            for k in list(s.sem):
                if k == e:
                    continue
                if s.cnt[k] > 0:
                    s._wait(e, (k, s.cnt[k]))
        s.lastw.clear()
        s.rd.clear()
        s.dfree += sorted(s.dmap.values())
        s.dmap = {}
        s.epoch += 1
        s._new_compute_sems()

    def final_wait(s, e='sp'):
        for k in list(s.sem):
            if k != e and s.cnt[k] > 0:
                s._wait(e, (k, s.cnt[k]))


def build(stop_after=None, dbg=False, only=None):
    nc = bass.Bass("TRN2", target_bir_lowering=False)
    okind = "ExternalOutput" if dbg else "Internal"

    def din(name, shape, dt=F32):
        return nc.dram_tensor(name, list(shape), dt, kind="ExternalInput").ap()

    def dscr(name, shape, dt=BF16):
        return nc.dram_tensor(name, list(shape), dt, kind=okind).ap()

    xT_in = din("xT", [D, TL])
    ctxT_in = din("ctxT", [D, CT])
    pp_in = din("pp", [128, NPP])
    wada_in = din("w_ada", [2, D, 6 * D])
    winx_in = din("winx", [2, D, NX])
    gwp_in = din("gwp", [64, 2, 256])
    onesrow_in = din("ones_row", [1, TT])
    cs_in = din("cossin", [128, 2, TT])
    cst_in = din("cst", [128, 3, 128])
    ctab_in = din("ctab", [2, 128, NCTAB * 8 * 128])
    wba_in = din("w_branch_a", [2, 512, D])
    wbb_in = din("w_branch_b", [2, 512, D])
    wbc_in = din("w_branch_c", [2, 512, D])
    wmerge_in = din("w_merge", [2, D, 3 * D])
    wout_in = din("w_out", [2, D, D])
    wffi_in = din("w_ffn_in", [2, D, 2 * HID])
    wffo_in = din("w_ffn_out", [2, HID, D])
    outT = nc.dram_tensor("outT", [D, 2048], F32, kind="ExternalOutput").ap()

    S = {}
    S['hT'] = dscr("s_hT", [D, TT])
    S['Aq'] = dscr("s_Aq", [512, TT]); S['Ak'] = dscr("s_Ak", [128, TT])
    S['Cq'] = dscr("s_Cq", [512, TT]); S['Ck'] = dscr("s_Ck", [512, TT])
    S['Bq'] = dscr("s_Bq", [256, TT]); S['Bk'] = dscr("s_Bk", [256, TT]); S['Br'] = dscr("s_Br", [512, TT])
    S['Bz'] = dscr("s_Bz", [64, TT], F32)
    S['Av'] = dscr("s_Av", [TT, 256]); S['Bv'] = dscr("s_Bv", [TT, 512]); S['Cv'] = dscr("s_Cv", [TT, 1024])
    S['ya'] = dscr("s_ya", [512, TT]); S['yb'] = dscr("s_yb", [512, TT]); S['yc'] = dscr("s_yc", [512, TT])
    S['xs'] = dscr("s_xs", [D, TT], F32)
    S['hf'] = dscr("s_hf", [D, TT])
    if dbg:
        S['mod'] = dscr("s_mod", [128, 2 * 6 * 8 * 2], F32)
        S['oT'] = dscr("s_oT", [512, TO], F32)

    with contextlib.ExitStack() as es:
        T = Trk(nc, es)

        ucnt = [0]

        def sbuf(st, name, shape, dt):
            ucnt[0] += 1
            return st.enter_context(nc.sbuf_tensor(f"{name}_{ucnt[0]}", list(shape), dt))

        ps = [es.enter_context(nc.psum_tensor(f"ps{i}", [128, 512], F32)) for i in range(7)]
        psb = es.enter_context(nc.psum_tensor("psb", [128, 1024], BF16))
        ppt = sbuf(es, "ppt", [128, NPP], F32)
        modt = sbuf(es, "modt", [128, 2, 6, 8, 2], F32)
        cst = sbuf(es, "cstf", [128, 3, 128], F32)
        cstb = sbuf(es, "cstb", [128, 3, 128], BF16)
        ones_bf = sbuf(es, "ones_bf", [128, 128], BF16)
        ones_f = sbuf(es, "ones_f", [128, 128], F32)

        T.dma('sp', 'g0', out=ppt[:], in_=pp_in, writes=['ppt'])
        T.dma('sp', 'g1', out=cst[:], in_=cst_in, writes=['cst'])
        T.op('dve', lambda e: e.tensor_copy(out=cstb[:], in_=cst[:]), reads=['cst'], writes=['cstb'])
        T.op('pool', lambda e: e.memset(ones_bf[:], 1.0), writes=['ones_bf'])
        T.op('pool', lambda e: e.memset(ones_f[:], 1.0), writes=['ones_f'])

        def ppv(name, n):
            return ppt[:, PP[name]:PP[name] + n]

        sc = sbuf(es, "ada_sc", [128, 16], F32)
        adatmp = sbuf(es, "ada_tmp", [128, 16], F32)
        T.op('act', lambda e: e.activation(out=sc[:], in_=ppv('c', 16), func=AF.Silu), reads=['ppt'], writes=['sc'])
        scb = sbuf(es, "ada_scb", [128, 16], BF16)
        T.op('dve', lambda e: e.tensor_copy(out=scb[:], in_=sc[:]), reads=['sc'], writes=['sc'])
        scv = scb[:].rearrange("p (k i) -> p k i", i=2)
        adacnt = [0]

        def ada_load(l, j, wa):
            sl = adacnt[0] % len(wa)
            adacnt[0] += 1
            T.dma('pool', f'adaw{sl}', out=wa[sl][:],
                  in_=wada_in[l, :, j * 1024:(j + 1) * 1024].rearrange("(k p) n -> p k n", p=128), writes=[('wa', sl)])
            return sl

        def ada_piece(l, j, wa, pbank, sl=None):
            if sl is None:
                sl = ada_load(l, j, wa)

            def mm(e):
                last = None
                for oc in range(8):
                    for k in range(8):
                        last = e.matmul(ps[pbank][:, oc * 2:oc * 2 + 2], wa[sl][:, k, oc * 128:(oc + 1) * 128], scv[:, k, :],
                                        start=(k == 0), stop=(k == 7))
                return last
            T.op('pe', mm, reads=[('wa', sl), 'sc'], writes=[('ps', pbank)])
            b0 = PP[f'bada{l}'] + j * 16
            T.op('dve', lambda e: e.tensor_tensor(out=adatmp[:], in0=ps[pbank][:, 0:16], in1=ppt[:, b0:b0 + 16], op=ALU.add),
                 reads=[('ps', pbank), 'ppt'], writes=['adatmp'])
            dst = {0: 1, 1: 0, 2: 2, 3: 4, 4: 3, 5: 5}[j]
            tv = adatmp[:].rearrange("p (k i) -> p k i", i=2)
            if j in (1, 4):
                nname = f'nmix{l}' if j == 1 else f'nffn{l}'
                T.op('dve', lambda e: e.tensor_scalar(out=adatmp[:], in0=adatmp[:], scalar1=1.0, scalar2=None, op0=ALU.add), reads=['adatmp'], writes=['adatmp'])
                T.op('dve', lambda e: e.tensor_tensor(out=modt[:, l, dst], in0=tv, in1=ppv(nname, 16).rearrange("p (k i) -> p k i", i=2), op=ALU.mult),
                     reads=['adatmp', 'ppt'], writes=[('modt', l, dst)])
            else:
                T.op('dve', lambda e: e.tensor_copy(out=modt[:, l, dst], in_=tv), reads=['adatmp'], writes=[('modt', l, dst)])

        def phase_ada():
            with contextlib.ExitStack() as ph:
                wa = [sbuf(ph, f"ada_w{i}", [128, 8, 1024], BF16) for i in range(2)]
                ada_piece(0, 0, wa, 0)
                ada_piece(0, 1, wa, 1)
            T.barrier()

        def norm_block(xt, xkey, sq, rs, tmpf, hT, hkp, n, l, which, isctx, part='all'):
            if part in ('all', 'a'):
                T.op('act', lambda e: e.activation(out=sq[:, :, 0:n], in_=xt[:, :, 0:n], func=AF.Square),
                     reads=[xkey], writes=['nb_sq'])
            if part == 'a':
                return

            def mm(e):
                last = None
                for k in range(8):
                    last = e.matmul(ps[6][:, 0:n], ones_bf[:], sq[:, k, 0:n], start=(k == 0), stop=(k == 7))
                return last
            T.op('pe', mm, reads=['nb_sq', 'ones_bf'], writes=[('ps', 6)])
            T.op('dve', lambda e: e.tensor_scalar(out=rs[:, 0:n], in0=ps[6][:, 0:n], scalar1=1.0 / D, scalar2=EPS, op0=ALU.mult, op1=ALU.add),
                 reads=[('ps', 6)], writes=['nb_rs'])
            T.op('act', lambda e: e.activation(out=rs[:, 0:n], in_=rs[:, 0:n], func=AF.Sqrt), reads=['nb_rs'], writes=['nb_rs'])
            T.op('dve', lambda e: e.reciprocal(out=rs[:, 0:n], in_=rs[:, 0:n]), reads=['nb_rs'], writes=['nb_rs'])
            for k in range(8):
                T.op('dve', lambda e, k=k: e.scalar_tensor_tensor(out=tmpf[:, k % 2, 0:n], in0=xt[:, k, 0:n], scalar=modt[:, l, which, k, isctx:isctx + 1],
                                                                  in1=rs[:, 0:n], op0=ALU.mult, op1=ALU.mult),
                     reads=[xkey, 'nb_rs'], writes=[('nb_tmpf', k % 2)])
                T.op('act', lambda e, k=k: e.activation(out=hT[:, k, 0:n], in_=tmpf[:, k % 2, 0:n], func=AF.Identity,
                                                        bias=modt[:, l, which + 1, k, isctx:isctx + 1], scale=1.0),
                     reads=[('nb_tmpf', k % 2)], writes=[(hkp, 'hT', k)])

        def phase_B(l):
            with contextlib.ExitStack() as ph:
                w = sbuf(ph, "B_w", [128, 8, NX], BF16)
                xts = [sbuf(ph, f"B_xt{i}", [128, 8, 512], F32) for i in range(2)]
                hTs = [sbuf(ph, f"B_hT{i}", [128, 8, 512], BF16) for i in range(2)]
                sq = sbuf(ph, "B_sq", [128, 8, 512], BF16)
                rs = sbuf(ph, "B_rs", [128, 512], F32)
                tmpf = sbuf(ph, "B_tmpf", [128, 2, 512], F32)
                css = [sbuf(ph, f"B_cs{i}", [128, 2, 512], F32) for i in range(2)]
                r1 = sbuf(ph, "B_r1", [128, 2, 512], F32)
                r2 = sbuf(ph, "B_r2", [128, 2, 512], F32)
                fmA = sbuf(ph, "B_fmA", [128, 5, 512], BF16)
                fmC = sbuf(ph, "B_fmC", [128, 8, 512], BF16)
                fmB = sbuf(ph, "B_fmB", [128, 8, 512], BF16)
                fmZ = sbuf(ph, "B_fmZ", [64, 512], F32)
                tmA = sbuf(ph, "B_tmA", [128, 4, 2, 128], BF16)
                tmB = sbuf(ph, "B_tmB", [128, 4, 512], BF16)
                tmC = sbuf(ph, "B_tmC", [128, 4, 8, 128], BF16)
                worder = ([('Aq', c) for c in range(4)] + [('Aqs', c) for c in range(4)] + [('Ak', 0), ('Aks', 0)] + [('Cq', c) for c in range(4)]
                          + [('Ck', c) for c in range(4)] + [('Bq', 0), ('Bq', 1)] + [('Br', c) for c in range(4)] + [('Bk', 0), ('Bk', 1), ('Bz', 0)])
                worder = ([x for c in range(4) for x in (('Aq', c), ('Aqs', c))] + worder[8:])
                for wi_, (grp, c) in enumerate(worder):
                    wd = 64 if grp == 'Bz' else 128
                    c0 = COL[grp] + c * 128
                    T.dma('pool', f'wB{wi_ % 12}', out=w[:, :, c0:c0 + wd], in_=winx_in[l, :, c0:c0 + wd].rearrange("(k p) n -> p k n", p=128), writes=[('w', grp, c)])
                for wi_, (grp, wd) in enumerate(TM_GROUPS):
                    c0 = COL[grp]
                    T.dma('pool', f'wBt{wi_}', out=w[:, :, c0:c0 + wd], in_=winx_in[l, :, c0:c0 + wd].rearrange("(k p) n -> p k n", p=128), writes=[('w', grp, 0)])
                T.op('pool', lambda e: e.memset(tmA[:], 1.0), writes=[('tmA', j) for j in range(4)])
                T.op('pool', lambda e: e.memset(tmC[:], 1.0), writes=[('tmC', j) for j in range(4)])
                psrot = [0]

                def nextps():
                    i = psrot[0] % 6
                    psrot[0] += 1
                    return i

                def blk_loads(bi):
                    t0, n, kind, isctx = BLOCKS[l][bi]
                    sl = bi % 2
                    key = ('blk', sl)
                    if l == 0:
                        src = (ctxT_in if isctx else xT_in[:, t0:t0 + n])
                    else:
                        src = S['xs'][:, t0:t0 + n]
                    T.dma('sp', f'Bx{sl}', out=xts[sl][:, :, 0:n], in_=src.rearrange("(k p) t -> p k t", p=128), writes=[(key, 'xt')])
                    T.dma('sp', f'Bcs{sl}', out=css[sl][:, :, 0:n], in_=cs_in[:, :, t0:t0 + n], writes=[('cs', sl)])

                def blk_norm(bi, part):
                    t0, n, kind, isctx = BLOCKS[l][bi]
                    sl = bi % 2
                    key = ('blk', sl)
                    norm_block(xts[sl], (key, 'xt'), sq, rs, tmpf, hTs[sl], key, n, l, 0, isctx, part=part)
                    if part != 'a' and kind == 'F':
                        T.dma('sp', f'BhT{sl}', out=S['hT'][:, t0:t0 + n].rearrange("(k p) t -> p k t", p=128), in_=hTs[sl][:, :, 0:n],
                              reads=[(key, 'hT', k) for k in range(8)])

                blk_loads(0)
                blk_norm(0, 'all')
                for bi, (t0, n, kind, isctx) in enumerate(BLOCKS[l]):
                    sl = bi % 2
                    xt, hT = xts[sl], hTs[sl]
                    cs = css[sl]
                    key = ('blk', sl)
                    hkeys = [(key, 'hT', k) for k in range(8)]
                    if bi + 1 < len(BLOCKS[l]):
                        blk_loads(bi + 1)

                    def proj_fm(grp, c, width=128):
                        pi = nextps()
                        c0 = COL[grp] + c * 128

                        def mm(e):
                            last = None
                            for k in range(8):
                                last = e.matmul(ps[pi][0:width, 0:n], w[:, k, c0:c0 + width], hT[:, k, 0:n], start=(k == 0), stop=(k == 7))
                            return last
                        T.op('pe', mm, reads=[('w', grp, c)] + hkeys, writes=[('ps', pi)])
                        return pi

                    def rope_chunk(grp, grps, c, dst, dkey):
                        p1 = proj_fm(grp, c)
                        p2 = proj_fm(grps, c)
                        rr = (c % 2)
                        T.op('dve', lambda e: e.tensor_tensor(out=r1[:, rr, 0:n], in0=ps[p1][:, 0:n], in1=cs[:, 0, 0:n], op=ALU.mult),
                             reads=[('ps', p1), ('cs', sl)], writes=[('r1', rr)])
                        T.op('dve', lambda e: e.tensor_tensor(out=r2[:, rr, 0:n], in0=ps[p2][:, 0:n], in1=cs[:, 1, 0:n], op=ALU.mult),
                             reads=[('ps', p2), ('cs', sl)], writes=[('r2', rr)])
                        T.op('pool', lambda e: e.tensor_tensor(out=dst, in0=r1[:, rr, 0:n], in1=r2[:, rr, 0:n], op=ALU.add),
                             reads=[('r1', rr), ('r2', rr)], writes=[dkey])

                    evr = [0]

                    def evac(pi, dst, dkey, scale=None, func=None, width=128):
                        src_ = ps[pi][0:width, 0:n]
                        if func is not None:
                            T.op('act', lambda e: e.activation(out=dst, in_=src_, func=func), reads=[('ps', pi)], writes=[dkey])
                        elif scale is not None:
                            T.op('act', lambda e: e.activation(out=dst, in_=src_, func=AF.Copy, scale=scale), reads=[('ps', pi)], writes=[dkey])
                        else:
                            evr[0] += 1
                            if evr[0] % 2:
                                T.op('act', lambda e: e.activation(out=dst, in_=src_, func=AF.Copy), reads=[('ps', pi)], writes=[dkey])
                            else:
                                T.op('dve', lambda e: e.tensor_copy(out=dst, in_=src_), reads=[('ps', pi)], writes=[dkey])

                    def store_fm(name, tile_ap, nchunks, keys, rows=128):
                        T.dma('sp', f'st_{name}', out=S[name][:, t0:t0 + n].rearrange("(c p) t -> p c t", p=rows) if nchunks > 1 else S[name][:, t0:t0 + n],
                              in_=tile_ap, reads=keys)

                    if kind == 'F':
                        for c in range(4):
                            rope_chunk('Aq', 'Aqs', c, fmA[:, c, 0:n], ('fmA', c))
                        store_fm('Aq', fmA[:, 0:4, 0:n], 4, [('fmA', c) for c in range(4)])
                    rope_chunk('Ak', 'Aks', 0, fmA[:, 4, 0:n], ('fmA', 4))
                    store_fm('Ak', fmA[:, 4, 0:n], 1, [('fmA', 4)])
                    if kind == 'F':
                        for c in range(4):
                            evac(proj_fm('Cq', c), fmC[:, c, 0:n], ('fmC', c))
                        store_fm('Cq', fmC[:, 0:4, 0:n], 4, [('fmC', c) for c in range(4)])
                    for c in range(4):
                        evac(proj_fm('Ck', c), fmC[:, 4 + c, 0:n], ('fmC', 4 + c))
                    store_fm('Ck', fmC[:, 4:8, 0:n], 4, [('fmC', 4 + c) for c in range(4)])
                    if bi + 1 < len(BLOCKS[l]):
                        blk_norm(bi + 1, 'a')
                    if kind == 'F':
                        for c in range(2):
                            evac(proj_fm('Bq', c), fmB[:, c, 0:n], ('fmB', c), scale=0.125)
                        store_fm('Bq', fmB[:, 0:2, 0:n], 2, [('fmB', c) for c in range(2)])
                        for c in range(4):
                            evac(proj_fm('Br', c), fmB[:, 4 + c, 0:n], ('fmB', 4 + c), func=AF.Silu)
                        store_fm('Br', fmB[:, 4:8, 0:n], 4, [('fmB', 4 + c) for c in range(4)])
                    for c in range(2):
                        evac(proj_fm('Bk', c), fmB[:, 2 + c, 0:n], ('fmB', 2 + c))
                    store_fm('Bk', fmB[:, 2:4, 0:n], 2, [('fmB', 2 + c) for c in range(2)])
                    pz = proj_fm('Bz', 0, width=64)
                    T.op('dve', lambda e: e.tensor_copy(out=fmZ[:, 0:n], in_=ps[pz][0:64, 0:n]), reads=[('ps', pz)], writes=['fmZ'])
                    T.dma('sp', 'st_Bz', out=S['Bz'][:, t0:t0 + n], in_=fmZ[:, 0:n], reads=['fmZ'])
                    if bi + 1 < len(BLOCKS[l]):
                        blk_norm(bi + 1, 'b')
                    nj = n // 128
                    for j in range(nj):
                        def proj_tm(grp, ncols, j=j):
                            pi = nextps()
                            c0 = COL[grp]

                            def mm(e):
                                last = None
                                for k in range(8):
                                    last = e.matmul(ps[pi][:, 0:ncols], hT[:, k, j * 128:(j + 1) * 128], w[:, k, c0:c0 + ncols], start=(k == 0), stop=(k == 7))
                                return last
                            T.op('pe', mm, reads=[('w', grp, 0)] + hkeys, writes=[('ps', pi)])
                            return pi
                        pa = proj_tm('Av', 128)
                        T.op('dve', lambda e, pa=pa, j=j: e.tensor_copy(out=tmA[:, j, :, 0:64], in_=ps[pa][:, 0:128].rearrange("p (g d) -> p g d", g=2)),
                             reads=[('ps', pa)], writes=[('tmA', j)])
                        pb = proj_tm('Bv', 512)
                        T.op('act', lambda e, pb=pb, j=j: e.activation(out=tmB[:, j, :], in_=ps[pb][:, 0:512], func=AF.Copy),
                             reads=[('ps', pb)], writes=[('tmB', j)])
                        pc = proj_tm('Cv', 512)
                        T.op('dve', lambda e, pc=pc, j=j: e.tensor_copy(out=tmC[:, j, :, 0:64], in_=ps[pc][:, 0:512].rearrange("p (g d) -> p g d", g=8)),
                             reads=[('ps', pc)], writes=[('tmC', j)])
                    T.dma('sp', 'st_Av', out=S['Av'][t0:t0 + n, :].rearrange("(j p) c -> p j c", p=128),
                          in_=tmA[:, 0:nj].rearrange("p j g d -> p j (g d)"), reads=[('tmA', j) for j in range(nj)])
                    T.dma('sp', 'st_Bv', out=S['Bv'][t0:t0 + n, :].rearrange("(j p) c -> p j c", p=128),
                          in_=tmB[:, 0:nj, :], reads=[('tmB', j) for j in range(nj)])
                    T.dma('sp', 'st_Cv', out=S['Cv'][t0:t0 + n, :].rearrange("(j p) c -> p j c", p=128),
                          in_=tmC[:, 0:nj].rearrange("p j g d -> p j (g d)"), reads=[('tmC', j) for j in range(nj)])
            T.barrier()


        def phase_G(l):
            with contextlib.ExitStack() as ph:
                qT = sbuf(ph, "G_q", [128, 2, TT], BF16)
                kT = sbuf(ph, "G_k", [128, 2, TT], BF16)
                v = sbuf(ph, "G_v", [128, TT // 128, 512], BF16)
                zT = sbuf(ph, "G_z", [64, TT], F32)
                oT = sbuf(ph, "G_o", [128, 4, TO], F32)
                gw = sbuf(ph, "G_gw", [64, 256], F32)
                mrep = sbuf(ph, "G_mrep", [128, 2, 4, 128], F32)
                NB = 3
                Sst = [sbuf(ph, f"G_S{i}", [128, 2, 128], F32) for i in range(2)]
                Sbf = [sbuf(ph, f"G_Sbf{i}", [128, 2, 128], BF16) for i in range(2)]
                e1 = [sbuf(ph, f"G_e1{i}", [128, 256], F32) for i in range(NB)]
                gpos = [sbuf(ph, f"G_gp{i}", [128, 256], BF16) for i in range(NB)]
                eb = [sbuf(ph, f"G_eb{i}", [128, 2, 128], F32) for i in range(NB)]
                enb = [sbuf(ph, f"G_enb{i}", [128, 2, 128], F32) for i in range(NB)]
                qt = [sbuf(ph, f"G_qt{i}", [128, 2, 128], BF16) for i in range(NB)]
                kt = [sbuf(ph, f"G_kt{i}", [128, 2, 128], BF16) for i in range(NB)]
                ktl = [sbuf(ph, f"G_ktl{i}", [128, 2, 128], BF16) for i in range(NB)]
                ktT = [sbuf(ph, f"G_ktT{i}", [128, 2, 128], BF16) for i in range(NB)]
                Am = [sbuf(ph, f"G_Am{i}", [128, 2, 2, 128], BF16) for i in range(NB)]
                T.dma('sp', 'Gq', out=qT[:], in_=S['Bq'].rearrange("(c p) t -> p c t", p=128), writes=['qT'])
                T.dma('sp', 'Gk', out=kT[:], in_=S['Bk'].rearrange("(c p) t -> p c t", p=128), writes=['kT'])
                T.dma('sp', 'Gv', out=v[:], in_=S['Bv'].rearrange("(j p) c -> p j c", p=128), writes=['v'])
                T.dma('sp', 'Gz', out=zT[:], in_=S['Bz'], writes=['zT'])
                T.dma('sp', 'Ggw', out=gw[:], in_=gwp_in[:, l, :], writes=['gw'])
                T.dma('sp', 'Gz', out=zT[16:17, :], in_=onesrow_in, writes=['zT'])
                T.dma('sp', 'Gz', out=zT[48:49, :], in_=onesrow_in, writes=['zT'])
                for d_ in range(2):
                    for h in range(4):
                        T.op('pool', lambda e, d_=d_, h=h: e.tensor_copy(out=mrep[:, d_, h, :], in_=cst[:, d_, :]), reads=['cst'], writes=['mrep'])

                def prep(idx, item):
                    if item[0] != 'chunk':
                        return
                    _, tok0, d_, need_out, ocol = item
                    cb = idx % NB
                    pz = idx % 2
                    rows = slice(0, 17) if d_ == 0 else slice(32, 49)
                    last = 127 if d_ == 0 else 0
                    bz = ps[pz]
                    kz, kcs = ('psz', pz), ('pscs', pz)
                    T.op('pe', lambda e: e.matmul(bz[:, 0:256], zT[rows, tok0:tok0 + 128], gw[rows, :], start=True, stop=True), reads=['zT', 'gw'], writes=[kz])
                    T.op('act', lambda e: e.activation(out=e1[cb][:], in_=bz[:, 0:256], func=AF.Exp, scale=-1.0), reads=[kz], writes=[('e1', cb)])
                    T.op('act', lambda e: e.activation(out=gpos[cb][:], in_=e1[cb][:], func=AF.Ln, bias=1.0, scale=1.0), reads=[('e1', cb)], writes=[('gpos', cb)])

                    def mmcs(e):
                        e.matmul(bz[:, 256:384], gpos[cb][:, 0:128], cstb[:, d_, :], start=True, stop=True)
                        return e.matmul(bz[:, 384:512], gpos[cb][:, 128:256], cstb[:, d_, :], start=True, stop=True)
                    T.op('pe', mmcs, reads=[('gpos', cb), 'cstb'], writes=[kcs])
                    csv = bz[:, 256:512].rearrange("p (a t) -> p a t", a=2)
                    T.op('act', lambda e: e.activation(out=eb[cb][:], in_=csv, func=AF.Exp, scale=-1.0 / 16), reads=[kcs], writes=[('eb', cb)])
                    T.op('act', lambda e: e.activation(out=enb[cb][:], in_=csv, func=AF.Exp, scale=1.0 / 16), reads=[kcs], writes=[('enb', cb)])
                    T.op('dve', lambda e: e.tensor_tensor(out=kt[cb][:], in0=kT[:, :, tok0:tok0 + 128], in1=enb[cb][:], op=ALU.mult),
                         reads=['kT', ('enb', cb)], writes=[('kt', cb)])
                    for p in range(2):
                        T.op('dve', lambda e, p=p: e.tensor_scalar(out=ktl[cb][:, p, :], in0=kt[cb][:, p, :], scalar1=eb[cb][:, p, last:last + 1], scalar2=None, op0=ALU.mult),
                             reads=[('kt', cb), ('eb', cb)], writes=[('ktl', cb, p)])
                    if need_out:
                        T.op('dve', lambda e: e.tensor_tensor(out=qt[cb][:], in0=qT[:, :, tok0:tok0 + 128], in1=eb[cb][:], op=ALU.mult),
                             reads=['qT', ('eb', cb)], writes=[('qt', cb)])
                    tb = (idx % 4) * 256

                    def mmtr(e):
                        e.transpose(psb[:, tb:tb + 128], ktl[cb][:, 0, :], cstb[:, 2, :])
                        return e.transpose(psb[:, tb + 128:tb + 256], ktl[cb][:, 1, :], cstb[:, 2, :])
                    T.op('pe', mmtr, reads=[('ktl', cb, 0), ('ktl', cb, 1), 'cstb'], writes=[('psb', idx % 4)])
                    T.op('act', lambda e: e.activation(out=ktT[cb][:].rearrange("p a t -> p (a t)"), in_=psb[:, tb:tb + 256], func=AF.Copy),
                         reads=[('psb', idx % 4)], writes=[('ktT', cb)])
                    if need_out:
                        def mmA(e):
                            last_ = None
                            for h in (0, 2, 1, 3):
                                p, r = h // 2, slice((h % 2) * 64, (h % 2) * 64 + 64)
                                last_ = e.matmul(ps[2 + h % 2][:, p * 128:(p + 1) * 128], kt[cb][r, p, :], qt[cb][r, p, :], start=True, stop=True)
                            return last_
                        T.op('pe', mmA, reads=[('kt', cb), ('qt', cb)], writes=[('ps', 2), ('ps', 3)])
                        for par in range(2):
                            T.op('dve', lambda e, par=par: e.tensor_tensor(out=Am[cb][:, par], in0=ps[2 + par][:, 0:256].rearrange("p (h t) -> p h t", h=2), in1=mrep[:, d_, 0:2, :], op=ALU.mult),
                                 reads=[('ps', 2 + par), 'mrep'], writes=[('Am', cb, par)])

                written = set()

                def zero_state(d_):
                    T.op('dve', lambda e: e.memset(Sst[d_][:], 0.0), writes=[('S', d_, h) for h in range(4)])
                    T.op('dve', lambda e: e.memset(Sbf[d_][:], 0.0), writes=[('Sbf', d_)])

                def fin(idx, item):
                    if item[0] == 'reset':
                        zero_state(item[1])
                        return
                    _, tok0, d_, need_out, ocol = item
                    cb = idx % NB
                    last = 127 if d_ == 0 else 0
                    cj = tok0 // 128
                    if need_out:
                        pO = ps[4 + idx % 2]

                        def mmO(e):
                            last_ = None
                            for h in range(4):
                                p, r = h // 2, slice((h % 2) * 64, (h % 2) * 64 + 64)
                                e.matmul(pO[:, h * 128:(h + 1) * 128], v[:, cj, h * 128:(h + 1) * 128], Am[cb][:, h % 2, h // 2, :], start=True, stop=False)
                                last_ = e.matmul(pO[:, h * 128:(h + 1) * 128], Sbf[d_][r, p, :], qt[cb][r, p, :], start=False, stop=True)
                            return last_
                        T.op('pe', mmO, reads=['v', ('Am', cb, 0), ('Am', cb, 1), ('Sbf', d_), ('qt', cb)], writes=[('ps', 4 + idx % 2)])
                        pOv = pO[:, 0:512].rearrange("p (h t) -> p h t", h=4)
                        okey = ('oT', ocol)
                        if ocol not in written:
                            written.add(ocol)
                            T.op('act', lambda e: e.activation(out=oT[:, :, ocol:ocol + 128], in_=pOv, func=AF.Copy), reads=[('ps', 4 + idx % 2)], writes=[okey])
                        else:
                            T.op('dve', lambda e: e.tensor_tensor(out=oT[:, :, ocol:ocol + 128], in0=pOv, in1=oT[:, :, ocol:ocol + 128], op=ALU.add),
                                 reads=[('ps', 4 + idx % 2), okey], writes=[okey])
                    pU = ps[6]

                    def mmU(e):
                        last_ = None
                        for h in range(4):
                            last_ = e.matmul(pU[:, h * 128:(h + 1) * 128], ktT[cb][:, h // 2, :], v[:, cj, h * 128:(h + 1) * 128], start=True, stop=True)
                        return last_
                    T.op('pe', mmU, reads=[('ktT', cb), 'v'], writes=[('ps', 6)])
                    for h in range(4):
                        p, r = h // 2, slice((h % 2) * 64, (h % 2) * 64 + 64)
                        T.op('dve', lambda e, h=h, p=p, r=r: e.scalar_tensor_tensor(out=Sst[d_][r, p, :], in0=Sst[d_][r, p, :], scalar=eb[cb][r, p, last:last + 1],
                                                                                   in1=pU[r, h * 128:(h + 1) * 128], op0=ALU.mult, op1=ALU.add),
                             reads=[('ps', 6), ('eb', cb), ('S', d_, h)], writes=[('S', d_, h)])
                    T.op('act', lambda e: e.activation(out=Sbf[d_][:], in_=Sst[d_][:], func=AF.Copy), reads=[('S', d_, h) for h in range(4)], writes=[('Sbf', d_)])

                n1 = TFULL[l] // 128
                ng = TGLA[l] // 128
                L1 = [('chunk', TL + j * 128, 0, l == 0, 2560 + j * 128) for j in range(2)]
                L1 += [('chunk', j * 128, 0, True, j * 128) for j in range(n1)]
                L2 = [('chunk', j * 128, 1, j < n1, j * 128) for j in range(ng - 1, -1, -1)]
                if l == 0:
                    L2 += [('reset', 1)] + [('chunk', TL + j * 128, 1, True, 2560 + j * 128) for j in (1, 0)]
                seq = []
                for i in range(max(len(L1), len(L2))):
                    if i < len(L2):
                        seq.append(L2[i])
                    if i < len(L1):
                        seq.append(L1[i])
                zero_state(0)
                zero_state(1)
                LA = 2
                for idx in range(len(seq) + LA):
                    if idx < len(seq):
                        prep(idx, seq[idx])
                    if idx - LA >= 0:
                        fin(idx - LA, seq[idx - LA])
                if dbg:
                    T.dma('sp', 'dbg', out=S['oT'].rearrange("(h p) t -> p h t", p=128), in_=oT[:], reads=[('oT', c_) for c_ in range(0, TO, 128)])
                with contextlib.ExitStack() as ph2:
                    br = [sbuf(ph2, f"G_br{i}", [128, 4, 512], BF16) for i in range(2)]
                    sq = [sbuf(ph2, f"G_sq{i}", [128, 512], BF16) for i in range(2)]
                    rs = [sbuf(ph2, f"G_rs{i}", [128, 512], F32) for i in range(2)]
                    yt = [sbuf(ph2, f"G_yt{i}", [128, 512], F32) for i in range(2)]
                    yst = [sbuf(ph2, f"G_yst{i}", [128, 4, 512], BF16) for i in range(2)]
                    blocks = [(t0, 512, t0) for t0 in range(0, TFULL[l], 512)]
                    if l == 0:
                        blocks.append((TL, 256, 2560))
                    it = 0
                    for bi, (t0, n, oc0) in enumerate(blocks):
                        sl = bi % 2
                        T.dma('sp', f'Gbr{sl}', out=br[sl][:, :, 0:n], in_=S['Br'][:, t0:t0 + n].rearrange("(h p) t -> p h t", p=128), writes=[('br', sl)])
                        okeys = [('oT', c_) for c_ in range(oc0, oc0 + n, 128)]
                        for h in range(4):
                            a = it % 2
                            it += 1
                            pi = 2 + a
                            T.op('act', lambda e, a=a, h=h: e.activation(out=sq[a][:, 0:n], in_=oT[:, h, oc0:oc0 + n], func=AF.Square), reads=okeys, writes=[('gsq', a)])
                            T.op('pe', lambda e, a=a, pi=pi: e.matmul(ps[pi][:, 0:n], ones_bf[:], sq[a][:, 0:n], start=True, stop=True), reads=[('gsq', a), 'ones_bf'], writes=[('ps', pi)])
                            T.op('act', lambda e, a=a, pi=pi: e.activation(out=rs[a][:, 0:n], in_=ps[pi][:, 0:n], func=AF.Sqrt, bias=EPS, scale=1.0 / 128),
                                 reads=[('ps', pi)], writes=[('grs', a)])
                            T.op('dve', lambda e, a=a: e.reciprocal(out=rs[a][:, 0:n], in_=rs[a][:, 0:n]), reads=[('grs', a)], writes=[('grs', a)])
                            T.op('dve', lambda e, a=a, h=h: e.tensor_tensor(out=yt[a][:, 0:n], in0=oT[:, h, oc0:oc0 + n], in1=rs[a][:, 0:n], op=ALU.mult),
                                 reads=okeys + [('grs', a)], writes=[('gyt', a)])
                            T.op('dve', lambda e, a=a, h=h: e.scalar_tensor_tensor(out=yst[sl][:, h, 0:n], in0=yt[a][:, 0:n], scalar=ppv(f'gnorm{l}', 1),
                                                                                   in1=br[sl][:, h, 0:n], op0=ALU.mult, op1=ALU.mult),
                                 reads=[('gyt', a), ('br', sl), 'ppt'], writes=[('yst', sl, h)])
                        T.dma('sp', f'Gyb{sl}', out=S['yb'][:, t0:t0 + n].rearrange("(h p) t -> p h t", p=128), in_=yst[sl][:, :, 0:n],
                              reads=[('yst', sl, h) for h in range(4)])
            T.barrier()

        def phase_A(l):
            with contextlib.ExitStack() as ph:
                qT = sbuf(ph, "A_q", [128, 4, TT], BF16)
                kT = sbuf(ph, "A_k", [128, TT], BF16)
                v = sbuf(ph, "A_v", [128, TT // 128, 256], BF16)
                esk = sbuf(ph, "A_esk", [128, 8], F32)
                mrepb = sbuf(ph, "A_mrep", [128, 2, 4, 128], BF16)
                P = [sbuf(ph, f"A_P{i}", [128, 5, 512], BF16) for i in range(2)]
                den = [sbuf(ph, f"A_den{i}", [128, 512], F32) for i in range(2)]
                yst = [sbuf(ph, f"A_yst{i}", [64, 4, 128], BF16) for i in range(2)]
                ada_todo = []
                ada_pending = None
                if l == 0:
                    wa = [sbuf(ph, f"A_adaw{i}", [128, 8, 1024], BF16) for i in range(2)]
                    ada_todo = [(0, j) for j in range(2, 6)] + [(1, j) for j in range(6)]
                T.dma('sp', 'Aq', out=qT[:], in_=S['Aq'].rearrange("(c p) t -> p c t", p=128), writes=['qT'])
                T.dma('sp', 'Ak', out=kT[:], in_=S['Ak'], writes=['kT'])
                T.dma('sp', 'Av', out=v[:], in_=S['Av'].rearrange("(j p) c -> p j c", p=128), writes=['v'])
                T.op('act', lambda e: e.activation(out=esk[:], in_=ppv(f'sink{l}', 8), func=AF.Exp), reads=['ppt'], writes=['esk'])
                for d_ in range(2):
                    for h in range(4):
                        T.op('pool', lambda e, d_=d_, h=h: e.tensor_copy(out=mrepb[:, d_, h, :], in_=cstb[:, d_, :]), reads=['cstb'], writes=['mrepb'])
                qtiles = [(n, False) for n in range(TFULL[l] // 128)]
                if l == 0:
                    qtiles += [(24, True), (25, True)]
                units = []
                for (n, isctx) in qtiles:
                    if isctx:
                        klist = [(24, None), (25, None)]
                    else:
                        klist = ([(n - 1, 1)] if n > 0 else []) + [(n, None), (n + 1, 0), (24, None), (25, None)]
                    for g in range(2):
                        units.append((n, klist, g))
                sbc = [0]
                adast = [ada_pending]

                def emit_S(ui):
                    n, klist, g = units[ui]
                    sl = ui % 2
                    if l == 0 and ui % 4 == 0:
                        if adast[0] is not None:
                            ada_piece(*adast[0][0], wa, 5, sl=adast[0][1])
                            adast[0] = None
                        if ada_todo:
                            lj = ada_todo.pop(0)
                            adast[0] = (lj, ada_load(*lj, wa))
                    r = slice(g * 64, g * 64 + 64)
                    for i, (kt_, m) in enumerate(klist):
                        pi = sbc[0] % 3
                        sbc[0] += 1
                        T.op('pe', lambda e, pi=pi, kt_=kt_: e.matmul(ps[pi][:, 0:512].rearrange("p (j t) -> p j t", j=4), kT[r, kt_ * 128:(kt_ + 1) * 128],
                                                                     qT[r, :, n * 128:(n + 1) * 128], start=True, stop=True),
                             reads=['kT', 'qT'], writes=[('ps', pi)])
                        T.op('act', lambda e, pi=pi, i=i: e.activation(out=P[sl][:, i, :], in_=ps[pi][:, 0:512], func=AF.Exp, scale=0.125),
                             reads=[('ps', pi)], writes=[('P', sl, i)])
                        if m is not None:
                            T.op('pool', lambda e, i=i, m=m: e.tensor_tensor(out=P[sl][:, i, :].rearrange("p (j t) -> p j t", j=4),
                                                                            in0=P[sl][:, i, :].rearrange("p (j t) -> p j t", j=4), in1=mrepb[:, m], op=ALU.mult),
                                 reads=[('P', sl, i), 'mrepb'], writes=[('P', sl, i)])

                def emit_O(ui):
                    n, klist, g = units[ui]
                    sl = ui % 2
                    po = 3 + sl

                    def mmO(e):
                        last_ = None
                        for i, (kt_, m) in enumerate(klist):
                            last_ = e.matmul(ps[po][:, 0:512], v[:, kt_, g * 128:(g + 1) * 128], P[sl][:, i, :], start=(i == 0), stop=(i == len(klist) - 1))
                        return last_
                    T.op('pe', mmO, reads=['v'] + [('P', sl, i) for i in range(len(klist))], writes=[('ps', po)])
                    for j in range(4):
                        T.op('dve', lambda e, j=j: e.tensor_scalar(out=den[sl][64:128, j * 128:(j + 1) * 128], in0=ps[po][64:128, j * 128:(j + 1) * 128],
                                                                  scalar1=esk[64:128, 4 * g + j:4 * g + j + 1], scalar2=None, op0=ALU.add),
                             reads=[('ps', po), 'esk'], writes=[('den', sl, j)])
                    T.op('dve', lambda e: e.reciprocal(out=den[sl][64:128, :], in_=den[sl][64:128, :]), reads=[('den', sl, j) for j in range(4)], writes=[('den', sl)])
                    T.op('dve', lambda e: e.tensor_tensor(out=yst[sl][0:64, :, :], in0=ps[po][0:64, 0:512].rearrange("p (j t) -> p j t", j=4),
                                                          in1=den[sl][64:128, :].rearrange("p (j t) -> p j t", j=4), op=ALU.mult),
                         reads=[('ps', po), ('den', sl)], writes=[('yst', sl)])
                    T.dma('sp', f'Ayst{sl}', out=S['ya'][g * 256:(g + 1) * 256, n * 128:(n + 1) * 128].rearrange("(j d) t -> d j t", d=64), in_=yst[sl][:],
                          reads=[('yst', sl)])

                emit_S(0)
                for ui in range(len(units)):
                    if ui + 1 < len(units):
                        emit_S(ui + 1)
                    emit_O(ui)
                ada_pending = adast[0]
                if ada_pending is not None:
                    ada_piece(*ada_pending[0], wa, 5, sl=ada_pending[1])
                assert not ada_todo
                if dbg and l == 0:
                    T.dma('sp', 'dbg', out=S['mod'], in_=modt[:].rearrange("p l a k i -> p (l a k i)"),
                          reads=[('modt', l_, a_) for l_ in range(2) for a_ in range(6)])
            T.barrier()

        def phase_C(l):
            with contextlib.ExitStack() as ph:
                qT = sbuf(ph, "C_q", [128, 4, TT], BF16)
                kT = sbuf(ph, "C_k", [128, 4, TT], BF16)
                v = sbuf(ph, "C_v", [128, TT // 128, 1024], BF16)
                EB = sbuf(ph, "C_EB", [128, NCTAB, 512 * 2], BF16)
                P = [sbuf(ph, f"C_P{i}", [128, 7, 512], BF16) for i in range(2)]
                den = [sbuf(ph, f"C_den{i}", [128, 512], F32) for i in range(2)]
                yst = [sbuf(ph, f"C_yst{i}", [64, 4, 128], BF16) for i in range(2)]
                T.dma('sp', 'Cq', out=qT[:], in_=S['Cq'].rearrange("(c p) t -> p c t", p=128), writes=['qT'])
                T.dma('sp', 'Ck', out=kT[:], in_=S['Ck'].rearrange("(c p) t -> p c t", p=128), writes=['kT'])
                T.dma('sp', 'Cv', out=v[:], in_=S['Cv'].rearrange("(j p) c -> p j c", p=128), writes=['v'])
                T.dma('pool', 'Ctab', out=EB[:].rearrange("p a b -> p (a b)"), in_=ctab_in[l], writes=['EBraw'])
                for a in range(NCTAB):
                    T.op('act', lambda e, a=a: e.activation(out=EB[:, a, :], in_=EB[:, a, :], func=AF.Exp), reads=['EBraw'], writes=[('EB', a)])
                ebkeys = [('EB', a) for a in range(NCTAB)]
                qtiles = [(n, False) for n in range(TFULL[l] // 128)]
                if l == 0:
                    qtiles += [(24, True), (25, True)]
                units = []
                for (n, isctx) in qtiles:
                    if isctx:
                        klist = [(24, None), (25, None)]
                    elif n == 0:
                        klist = [(0, 0), (1, 1), (2, 2), (3, 3), (24, None), (25, None)]
                    elif n == 1:
                        klist = [(0, 4), (1, 5), (2, 6), (3, 7), (24, None), (25, None)]
                    else:
                        klist = [(n - 2 + i, 8 + i) for i in range(5)] + [(24, None), (25, None)]
                    for hq in range(2):
                        units.append((n, klist, hq))
                sbc = [0]
                altc = [0]

                def emit_S(ui):
                    n, klist, hq = units[ui]
                    sl = ui % 2
                    for i, (kt_, ti) in enumerate(klist):
                        pr_ = (sbc[0] % 2) * 2
                        sbc[0] += 1

                        def mmS(e, pr_=pr_, kt_=kt_, hq=hq):
                            last_ = None
                            for j in (0, 2, 1, 3):
                                h = 4 * hq + j
                                r = slice((h % 2) * 64, (h % 2) * 64 + 64)
                                last_ = e.matmul(ps[pr_ + j % 2][:, (j // 2) * 128:(j // 2 + 1) * 128], kT[r, h // 2, kt_ * 128:(kt_ + 1) * 128],
                                                 qT[r, h // 2, n * 128:(n + 1) * 128], start=True, stop=True)
                            return last_
                        T.op('pe', mmS, reads=['kT', 'qT'], writes=[('ps', pr_), ('ps', pr_ + 1)])
                        for par in range(2):
                            T.op('act', lambda e, pr_=pr_, i=i, sl=sl, par=par: e.activation(out=P[sl][:, i, par * 256:(par + 1) * 256], in_=ps[pr_ + par][:, 0:256], func=AF.Exp, scale=0.125),
                                 reads=[('ps', pr_ + par)], writes=[('P', sl, i, par)])
                        if ti is not None:
                            altc[0] += 1
                            eng = 'pool' if altc[0] % 2 else 'dve'
                            T.op(eng, lambda e, i=i, ti=ti, sl=sl, hq=hq: e.tensor_tensor(out=P[sl][:, i, :], in0=P[sl][:, i, :], in1=EB[:, ti, hq * 512:(hq + 1) * 512], op=ALU.mult),
                                 reads=[('P', sl, i, 0), ('P', sl, i, 1)] + ebkeys, writes=[('P', sl, i, 0), ('P', sl, i, 1)])

                def emit_O(ui):
                    n, klist, hq = units[ui]
                    sl = ui % 2
                    po = 4 + sl

                    def mmO(e, klist=klist, sl=sl, po=po, hq=hq):
                        last_ = None
                        for j in range(4):
                            h = 4 * hq + j
                            sj = (j % 2) * 2 + j // 2
                            for i, (kt_, ti) in enumerate(klist):
                                last_ = e.matmul(ps[po][:, j * 128:(j + 1) * 128], v[:, kt_, h * 128:(h + 1) * 128], P[sl][:, i, sj * 128:(sj + 1) * 128],
                                                 start=(i == 0), stop=(i == len(klist) - 1))
                        return last_
                    T.op('pe', mmO, reads=['v'] + [('P', sl, i, par) for i in range(len(klist)) for par in range(2)], writes=[('ps', po)])
                    T.op('dve', lambda e, sl=sl, po=po: e.reciprocal(out=den[sl][64:128, :], in_=ps[po][64:128, 0:512]), reads=[('ps', po)], writes=[('den', sl)])
                    T.op('dve', lambda e, sl=sl, po=po: e.tensor_tensor(out=yst[sl][0:64, :, :], in0=ps[po][0:64, 0:512].rearrange("p (j t) -> p j t", j=4),
                                                                        in1=den[sl][64:128, :].rearrange("p (j t) -> p j t", j=4), op=ALU.mult),
                         reads=[('ps', po), ('den', sl)], writes=[('yst', sl)])
                    T.dma('sp', f'Cyst{sl}', out=S['yc'][hq * 256:(hq + 1) * 256, n * 128:(n + 1) * 128].rearrange("(j d) t -> d j t", d=64), in_=yst[sl][:],
                          reads=[('yst', sl)])
                emit_S(0)
                for ui in range(len(units)):
                    if ui + 1 < len(units):
                        emit_S(ui + 1)
                    emit_O(ui)
            T.barrier()

        def phase_M(l):
            with contextlib.ExitStack() as ph:
                wm = sbuf(ph, "M_wm", [128, 8, 3072], BF16)
                wb = sbuf(ph, "M_wb", [128, 3, 4, 1024], BF16)
                wo = sbuf(ph, "M_wo", [128, 8, 1024], BF16)
                hTs = [sbuf(ph, f"M_hT{i}", [128, 8, 512], BF16) for i in range(2)]
                ys = [sbuf(ph, f"M_y{i}", [128, 3, 4, 512], BF16) for i in range(2)]
                xts = [sbuf(ph, f"M_xt{i}", [128, 8, 512], F32) for i in range(2)]
                mix = sbuf(ph, "M_mix", [128, 8, 512], BF16)
                gsb = sbuf(ph, "M_gsb", [128, 3, 512], F32)
                mt = sbuf(ph, "M_mt", [128, 3, 512], F32)
                sq = sbuf(ph, "M_sq", [128, 8, 512], BF16)
                rs = sbuf(ph, "M_rs", [128, 512], F32)
                tmpf = sbuf(ph, "M_tmpf", [128, 2, 512], F32)
                hf = sbuf(ph, "M_hf", [128, 8, 512], BF16)
                wsrcs = (wba_in, wbb_in, wbc_in)
                di = 0
                for oc in range(8):
                    for b_ in range(3):
                        c0 = b_ * 1024 + oc * 128
                        T.dma('pool', f'wM{di % 16}', out=wm[:, :, c0:c0 + 128], in_=wmerge_in[l, :, c0:c0 + 128].rearrange("(k p) n -> p k n", p=128), writes=[('wm', b_, oc)], throttle=True)
                        di += 1
                        T.dma('pool', f'wM{di % 16}', out=wb[:, b_, :, oc * 128:(oc + 1) * 128], in_=wsrcs[b_][l, :, oc * 128:(oc + 1) * 128].rearrange("(k p) n -> p k n", p=128), writes=[('wb', b_, oc)], throttle=True)
                        di += 1
                for oc in range(8):
                    T.dma('pool', f'wM{di % 16}', out=wo[:, :, oc * 128:(oc + 1) * 128], in_=wout_in[l, :, oc * 128:(oc + 1) * 128].rearrange("(k p) n -> p k n", p=128), writes=[('wo', oc)], throttle=True)
                    di += 1
                blocks = [(t0, 512, 0) for t0 in range(0, TFULL[l], 512)]
                if l == 0:
                    blocks.append((TL, 256, 1))
                pr = [0]

                def nextps():
                    i = pr[0] % 6
                    pr[0] += 1
                    return i
                def m_loads(bi):
                    t0, n, isctx = blocks[bi]
                    sl = bi % 2
                    hT, y, xt = hTs[sl], ys[sl], xts[sl]
                    key = ('mblk', sl)
                    T.dma('sp', f'MhT{sl}', out=hT[:, :, 0:n], in_=S['hT'][:, t0:t0 + n].rearrange("(k p) t -> p k t", p=128), writes=[(key, 'hTm')])
                    for bi_, nm in enumerate(('ya', 'yb', 'yc')):
                        T.dma('sp', f'My{sl}', out=y[:, bi_, :, 0:n], in_=S[nm][:, t0:t0 + n].rearrange("(k p) t -> p k t", p=128), writes=[(key, 'y', bi_)])
                    if l == 0:
                        src = (ctxT_in if isctx else xT_in[:, t0:t0 + n])
                    else:
                        src = S['xs'][:, t0:t0 + n]
                    T.dma('sp', f'Mx{sl}', out=xt[:, :, 0:n], in_=src.rearrange("(k p) t -> p k t", p=128), reads=[('xsd', t0)], writes=[(key, 'xt')])

                m_loads(0)
                for bi, (t0, n, isctx) in enumerate(blocks):
                    sl = bi % 2
                    hT, y, xt = hTs[sl], ys[sl], xts[sl]
                    key = ('mblk', sl)
                    if bi + 1 < len(blocks):
                        m_loads(bi + 1)
                    for oc in range(8):
                        for b_ in range(3):
                            pg = nextps()

                            def mmg(e, pg=pg, b_=b_, oc=oc):
                                last_ = None
                                c0 = b_ * 1024 + oc * 128
                                for k in range(8):
                                    last_ = e.matmul(ps[pg][:, 0:n], wm[:, k, c0:c0 + 128], hT[:, k, 0:n], start=(k == 0), stop=(k == 7))
                                return last_
                            T.op('pe', mmg, reads=[('wm', b_, oc), (key, 'hTm')], writes=[('ps', pg)])
                            T.op('act', lambda e, pg=pg, b_=b_, oc=oc: e.activation(out=gsb[:, b_, 0:n], in_=ps[pg][:, 0:n], func=AF.Sigmoid,
                                                                                  bias=ppt[:, PP[f'bmerge{l}'] + b_ * 8 + oc:PP[f'bmerge{l}'] + b_ * 8 + oc + 1], scale=1.0),
                                 reads=[('ps', pg), 'ppt'], writes=[('gsb', b_)])
                            pp_ = nextps()

                            def mmp(e, pp_=pp_, b_=b_, oc=oc):
                                last_ = None
                                for k in range(4):
                                    last_ = e.matmul(ps[pp_][:, 0:n], wb[:, b_, k, oc * 128:(oc + 1) * 128], y[:, b_, k, 0:n], start=(k == 0), stop=(k == 3))
                                return last_
                            T.op('pe', mmp, reads=[('wb', b_, oc), (key, 'y', b_)], writes=[('ps', pp_)])
                            T.op('dve', lambda e, pp_=pp_, b_=b_: e.tensor_tensor(out=mt[:, b_, 0:n], in0=ps[pp_][:, 0:n], in1=gsb[:, b_, 0:n], op=ALU.mult),
                                 reads=[('ps', pp_), ('gsb', b_)], writes=[('mt', b_)])
                        T.op('dve', lambda e: e.tensor_tensor(out=mt[:, 0, 0:n], in0=mt[:, 0, 0:n], in1=mt[:, 1, 0:n], op=ALU.add),
                             reads=[('mt', 0), ('mt', 1)], writes=[('mt', 0)])
                        T.op('dve', lambda e, oc=oc: e.tensor_tensor(out=mix[:, oc, 0:n], in0=mt[:, 0, 0:n], in1=mt[:, 2, 0:n], op=ALU.add),
                             reads=[('mt', 0), ('mt', 2)], writes=[('mix', oc)])
                    for oc in range(8):
                        po = nextps()

                        def mmo(e, po=po, oc=oc):
                            last_ = None
                            for k in range(8):
                                last_ = e.matmul(ps[po][:, 0:n], wo[:, k, oc * 128:(oc + 1) * 128], mix[:, k, 0:n], start=(k == 0), stop=(k == 7))
                            return last_
                        T.op('pe', mmo, reads=[('wo', oc)] + [('mix', k) for k in range(8)], writes=[('ps', po)])
                        T.op('dve', lambda e, po=po, oc=oc: e.scalar_tensor_tensor(out=xt[:, oc, 0:n], in0=ps[po][:, 0:n], scalar=modt[:, l, 2, oc, isctx:isctx + 1],
                                                                                 in1=xt[:, oc, 0:n], op0=ALU.mult, op1=ALU.add),
                             reads=[('ps', po), (key, 'xt')], writes=[(key, 'xt')])
                    T.dma('sp', f'Mxs{sl}', out=S['xs'][:, t0:t0 + n].rearrange("(k p) t -> p k t", p=128), in_=xt[:, :, 0:n], reads=[(key, 'xt')], writes=[('xsd', t0)])
                    norm_block(xt, (key, 'xt'), sq, rs, tmpf, hf, 'hfm', n, l, 3, isctx)
                    T.dma('sp', 'Mhf', out=S['hf'][:, t0:t0 + n].rearrange("(k p) t -> p k t", p=128), in_=hf[:, :, 0:n], reads=[('hfm', 'hT', k) for k in range(8)])
            T.barrier()

        def phase_F(l):
            with contextlib.ExitStack() as ph:
                wi = sbuf(ph, "F_wi", [128, 8, 2 * HID], BF16)
                wo2 = sbuf(ph, "F_wo", [128, NHC, 1024], BF16)
                hfs = [sbuf(ph, f"F_hf{i}", [128, 8, 512], BF16) for i in range(2)]
                act = sbuf(ph, "F_act", [128, NHC, 512], BF16)
                xt = sbuf(ph, "F_xt", [128, 8, 512], F32)
                gs = sbuf(ph, "F_gs", [128, 2, 512], F32)
                di = 0
                for hc in range(NHC):
                    for half in range(2):
                        c0 = half * HID + hc * 128
                        T.dma('pool', f'wF{di % 16}', out=wi[:, :, c0:c0 + 128], in_=wffi_in[l, :, c0:c0 + 128].rearrange("(k p) n -> p k n", p=128), writes=[('wi', half, hc)], throttle=True)
                        di += 1
                for k in range(2):
                    T.dma('pool', f'wF{di % 16}', out=wo2[:, k * 11:(k + 1) * 11, :], in_=wffo_in[l, k * 1408:(k + 1) * 1408, :].rearrange("(k p) n -> p k n", p=128), writes=[('wo2', k)], throttle=True)
                    di += 1
                blocks = [(t0, 512, 0) for t0 in range(0, TFULL[l], 512)]
                if l == 0:
                    blocks.append((TL, 256, 1))
                pr = [0]

                def nextps():
                    i = pr[0] % 6
                    pr[0] += 1
                    return i
                for bi, (t0, n, isctx) in enumerate(blocks):
                    sl = bi % 2
                    hf = hfs[sl]
                    if bi == 0:
                        T.dma('sp', f'Fhf{sl}', out=hf[:, :, 0:n], in_=S['hf'][:, t0:t0 + n].rearrange("(k p) t -> p k t", p=128), writes=[('hf', sl)])
                    if bi + 1 < len(blocks):
                        t0n, nn, _ = blocks[bi + 1]
                        T.dma('sp', f'Fhf{1 - sl}', out=hfs[1 - sl][:, :, 0:nn], in_=S['hf'][:, t0n:t0n + nn].rearrange("(k p) t -> p k t", p=128), writes=[('hf', 1 - sl)])
                    T.dma('sp', 'Fx', out=xt[:, :, 0:n], in_=S['xs'][:, t0:t0 + n].rearrange("(k p) t -> p k t", p=128), reads=[('xsd', t0)], writes=['xt'])
                    for hc in range(NHC):
                        pg = nextps()
                        pu = nextps()

                        def mmg(e, pg=pg, pu=pu, hc=hc):
                            last_ = None
                            for k in range(8):
                                e.matmul(ps[pg][:, 0:n], wi[:, k, hc * 128:(hc + 1) * 128], hf[:, k, 0:n], start=(k == 0), stop=(k == 7))
                            for k in range(8):
                                last_ = e.matmul(ps[pu][:, 0:n], wi[:, k, HID + hc * 128:HID + (hc + 1) * 128], hf[:, k, 0:n], start=(k == 0), stop=(k == 7))
                            return last_
                        T.op('pe', mmg, reads=[('wi', 0, hc), ('wi', 1, hc), ('hf', sl)], writes=[('ps', pg), ('ps', pu)])
                        a = hc % 2
                        T.op('act', lambda e, pg=pg, a=a: e.activation(out=gs[:, a, 0:n], in_=ps[pg][:, 0:n], func=AF.Silu), reads=[('ps', pg)], writes=[('gs', a)])
                        T.op('dve', lambda e, pu=pu, a=a, hc=hc: e.tensor_tensor(out=act[:, hc, 0:n], in0=ps[pu][:, 0:n], in1=gs[:, a, 0:n], op=ALU.mult),
                             reads=[('ps', pu), ('gs', a)], writes=[('act', hc)])
                    for oc in range(8):
                        po = nextps()

                        def mmo(e, po=po, oc=oc):
                            last_ = None
                            for hc in range(NHC):
                                last_ = e.matmul(ps[po][:, 0:n], wo2[:, hc, oc * 128:(oc + 1) * 128], act[:, hc, 0:n], start=(hc == 0), stop=(hc == NHC - 1))
                            return last_
                        T.op('pe', mmo, reads=[('wo2', 0), ('wo2', 1)] + [('act', hc) for hc in range(NHC)], writes=[('ps', po)])
                        T.op('dve', lambda e, po=po, oc=oc: e.scalar_tensor_tensor(out=xt[:, oc, 0:n], in0=ps[po][:, 0:n], scalar=modt[:, l, 5, oc, isctx:isctx + 1],
                                                                                 in1=xt[:, oc, 0:n], op0=ALU.mult, op1=ALU.add),
                             reads=[('ps', po), 'xt'], writes=['xt'])
                    if l == 0:
                        T.dma('sp', 'Fxs', out=S['xs'][:, t0:t0 + n].rearrange("(k p) t -> p k t", p=128), in_=xt[:, :, 0:n], reads=['xt'], writes=[('xsd', t0)])
                    else:
                        sqf = act[:, 0:8, :]
                        T.op('act', lambda e: e.activation(out=sqf[:, :, 0:n], in_=xt[:, :, 0:n], func=AF.Square), reads=['xt'] + [('act', hc) for hc in range(8)], writes=[('act', hc) for hc in range(8)])

                        def mms(e):
                            last_ = None
                            for k in range(8):
                                last_ = e.matmul(ps[6][:, 0:n], ones_bf[:], sqf[:, k, 0:n], start=(k == 0), stop=(k == 7))
                            return last_
                        T.op('pe', mms, reads=[('act', hc) for hc in range(8)] + ['ones_bf'], writes=[('ps', 6)])
                        T.op('dve', lambda e: e.tensor_scalar(out=gs[:, 0, 0:n], in0=ps[6][:, 0:n], scalar1=1.0 / D, scalar2=EPS, op0=ALU.mult, op1=ALU.add),
                             reads=[('ps', 6)], writes=[('gs', 0)])
                        T.op('act', lambda e: e.activation(out=gs[:, 0, 0:n], in_=gs[:, 0, 0:n], func=AF.Sqrt), reads=[('gs', 0)], writes=[('gs', 0)])
                        T.op('dve', lambda e: e.reciprocal(out=gs[:, 0, 0:n], in_=gs[:, 0, 0:n]), reads=[('gs', 0)], writes=[('gs', 0)])
                        for k in range(8):
                            T.op('dve', lambda e, k=k: e.scalar_tensor_tensor(out=xt[:, k, 0:n], in0=xt[:, k, 0:n], scalar=ppt[:, PP['fnorm'] + k:PP['fnorm'] + k + 1],
                                                                              in1=gs[:, 0, 0:n], op0=ALU.mult, op1=ALU.mult),
                                 reads=['xt', ('gs', 0), 'ppt'], writes=['xt'])
                        T.dma('sp', 'Fout', out=outT[:, t0:t0 + n].rearrange("(k p) t -> p k t", p=128), in_=xt[:, :, 0:n], reads=['xt'])
            T.barrier()

        phases = [('ada', phase_ada)]
        for l_ in range(2):
            phases += [(f'B{l_}', lambda l_=l_: phase_B(l_)), (f'G{l_}', lambda l_=l_: phase_G(l_)), (f'A{l_}', lambda l_=l_: phase_A(l_)),
                       (f'C{l_}', lambda l_=l_: phase_C(l_)), (f'M{l_}', lambda l_=l_: phase_M(l_)), (f'F{l_}', lambda l_=l_: phase_F(l_))]
        if only is not None:
            phases = [p for p in phases if p[0] in only]
        for name, fn in phases:
            fn()
            if stop_after == name:
                break
        T.final_wait('sp')
        print("instructions emitted:", T.ninst)
    return nc


IN_SIZES = (512, 128, 128, 256, 256, 512, 512, 32, 512, 512, 512)
IN_OFF = np.concatenate([[0], np.cumsum(IN_SIZES)])


def _local_to_global(half):
    tau = np.arange(TL)
    return tau if half == 0 else (SEQ - 1 - tau)


def _winx(w_in, half):
    o = IN_OFF
    aq = w_in[:, o[0]:o[1]]; ak = w_in[:, o[1]:o[2]]; av = w_in[:, o[2]:o[3]]
    bq = w_in[:, o[3]:o[4]]; bk = w_in[:, o[4]:o[5]]; bv = w_in[:, o[5]:o[6]]; br = w_in[:, o[6]:o[7]]
    bz = w_in[:, o[7]:o[8]]
    cq = w_in[:, o[8]:o[9]]; ck = w_in[:, o[9]:o[10]]; cv = w_in[:, o[10]:o[11]]
    out = np.zeros((D, NX), np.float32)
    sw = (np.arange(64) + 32) % 64
    heads = []
    for c in range(4):
        heads += [c, 4 + c]
    idx = np.concatenate([h * 64 + np.arange(64) for h in heads])
    idxs = np.concatenate([h * 64 + sw for h in heads])
    out[:, COL['Aq']:COL['Aq'] + 512] = aq[:, idx]
    out[:, COL['Aqs']:COL['Aqs'] + 512] = aq[:, idxs]
    out[:, COL['Ak']:COL['Ak'] + 128] = ak
    out[:, COL['Aks']:COL['Aks'] + 128] = ak[:, np.concatenate([sw, 64 + sw])]
    out[:, COL['Cq']:COL['Cq'] + 512] = cq
    out[:, COL['Ck']:COL['Ck'] + 512] = ck
    out[:, COL['Bq']:COL['Bq'] + 256] = bq
    out[:, COL['Bk']:COL['Bk'] + 256] = bk
    out[:, COL['Br']:COL['Br'] + 512] = br
    z1, z2 = (bz[:, 0:16], bz[:, 16:32]) if half == 0 else (bz[:, 16:32], bz[:, 0:16])
    out[:, COL['Bz']:COL['Bz'] + 16] = z1
    out[:, COL['Bz'] + 32:COL['Bz'] + 48] = z2
    out[:, COL['Av']:COL['Av'] + 128] = av
    out[:, COL['Bv']:COL['Bv'] + 512] = bv
    out[:, COL['Cv']:COL['Cv'] + 512] = cv
    return out


def _rope_tables(half):
    t = _local_to_global(half)
    row = (t // 64).astype(np.float32)
    col = (t % 64).astype(np.float32)
    inv = (np.float32(10000.0) ** (-np.arange(16, dtype=np.float32) / np.float32(16))).astype(np.float32)
    ang = np.concatenate([row[:, None] * inv[None], col[:, None] * inv[None]], axis=-1).astype(np.float32)
    cos = np.cos(ang).astype(np.float32).T
    sin = np.sin(ang).astype(np.float32).T
    cs = np.zeros((128, 2, TT), np.float32)
    cs[:, 0, TL:] = 1.0
    for rep in range(2):
        b = rep * 64
        cs[b:b + 32, 0, :TL] = cos; cs[b + 32:b + 64, 0, :TL] = cos
        cs[b:b + 32, 1, :TL] = -sin; cs[b + 32:b + 64, 1, :TL] = sin
    return cs


def _ctab(rpb, half):
    tab = np.full((128, NCTAB, 8, 128), -30000.0, np.float32)
    pairs = [(0, 0), (0, 1), (0, 2), (0, 3), (1, 0), (1, 1), (1, 2), (1, 3), (4, 2), (4, 3), (4, 4), (4, 5), (4, 6)]
    loc = np.arange(128)
    for ti, (qn, kn) in enumerate(pairs):
        tq = qn * 128 + loc
        tk = kn * 128 + loc
        if half == 1:
            tq = SEQ - 1 - tq
            tk = SEQ - 1 - tk
        qr, qc = tq // 64, tq % 64
        kr, kc = tk // 64, tk % 64
        rs = np.clip(qr - 4, 0, 64 - 8)
        ws = np.clip(qc - 8, 0, 64 - 16)
        valid = ((kr[:, None] >= rs[None]) & (kr[:, None] < rs[None] + 8) &
                 (kc[:, None] >= ws[None]) & (kc[:, None] < ws[None] + 16))
        dr = np.clip(kr[:, None] - qr[None] + 7, 0, 14)
        dc = np.clip(kc[:, None] - qc[None], -15, 15) + 15
        vals = rpb[:, dr, dc]
        tab[:, ti] = np.where(valid[None], vals, np.float32(-30000.0)).transpose(1, 0, 2)[:, [0, 2, 1, 3, 4, 6, 5, 7], :]
    return tab.reshape(128, NCTAB * 8 * 128)


def _dup2(v):
    a = v.reshape(-1, 128).T
    return np.repeat(a[:, :, None], 2, axis=2).reshape(128, -1)


def prep_core(inputs, core):
    b, half = core // 2, core % 2
    x = inputs['x'][b]
    t = np.arange(TL) if half == 0 else (SEQ - 1 - np.arange(TL))
    m = {}
    m['xT'] = np.ascontiguousarray(x[t].T)
    ctx = inputs['ctx'][b]
    if half == 1:
        ctx = ctx[::-1]
    m['ctxT'] = np.ascontiguousarray(ctx.T)
    pp = np.zeros((128, NPP), np.float32)
    cc = np.stack([inputs['c'][b].reshape(8, 128).T, inputs['c_ctx'].reshape(8, 128).T], axis=2)
    pp[:, PP['c']:PP['c'] + 16] = cc.reshape(128, 16)
    for l in range(2):
        pp[:, PP[f'bada{l}']:PP[f'bada{l}'] + 96] = _dup2(inputs['b_ada'][l])
        pp[:, PP[f'nmix{l}']:PP[f'nmix{l}'] + 16] = _dup2(inputs['norm_mix'][l])
        pp[:, PP[f'nffn{l}']:PP[f'nffn{l}'] + 16] = _dup2(inputs['norm_ffn'][l])
        pp[:, PP[f'bmerge{l}']:PP[f'bmerge{l}'] + 24] = inputs['b_merge'][l].reshape(24, 128).T
        pp[:, PP[f'gnorm{l}']] = inputs['gla_norm'][l]
        pp[:, PP[f'sink{l}']:PP[f'sink{l}'] + 8] = inputs['attn_sink'][l][None, :]
    pp[:, PP['fnorm']:PP['fnorm'] + 8] = inputs['final_norm'].reshape(8, 128).T
    m['pp'] = pp
    m['w_ada'] = inputs['w_ada']
    m['winx'] = np.stack([_winx(inputs['w_in'][l], half) for l in range(2)])
    gw = np.zeros((64, 2, 256), np.float32)
    d1w, d2w = ('gla_gate_w_fwd', 'gla_gate_w_bwd') if half == 0 else ('gla_gate_w_bwd', 'gla_gate_w_fwd')
    d1b, d2b = ('gla_gate_b_fwd', 'gla_gate_b_bwd') if half == 0 else ('gla_gate_b_bwd', 'gla_gate_b_fwd')
    for l in range(2):
        gw[0:16, l] = inputs[d1w][l]; gw[32:48, l] = inputs[d2w][l]
        gw[16, l] = inputs[d1b][l]; gw[48, l] = inputs[d2b][l]
    m['gwp'] = gw
    m['ones_row'] = np.ones((1, TT), np.float32)
    m['cossin'] = _rope_tables(half)
    s_, t_ = np.meshgrid(np.arange(128), np.arange(128), indexing='ij')
    m['cst'] = np.stack([(s_ <= t_), (s_ >= t_), (s_ == t_)], axis=1).astype(np.float32)
    m['ctab'] = np.stack([_ctab(inputs['na_rpb'][l], half) for l in range(2)])
    for k in ('w_branch_a', 'w_branch_b', 'w_branch_c', 'w_merge', 'w_out', 'w_ffn_in', 'w_ffn_out'):
        m[k] = inputs[k]
    return m


_NC_CACHE = {}


def kernel(**inputs):
    inputs = {k: np.asarray(v) for k, v in inputs.items()}
    if 'nc' not in _NC_CACHE:
        _NC_CACHE['nc'] = build()
    nc = _NC_CACHE['nc']
    in_maps = [prep_core(inputs, c) for c in range(8)]
    res = run_bass_kernel_spmd(nc, in_maps, core_ids=list(range(8)))
    out = np.zeros((4, SEQ, D), np.float32)
    for c in range(8):
        b, half = c // 2, c % 2
        o = res.results[c]["outT"].T
        if half == 0:
            out[b, 0:2048] = o
        else:
            out[b, 2048:] = o[::-1]
    return out
```
